# Optimizing a Trainium2 kernel written in Bass

```python
import jax, jax.numpy as jnp
from jax import lax
import numpy as np

D_MODEL = 2048
BATCH = 4
SEQ = 8192
DEPTH = 1

GLA_HEADS = 4
GLA_DK = (D_MODEL // 2) // GLA_HEADS
GLA_DV = D_MODEL // GLA_HEADS
GLA_GATE_RANK = 16
GLA_GATE_NORMALIZER = 16.0
GLA_CHUNK = 64
MLA_HEADS = 16
MLA_Q_RANK = D_MODEL // 4
MLA_KV_RANK = 512
MLA_NOPE = 128
MLA_ROPE = 64
MLA_V = 128
ROPE_THETA = 10000.0
Q_BLOCK = 128
FFN_HIDDEN = ((8 * D_MODEL // 3 + 255) // 256) * 256
NORM_EPS = 1e-6
N_MOD = 6

IN_WIDTHS = (GLA_HEADS * GLA_DK, GLA_HEADS * GLA_DK, GLA_HEADS * GLA_DV, GLA_HEADS * GLA_DV,
             GLA_GATE_RANK, MLA_Q_RANK, MLA_KV_RANK, MLA_ROPE, D_MODEL, D_MODEL)
IN_TOTAL = sum(IN_WIDTHS)

kernel_name = "hybrid_gla_mla_adaln_block"


def rmsnorm(x, g):
    xf = x.astype(jnp.float32)
    y = xf * lax.rsqrt(jnp.mean(xf * xf, axis=-1, keepdims=True) + NORM_EPS)
    return (y * g.astype(jnp.float32)).astype(x.dtype)


def rope(t, cos, sin):
    t1, t2 = jnp.split(t, 2, axis=-1)
    return jnp.concatenate([t1 * cos - t2 * sin, t2 * cos + t1 * sin], axis=-1)


def gla_chunked(q, k, v, log_a):
    B, H, S, dk = q.shape
    dv = v.shape[-1]
    C = GLA_CHUNK
    N = S // C

    def to_chunks(t):
        return t.astype(jnp.float32).reshape(B, H, N, C, t.shape[-1]).transpose(2, 0, 1, 3, 4)

    qc, kc, vc = to_chunks(q), to_chunks(k), to_chunks(v)
    bc = jnp.cumsum(to_chunks(log_a), axis=3)
    causal = jnp.tril(jnp.ones((C, C), dtype=bool))

    def step(state, inp):
        q_i, k_i, v_i, b_i = inp
        o_inter = jnp.einsum('bhid,bhdv->bhiv', q_i * jnp.exp(b_i), state)
        diff = b_i[:, :, :, None, :] - b_i[:, :, None, :, :]
        w = jnp.exp(jnp.where(causal[:, :, None], diff, -jnp.inf))
        att = jnp.einsum('bhid,bhjd,bhijd->bhij', q_i, k_i, w)
        o_intra = jnp.einsum('bhij,bhjv->bhiv', att, v_i)
        b_last = b_i[:, :, -1, :]
        k_dec = k_i * jnp.exp(b_last[:, :, None, :] - b_i)
        state = jnp.exp(b_last)[..., None] * state + jnp.einsum('bhjd,bhjv->bhdv', k_dec, v_i)
        return state, o_inter + o_intra

    state0 = jnp.zeros((B, H, dk, dv), jnp.float32)
    _, o = lax.scan(step, state0, (qc, kc, vc, bc))
    return o.transpose(1, 2, 0, 3, 4).reshape(B, H, S, dv).astype(v.dtype)


def mla_attention(q_nope, q_rope, k_nope, k_rope, v):
    B, S, H, dn = q_nope.shape
    R = q_rope.shape[-1]
    dv = v.shape[-1]
    NB = S // Q_BLOCK
    scale = (dn + R) ** -0.5
    qn = q_nope.reshape(B, NB, Q_BLOCK, H, dn).transpose(1, 0, 3, 2, 4)
    qr = q_rope.reshape(B, NB, Q_BLOCK, H, R).transpose(1, 0, 3, 2, 4)
    kn = k_nope.transpose(0, 2, 1, 3)
    vv = v.transpose(0, 2, 1, 3)
    key_pos = jnp.arange(S)

    def block(args):
        qn_b, qr_b, i = args
        s = (jnp.einsum('bhqd,bhkd->bhqk', qn_b, kn)
             + jnp.einsum('bhqr,bkr->bhqk', qr_b, k_rope)).astype(jnp.float32) * scale
        q_pos = i * Q_BLOCK + jnp.arange(Q_BLOCK)
        mask = key_pos[None, :] <= q_pos[:, None]
        p = jax.nn.softmax(jnp.where(mask, s, -jnp.inf), axis=-1)
        return jnp.einsum('bhqk,bhkd->bhqd', p.astype(vv.dtype), vv)

    o = lax.map(block, (qn, qr, jnp.arange(NB)))
    return o.transpose(1, 0, 3, 2, 4).reshape(B, S, H, dv)


def token_mix(h, cos, sin, w_in, gla_gk_w, gla_gk_b, gla_onorm_g, gla_wo,
              mla_q_norm_g, mla_wuq, mla_kv_norm_g, mla_wukv, mla_wo, w_out):
    B, S, _ = h.shape
    idx = [sum(IN_WIDTHS[:i + 1]) for i in range(len(IN_WIDTHS) - 1)]
    (p_q, p_k, p_v, p_g, p_gk, p_cq, p_ckv, p_kr, p_ga, p_gb) = jnp.split(h @ w_in, idx, axis=-1)

    def heads(t, d):
        return t.reshape(B, S, GLA_HEADS, d).transpose(0, 2, 1, 3)
    q = heads(p_q, GLA_DK) * (GLA_DK ** -0.5)
    k = heads(p_k, GLA_DK)
    v = heads(p_v, GLA_DV)
    log_a = jax.nn.log_sigmoid((p_gk @ gla_gk_w + gla_gk_b).astype(jnp.float32)) / GLA_GATE_NORMALIZER
    o = gla_chunked(q, k, v, heads(log_a, GLA_DK)).transpose(0, 2, 1, 3)
    o = rmsnorm(o, gla_onorm_g) * jax.nn.silu(p_g.reshape(B, S, GLA_HEADS, GLA_DV))
    y_gla = o.reshape(B, S, GLA_HEADS * GLA_DV) @ gla_wo

    q_lat = (rmsnorm(p_cq, mla_q_norm_g) @ mla_wuq).reshape(B, S, MLA_HEADS, MLA_NOPE + MLA_ROPE)
    q_nope, q_rope = q_lat[..., :MLA_NOPE], q_lat[..., MLA_NOPE:]
    q_rope = rope(q_rope, cos[:, :, None, :], sin[:, :, None, :])
    kv = (rmsnorm(p_ckv, mla_kv_norm_g) @ mla_wukv).reshape(B, S, MLA_HEADS, MLA_NOPE + MLA_V)
    k_nope, v_m = kv[..., :MLA_NOPE], kv[..., MLA_NOPE:]
    k_rope = rope(p_kr, cos, sin)
    o_m = mla_attention(q_nope, q_rope, k_nope, k_rope, v_m)
    y_mla = o_m.reshape(B, S, MLA_HEADS * MLA_V) @ mla_wo

    merged = jax.nn.sigmoid(p_ga) * y_gla + jax.nn.sigmoid(p_gb) * y_mla
    return merged @ w_out


def swiglu(h, w_in, w_down):
    g, u = jnp.split(h @ w_in, 2, axis=-1)
    return (jax.nn.silu(g) * u) @ w_down


def setup_inputs(seed: int = 0) -> dict:
    key = jax.random.key(seed)
    ks = jax.random.split(key, 24)
    L, D = DEPTH, D_MODEL

    def nrm(k, shape, scale):
        return jax.random.normal(k, shape, jnp.float32) * scale

    def gain(k, n):
        return 1.0 + 0.02 * jax.random.normal(k, (L, n), jnp.float32)

    x = jax.random.normal(ks[0], (BATCH, SEQ, D), jnp.float32)
    c = jax.random.normal(ks[1], (BATCH, D), jnp.float32)
    offsets = jax.random.randint(ks[2], (BATCH, 1), 0, 1024, dtype=jnp.int32)
    positions = offsets + jnp.arange(SEQ, dtype=jnp.int32)[None, :]
    return {
        "x": x,
        "c": c,
        "positions": positions,
        "ada_w": nrm(ks[3], (L, D, N_MOD * D), 0.5 * D ** -0.5),
        "ada_b": nrm(ks[4], (L, N_MOD * D), 0.02),
        "norm_mix_g": gain(ks[5], D),
        "w_in": nrm(ks[6], (L, D, IN_TOTAL), D ** -0.5),
        "gla_gk_w": nrm(ks[7], (L, GLA_GATE_RANK, GLA_HEADS * GLA_DK), GLA_GATE_RANK ** -0.5),
        "gla_gk_b": nrm(ks[8], (L, GLA_HEADS * GLA_DK), 0.5),
        "gla_onorm_g": gain(ks[9], GLA_DV),
        "gla_wo": nrm(ks[10], (L, GLA_HEADS * GLA_DV, D), (GLA_HEADS * GLA_DV) ** -0.5),
        "mla_q_norm_g": gain(ks[11], MLA_Q_RANK),
        "mla_wuq": nrm(ks[12], (L, MLA_Q_RANK, MLA_HEADS * (MLA_NOPE + MLA_ROPE)), MLA_Q_RANK ** -0.5),
        "mla_kv_norm_g": gain(ks[13], MLA_KV_RANK),
        "mla_wukv": nrm(ks[14], (L, MLA_KV_RANK, MLA_HEADS * (MLA_NOPE + MLA_V)), MLA_KV_RANK ** -0.5),
        "mla_wo": nrm(ks[15], (L, MLA_HEADS * MLA_V, D), (MLA_HEADS * MLA_V) ** -0.5),
        "w_out": nrm(ks[16], (L, D, D), D ** -0.5),
        "norm_ffn_g": gain(ks[17], D),
        "ffn_w_in": nrm(ks[18], (L, D, 2 * FFN_HIDDEN), D ** -0.5),
        "ffn_w_down": nrm(ks[19], (L, FFN_HIDDEN, D), FFN_HIDDEN ** -0.5),
        "final_norm_g": 1.0 + 0.02 * jax.random.normal(ks[20], (D,), jnp.float32),
    }


def reference(x, c, positions, ada_w, ada_b, norm_mix_g, w_in, gla_gk_w, gla_gk_b, gla_onorm_g,
              gla_wo, mla_q_norm_g, mla_wuq, mla_kv_norm_g, mla_wukv, mla_wo, w_out,
              norm_ffn_g, ffn_w_in, ffn_w_down, final_norm_g):
    B, S, D = x.shape
    inv_freq = ROPE_THETA ** (-jnp.arange(0, MLA_ROPE, 2, dtype=jnp.float32) / MLA_ROPE)
    ang = positions.astype(jnp.float32)[..., None] * inv_freq
    cos = jnp.cos(ang).astype(x.dtype)
    sin = jnp.sin(ang).astype(x.dtype)
    c_act = jax.nn.silu(c)
    for l in range(DEPTH):
        mod = (c_act @ ada_w[l] + ada_b[l]).reshape(B, N_MOD, D)[:, :, None, :]
        shift_m, scale_m, gate_m, shift_f, scale_f, gate_f = [mod[:, i] for i in range(N_MOD)]
        h = rmsnorm(x, norm_mix_g[l]) * (1.0 + scale_m) + shift_m
        x = x + gate_m * token_mix(h, cos, sin, w_in[l], gla_gk_w[l], gla_gk_b[l], gla_onorm_g[l],
                                   gla_wo[l], mla_q_norm_g[l], mla_wuq[l], mla_kv_norm_g[l],
                                   mla_wukv[l], mla_wo[l], w_out[l])
        h = rmsnorm(x, norm_ffn_g[l]) * (1.0 + scale_f) + shift_f
        x = x + gate_f * swiglu(h, ffn_w_in[l], ffn_w_down[l])
    return rmsnorm(x, final_norm_g)
```

```python
import math
import os
STOP = int(os.environ.get('KSTOP', '0'))
KSUB = int(os.environ.get('KSUB', '0'))
import contextlib
import numpy as np
import ml_dtypes
import concourse.bass as bass
import concourse.mybir as mybir
from concourse.bass_utils import run_bass_kernel_spmd

F32 = mybir.dt.float32
BF16 = mybir.dt.bfloat16
I32 = mybir.dt.int32
AF = mybir.ActivationFunctionType
ALU = mybir.AluOpType

D = 2048
NKC = 16
FFH = 5632
NEG = -30000.0


class _Op:
    __slots__ = ("eng", "fn", "reads", "writes", "dma", "idx", "waits", "signaled", "dsem", "dval", "rank", "barrier")


class Sched:
    ENGS = ("pe", "act", "dve", "pool", "sp")

    def __init__(self, n_dma_sems=14):
        self.ops = []
        self.n_dma_sems = n_dma_sems

    def add(self, eng, fn, reads=(), writes=(), dma=False, barrier=False):
        op = _Op()
        op.eng = eng
        op.fn = fn
        writes = list(writes) + [r for r in reads if r == "psb" or (isinstance(r, tuple) and r[0] == "ps")]
        op.reads = tuple(reads)
        op.writes = tuple(writes)
        op.dma = dma
        op.signaled = False
        op.barrier = barrier
        self.ops.append(op)
        return op

    def barrier(self):
        for e in self.ENGS:
            self.add(e, None, barrier=True)

    def analyze(self):
        last_w = {}
        readers = {}
        cnt = {e: 0 for e in self.ENGS}
        clock = {e: {} for e in self.ENGS}
        opclock = {}
        dma_rr = {e: 0 for e in self.ENGS}
        dma_val = {}
        dma_last = {}
        last_on = {}
        ops = self.ops
        for i, op in enumerate(ops):
            e = op.eng
            cnt[e] += 1
            op.idx = cnt[e]
            deps = set()
            if op.barrier:
                deps.update(last_on.values())
                deps.update(dma_last.values())
            for r in op.reads:
                w = last_w.get(r)
                if w is not None:
                    deps.add(w)
            for w_ in op.writes:
                w = last_w.get(w_)
                if w is not None:
                    deps.add(w)
                for rd in readers.get(w_, ()):
                    deps.add(rd)
            if op.dma:
                k = dma_rr[e]
                dma_rr[e] = (k + 1) % self.n_dma_sems
                skey = ("dma", e, k)
                op.dsem = skey
                op.dval = dma_val.get(skey, 0) + 16
                dma_val[skey] = op.dval
                prev = dma_last.get(skey)
                if prev is not None:
                    deps.add(prev)
                dma_last[skey] = i
            ck = clock[e]
            waits = []
            for d in sorted(deps):
                dop = ops[d]
                if dop.fn is None:
                    continue
                if dop.dma:
                    key, val = dop.dsem, dop.dval
                else:
                    key, val = dop.eng, dop.idx
                    if key == e and e in ("pe", "sp"):
                        continue
                if ck.get(key, 0) >= val:
                    continue
                waits.append((key, val, d))
                dop.signaled = True
                ck[key] = val
                oc = opclock.get(d)
                if oc is not None:
                    for k2, v2 in oc.items():
                        if ck.get(k2, 0) < v2:
                            ck[k2] = v2
            op.waits = waits
            oc = dict(ck)
            if not op.dma:
                oc[e] = op.idx
                if op.fn is not None:
                    last_on[e] = i
            opclock[i] = oc
            for r in op.reads:
                readers.setdefault(r, []).append(i)
            for w_ in op.writes:
                last_w[w_] = i
                readers[w_] = []
            if len(opclock) > 60000:
                for j in list(opclock.keys())[:30000]:
                    del opclock[j]
        rank = {e: 0 for e in self.ENGS}
        for op in ops:
            if not op.dma and op.signaled:
                rank[op.eng] += 1
                op.rank = rank[op.eng]
        self.stats = dict(cnt=cnt, rank=rank, dma=dict(dma_val))

    def emit(self, nc):
        self.analyze()
        ops = self.ops
        with contextlib.ExitStack() as st:
            sems = {}
            for e in self.ENGS:
                sems[e] = st.enter_context(nc.semaphore("s_" + e))
            for key in self.stats["dma"]:
                sems[key] = st.enter_context(nc.semaphore("d_%s_%d" % (key[1], key[2])))
            block = st.enter_context(nc.Block())
            per = {e: [op for op in ops if op.eng == e] for e in self.ENGS}

            def make(ename):
                def body(eng):
                    for op in per[ename]:
                        for (key, val, d) in op.waits:
                            dop = ops[d]
                            eng.wait_ge(sems[key], val if dop.dma else dop.rank)
                        if op.fn is None:
                            continue
                        ins = op.fn(eng)
                        if op.dma:
                            ins.then_inc(sems[op.dsem], 16)
                        elif op.signaled:
                            ins.then_inc(sems[ename], 1)
                return body

            block.tensor(make("pe"))
            block.scalar(make("act"))
            block.vector(make("dve"))
            block.gpsimd(make("pool"))
            block.sync(make("sp"))


WB = {
    "qk": (4, 16, 512), "v": (4, 16, 512), "g": (4, 16, 512), "gk": (1, 16, 16), "cq": (1, 16, 512),
    "ckv": (1, 16, 512), "kr": (1, 16, 128), "ga": (4, 16, 512), "gb": (4, 16, 512),
    "uq": (8, 4, 512), "uk": (4, 4, 512), "uv": (4, 4, 512),
    "mo": (4, 16, 512), "go": (4, 16, 512), "out": (4, 16, 512), "fin": (22, 16, 512), "dn": (4, 44, 512),
}


class Builder:
    def __init__(self, nc, PRE, OWN):
        self.nc = nc
        self.PRE = PRE
        self.OWN = OWN
        self.NK = PRE + OWN
        self.S = Sched()
        self.st = contextlib.ExitStack()
        self.bank_i = 0
        self.eng_i = 0
        self.wslot_i = 0

    def din(self, name, shape, dt=F32):
        return self.nc.dram_tensor(name, list(shape), dt, kind="ExternalInput").ap()

    def sb(self, name, shape, dt):
        return self.st.enter_context(self.nc.sbuf_tensor("sb_" + name, list(shape), dt))

    def mm(self, out, lhsT, rhs, start, stop, R, W):
        self.S.add("pe", lambda e: e.matmul(out, lhsT=lhsT, rhs=rhs, start=start, stop=stop), R, W)

    def tr(self, out, in_, ident, R, W):
        self.S.add("pe", lambda e: e.transpose(out=out, in_=in_, identity=ident), R, W)

    def act(self, out, in_, func, R, W, scale=1.0, bias=None, accum=None):
        kw = {}
        if bias is not None:
            kw["bias"] = bias
        if accum is not None:
            kw["accum_out"] = accum
        self.S.add("act", lambda e: e.activation(out=out, in_=in_, func=func, scale=scale, **kw), R, W)

    def tt(self, eng, out, a, b, op, R, W):
        self.S.add(eng, lambda e: e.tensor_tensor(out=out, in0=a, in1=b, op=op), R, W)

    def ts(self, eng, out, a, s1, s2, op0, op1, R, W):
        if s2 is None:
            self.S.add(eng, lambda e: e.tensor_scalar(out=out, in0=a, scalar1=s1, scalar2=None, op0=op0), R, W)
        else:
            self.S.add(eng, lambda e: e.tensor_scalar(out=out, in0=a, scalar1=s1, scalar2=s2, op0=op0, op1=op1), R, W)

    def stt(self, eng, out, in0, scalar, in1, op0, op1, R, W):
        self.S.add(eng, lambda e: e.scalar_tensor_tensor(out=out, in0=in0, scalar=scalar, in1=in1, op0=op0, op1=op1), R, W)

    def cp(self, eng, out, in_, R, W):
        if eng == "act":
            self.S.add("act", lambda e: e.activation(out=out, in_=in_, func=AF.Identity), R, W)
        else:
            self.S.add(eng, lambda e: e.tensor_copy(out=out, in_=in_), R, W)

    def recip(self, out, in_, R, W):
        self.S.add("dve", lambda e: e.reciprocal(out=out, in_=in_), R, W)

    def memset(self, eng, ap, val, W):
        self.S.add(eng, lambda e: e.memset(ap, val), (), W)

    def dma(self, q, out, in_, R, W):
        self.S.add(q, lambda e: e.dma_start(out=out, in_=in_), R, W, dma=True)

    def bank(self):
        i = self.bank_i
        self.bank_i = (i + 1) % 7
        return self.ps[i], ("ps", i)

    def rot(self, engs=("act", "dve")):
        self.eng_i += 1
        return engs[self.eng_i % len(engs)]

    def wload(self, blk, g, kc0, nkc):
        i = self.wslot_i
        self.wslot_i = (i + 1) % self.NSLOT
        ng, kcw, ncols = WB[blk]
        dst = self.wring[:, i, 0:nkc, 0:ncols]
        src = self.wscr[blk][g, :, kc0:kc0 + nkc, :]
        self.dma("sp", dst, src, [], [("w", i)])
        return self.wring[:, i], ("w", i)

    def wload_multi(self, blk, g0, ng):
        i = self.wslot_i
        self.wslot_i = (i + 1) % self.NSLOT
        for j in range(ng):
            self.dma("sp", self.wring[:, i, j * 4:(j + 1) * 4, :], self.wscr[blk][g0 + j], [], [("w", i)])
        return self.wring[:, i], ("w", i)

    def proj_tm(self, blk, g, actT, actkeys, ncols=512, col0=0, kcw=16):
        ps, pk = self.bank()
        for half in range(0, kcw, 8):
            n = min(8, kcw - half)
            w, wk = self.wload(blk, g, half, n)
            for j in range(n):
                kc = half + j
                self.mm(ps[:, 0:ncols], actT[:, kc, :], w[:, j, col0:col0 + ncols], kc == 0, kc == kcw - 1,
                        [wk] + actkeys, [pk])
        return ps, pk

    def build(self):
        nc = self.nc
        PRE, OWN, NK = self.PRE, self.OWN, self.NK
        NTP, NTO = PRE // 128, OWN // 128
        self.NSLOT = 3
        I = {}
        I["xp"] = self.din("xp", [PRE, D])
        I["xo"] = self.din("xo", [OWN, D])
        I["posp"] = self.din("posp", [1, PRE], I32)
        I["poso"] = self.din("poso", [1, OWN], I32)
        I["c_fm"] = self.din("c_fm", [128, 16])
        I["flag"] = self.din("flag", [128, 2])
        I["ada_w"] = self.din("ada_w", [D, 6 * D])
        I["ada_b_fm"] = self.din("ada_b_fm", [128, 96])
        I["nmg_fm"] = self.din("nmg_fm", [128, 16])
        I["nfg_fm"] = self.din("nfg_fm", [128, 16])
        I["w_in"] = self.din("w_in", [D, 11344])
        I["gkw"] = self.din("gkw", [17, 1024])
        I["onorm"] = self.din("onorm", [1, 512])
        I["gla_wo"] = self.din("gla_wo", [D, D])
        I["qng_fm"] = self.din("qng_fm", [128, 4])
        I["kvng_fm"] = self.din("kvng_fm", [128, 4])
        I["wuq"] = self.din("wuq", [512, 3072])
        I["wukv"] = self.din("wukv", [512, 4096])
        I["mla_wo"] = self.din("mla_wo", [D, D])
        I["w_out"] = self.din("w_out", [D, D])
        I["ffn_in"] = self.din("ffn_in", [D, 2 * FFH])
        I["ffn_dn"] = self.din("ffn_dn", [FFH, D])
        I["fing"] = self.din("fing", [1, D])
        I["ident"] = self.din("ident", [128, 128])
        I["identb"] = self.din("identb", [128, 128], BF16)
        I["tri"] = self.din("tri", [128, 128])
        I["uni"] = self.din("uni", [128, 128])
        I["mask01"] = self.din("mask01", [128, 128])
        I["maskneg"] = self.din("maskneg", [128, 128], BF16)
        I["invf"] = self.din("invf", [64, 1])
        I["sgn"] = self.din("sgn", [64, 1])
        out = nc.dram_tensor("out", [OWN, D], F32, kind="ExternalOutput").ap()
        self.wscr = {}
        for name, (ng, kcw, ncols) in WB.items():
            self.wscr[name] = nc.dram_tensor("ws_" + name, [ng, 128, kcw, ncols], BF16).ap()
        KTc = nc.dram_tensor("KTc", [16, 128, NK], BF16).ap()
        KRc = nc.dram_tensor("KRc", [64, NK], BF16).ap()
        Vc = nc.dram_tensor("Vc", [NK, 2048], BF16).ap()

        sb = self.sb
        self.ps = [self.st.enter_context(nc.psum_tensor("ps%d" % i, [128, 512], F32)) for i in range(7)]
        psb = self.st.enter_context(nc.psum_tensor("psb", [128, 1024], BF16))
        xstg = sb("xstg", [128, D], F32)
        xt = sb("xt", [128, D], F32)
        hT = sb("hT", [128, 16, 128], BF16)
        self.wring = sb("wring", [128, self.NSLOT, 8, 512], BF16)
        S32 = sb("S32", [128, 4, 2, 512], F32)
        Sb = sb("Sb", [128, 4, 2, 512], BF16)
        Fb = sb("Fb", [128, 8, 512], F32)
        Fbf = Fb[:].bitcast(BF16)
        Vb = sb("Vb", [128, 2, 16, 132], BF16)
        pb = sb("pb", [128, 3, 128], BF16)
        big = sb("big", [128, 48, 128], BF16)
        otm = sb("otm", [128, D], BF16)
        mtm = sb("mtm", [128, D], BF16)
        tmp = [sb("tmp%d" % i, [128, 512], F32) for i in range(4)]
        gate_m = sb("gate_m", [128, D], F32)
        gate_f = sb("gate_f", [128, D], F32)
        fing = sb("fing", [128, D], F32)
        onorm = sb("onormb", [128, 512], F32)
        ident = sb("ident", [128, 128], F32)
        identb = sb("identb", [128, 128], BF16)
        tri = sb("tri", [128, 128], F32)
        uni = sb("uni", [128, 128], F32)
        mask01 = sb("mask01", [128, 128], F32)
        maskneg = sb("maskneg", [128, 128], BF16)
        cst = sb("cst", [128, 8], F32)
        flag = sb("flag", [128, 2], F32)
        mods = sb("mods", [128, 96], F32)
        g1 = sb("g1", [128, 2, 16], F32)
        cfm = sb("cfm", [128, 16], F32)
        cact = sb("cact", [128, 16], F32)
        nmg = sb("nmg", [128, 2, 16], F32)
        adab = sb("adab", [128, 96], F32)
        qng = sb("qng", [128, 4], F32)
        kvng = sb("kvng", [128, 4], F32)
        invf = sb("invf", [64, 1], F32)
        sgn = sb("sgn", [64, 1], F32)
        st1 = sb("st1", [128, 8], F32)
        pgk = sb("pgk", [32, 128], F32)
        gkw = sb("gkw", [17, 1024], F32)
        eL = sb("eL", [128, 2, 256], F32)
        ebuf = sb("ebuf", [128, 3, 256], F32)
        qp = sb("qp", [128, 2, 128], BF16)
        kp = sb("kp", [128, 2, 128], BF16)
        kd = sb("kd", [128, 256], BF16)
        AT = sb("AT", [128, 128], BF16)
        vtm = sb("vtm", [128, 512], BF16)
        on = sb("on", [128, 512], BF16)
        cT = sb("cT", [128, 512], F32)
        sq = sb("sq", [128, 512], BF16)
        rq = sb("rq", [128, 128], F32)
        cqn = sb("cqn", [128, 4, 128], BF16)
        ckvn = sb("ckvn", [128, 4, 128], BF16)
        posi = sb("posi", [64, 128], I32)
        pf = sb("pf", [64, 4, 128], F32)
        cs = sb("cs", [64, 2, 128], F32)
        QN = sb("QN", [128, 16, 128], BF16)
        QR = sb("QR", [64, 16, 128], BF16)
        krT = sb("krT", [64, 128], BF16)
        KTt = sb("KTt", [128, 16, 128], BF16)
        Vt = sb("Vt", [128, D], BF16)
        rec = sb("rec", [128, 2], F32)

        S = self.S
        EPS = cst[:, 0:1]
        for nm, t in (("ident", ident), ("identb", identb), ("tri", tri), ("uni", uni), ("mask01", mask01),
                      ("maskneg", maskneg), ("flag", flag), ("qng_fm", qng), ("kvng_fm", kvng), ("invf", invf),
                      ("sgn", sgn), ("c_fm", cfm), ("ada_b_fm", adab), ("gkw", gkw)):
            self.dma("sp", t[:], I[nm], [], [nm])
        self.dma("sp", nmg[:, 0, :], I["nmg_fm"], [], ["nmg0"])
        self.dma("sp", nmg[:, 1, :], I["nfg_fm"], [], ["nmg1"])
        self.dma("sp", onorm[:], I["onorm"].partition_broadcast(128), [], ["onorm"])
        self.dma("sp", fing[:], I["fing"].partition_broadcast(128), [], ["fing"])
        self.memset("pool", cst[:, 0:1], 1e-6, ["cst"])
        self.memset("pool", cst[:, 1:2], math.log(1.0 / 16.0), ["cst"])
        self.memset("pool", cst[:, 2:3], 0.0, ["cst"])
        self.memset("pool", pgk[:], 1.0, ["pgk"])
        self.memset("pool", Vb[:], 1.0, [("Vb", 0), ("Vb", 1)])
        self.act(cact[:], cfm[:], AF.Exp, ["c_fm"], ["cact"], scale=-1.0)
        self.ts("dve", cact[:], cact[:], 1.0, None, ALU.add, None, ["cact"], ["cact"])
        self.recip(cact[:], cact[:], ["cact"], ["cact"])
        self.tt("dve", cact[:], cact[:], cfm[:], ALU.mult, ["cact", "c_fm"], ["cact"])
        stgA = S32[:].rearrange("p a b c -> p (a b) c")
        stgB = Fb[:]
        SKEYS = [("S", h, k) for h in range(4) for k in range(2)]
        pm, pmk = self.ps[6], ("ps", 6)
        adaw = I["ada_w"].rearrange("(kc p) n -> p kc n", p=128)
        crep = sb("crep", [128, 16, 128], F32)
        self.memset("pool", crep[:], 1.0, ["crep"])
        for kc in range(16):
            self.ts("dve", crep[:, kc, :], crep[:, kc, :], cact[:, kc:kc + 1], None, ALU.mult, None, ["cact", "crep"], ["crep"])
        for g in range(24):
            self.dma("sp", stgA, adaw[:, 0:8, g * 512:(g + 1) * 512], [], SKEYS)
            self.dma("sp", stgB, adaw[:, 8:16, g * 512:(g + 1) * 512], [], ["Fb"])
            for cc in range(4):
                j = g * 4 + cc
                for kc in range(16):
                    stg, skeys = (stgA, SKEYS) if kc < 8 else (stgB, ["Fb"])
                    self.mm(pm[:, j:j + 1], stg[:, kc % 8, cc * 128:(cc + 1) * 128], cact[:, kc:kc + 1],
                            kc == 0, kc == 15, skeys + ["cact"], [pmk])
            if g // 4 in (2, 5):
                gbank, gbk = self.ps[g % 2], ("ps", g % 2)
                for kc in range(16):
                    stg, skeys = (stgA, SKEYS) if kc < 8 else (stgB, ["Fb"])
                    self.mm(gbank[:, :], crep[:, kc, :], stg[:, kc % 8, :], kc == 0, kc == 15, skeys + ["crep"], [gbk])
                dst = gate_m if g // 4 == 2 else gate_f
                c0 = (g % 4) * 512
                self.cp("act", dst[:, c0:c0 + 512], gbank[:, :], [gbk], [("gate", g // 4, g % 4)])
        self.tt("dve", mods[:], pm[:, 0:96], adab[:], ALU.add, [pmk, "ada_b_fm"], ["mods"])
        for gi, dst in ((2, gate_m), (5, gate_f)):
            for q4 in range(4):
                self.dma("sp", tmp[q4][:], I_adab_row(self, gi, q4), [], [("tmp", q4)])
                self.tt("dve", dst[:, q4 * 512:(q4 + 1) * 512], dst[:, q4 * 512:(q4 + 1) * 512], tmp[q4][:], ALU.add,
                        [("gate", gi, q4), ("tmp", q4)], [("gate", gi, q4)])
        for i, off in ((0, 16), (1, 64)):
            self.ts("dve", g1[:, i, :], mods[:, off:off + 16], 1.0, None, ALU.add, None, ["mods"], [("g1", i)])
            self.tt("dve", g1[:, i, :], g1[:, i, :], nmg[:, i, :], ALU.mult, [("g1", i), "nmg%d" % i], [("g1", i)])

        if STOP == 1:
            S.barrier(); S.emit(nc); self.st.close(); return
        self.cast_i = 0

        def cast_unit(src, dst, nkc, ncols, perm=None):
            i = self.cast_i
            self.cast_i += 1
            stg, skeys = (stgA, SKEYS) if i % 2 == 0 else (stgB, ["Fb"])
            slot = i % self.NSLOT
            cb = self.wring[:, slot]
            self.dma("sp", stg[:, 0:nkc, 0:ncols], src, [], skeys)
            eng = ("dve", "act", "pool")[i % 3]
            if perm is None:
                self.cp(eng, cb[:, 0:nkc, 0:ncols], stg[:, 0:nkc, 0:ncols], skeys, [("w", slot)])
                self.dma("sp", dst, cb[:, 0:nkc, 0:ncols], [("w", slot)], ["wscr"])
            else:
                dcols = 0
                for (d0, s0, n) in perm:
                    self.cp(eng, cb[:, 0:nkc, d0:d0 + n], stg[:, 0:nkc, s0:s0 + n], skeys, [("w", slot)])
                    dcols = max(dcols, d0 + n)
                self.dma("sp", dst, cb[:, 0:nkc, 0:dcols], [("w", slot)], ["wscr"])

        def cast_block(src2d, K, c0, ncols, dst4, g, dcol0=0, perm=None, dcols=None):
            srcv = src2d.rearrange("(kc p) n -> p kc n", p=128)
            kcw = K // 128
            for k0 in range(0, kcw, 8):
                n = min(8, kcw - k0)
                dc = ncols if dcols is None else dcols
                cast_unit(srcv[:, k0:k0 + n, c0:c0 + ncols], dst4[g, :, k0:k0 + n, dcol0:dcol0 + dc], n, ncols, perm)

        W = self.wscr
        win = I["w_in"]
        for h in range(4):
            cast_block(win, D, h * 256, 256, W["qk"], h, 0)
            cast_block(win, D, 1024 + h * 256, 256, W["qk"], h, 256)
            cast_block(win, D, 2048 + h * 512, 512, W["v"], h)
            cast_block(win, D, 4096 + h * 512, 512, W["g"], h)
            cast_block(win, D, 7248 + h * 512, 512, W["ga"], h)
            cast_block(win, D, 9296 + h * 512, 512, W["gb"], h)
            cast_block(I["mla_wo"], D, h * 512, 512, W["mo"], h)
            cast_block(I["gla_wo"], D, h * 512, 512, W["go"], h)
            cast_block(I["w_out"], D, h * 512, 512, W["out"], h)
            cast_block(I["ffn_dn"], FFH, h * 512, 512, W["dn"], h)
        cast_block(win, D, 6144, 16, W["gk"], 0)
        cast_block(win, D, 6160, 512, W["cq"], 0)
        cast_block(win, D, 6672, 512, W["ckv"], 0)
        cast_block(win, D, 7184, 64, W["kr"], 0, perm=[(0, 0, 64), (64, 32, 32), (96, 0, 32)], dcols=128)
        for g in range(22):
            cast_block(I["ffn_in"], D, g * 512, 512, W["fin"], g)
        for g in range(8):
            perm = []
            for hh in range(2):
                perm += [(hh * 256, hh * 192, 192), (hh * 256 + 192, hh * 192 + 160, 32), (hh * 256 + 224, hh * 192 + 128, 32)]
            cast_block(I["wuq"], 512, g * 384, 384, W["uq"], g, perm=perm, dcols=512)
        for g in range(4):
            permk = [(j * 128, j * 256, 128) for j in range(4)]
            permv = [(j * 128, j * 256 + 128, 128) for j in range(4)]
            for hf in range(2):
                pk_ = [(hf * 256 + j * 128, j * 256, 128) for j in range(2)]
                pv_ = [(hf * 256 + j * 128, j * 256 + 128, 128) for j in range(2)]
                self._cast_partial(cast_unit, I["wukv"], g * 1024 + hf * 512, W["uk"], g, pk_, hf)
                self._cast_partial(cast_unit, I["wukv"], g * 1024 + hf * 512, W["uv"], g, pv_, hf)
        S.barrier()
        if STOP == 2:
            S.emit(nc); self.st.close(); return
        self.memset("dve", S32[:].rearrange("p a b c -> p (a b c)"), 0.0, [("S", h, k) for h in range(4) for k in range(2)])
        self.memset("dve", Sb[:].rearrange("p a b c -> p (a b c)"), 0.0, [("Sb", h, k) for h in range(4) for k in range(2)])

        XT = ["xt"]
        HT = [("hT", kc) for kc in range(16)]

        def rstd_from_ss(ss_ap, inv_n, out_ap, R, Wk, extra_bias=None):
            self.act(out_ap, ss_ap, AF.Ln, R + ["cst"], Wk, scale=inv_n, bias=EPS)
            self.act(out_ap, out_ap, AF.Exp, Wk + ["cst"], Wk, scale=-0.5, bias=(extra_bias if extra_bias is not None else cst[:, 2:3]))

        def norm_to_hT(which):
            self.act(xstg[:], xt[:], AF.Square, XT, ["xstg", "ss"], accum=st1[:, 0:1])
            rstd_from_ss(st1[:, 0:1], 1.0 / D, st1[:, 1:2], ["ss"], ["rstd"])
            self.ts("dve", xstg[:], xt[:], st1[:, 1:2], None, ALU.mult, None, XT + ["rstd", "xstg"], ["xstg"])
            sh0 = 0 if which == 0 else 48
            for q4 in range(4):
                ps, pk = self.bank()
                for j in range(4):
                    kc = q4 * 4 + j
                    self.tr(ps[:, j * 128:(j + 1) * 128], xstg[:, kc * 128:(kc + 1) * 128], ident[:], ["xstg", "ident"], [pk])
                for j in range(4):
                    kc = q4 * 4 + j
                    if True:
                        self.act(hT[:, kc, :], ps[:, j * 128:(j + 1) * 128], AF.Identity, [pk, ("g1", which), "mods"], [("hT", kc)],
                                 scale=g1[:, which, kc:kc + 1], bias=mods[:, sh0 + kc:sh0 + kc + 1])
                    else:
                        self.ts("dve", hT[:, kc, :], ps[:, j * 128:(j + 1) * 128], g1[:, which, kc:kc + 1],
                                mods[:, sh0 + kc:sh0 + kc + 1], ALU.mult, ALU.add, [pk, ("g1", which), "mods"], [("hT", kc)])

        def fm_proj(ps_out, pk, w, wk, kcs, col0, M, rhsT, rkeys, start_kc=0, total=None):
            n = len(kcs)
            for j, kc in enumerate(kcs):
                self.mm(ps_out, w[:, j, col0:col0 + M], rhsT[:, kc, :], kc == 0, kc == (total or 16) - 1, [wk] + rkeys, [pk])

        def tile(ti, prefix):
            key0 = ti * 128 if prefix else PRE + ti * 128
            xsrc = (I["xp"] if prefix else I["xo"])[ti * 128:(ti + 1) * 128, :]
            psrc = (I["posp"] if prefix else I["poso"])[:, ti * 128:(ti + 1) * 128]
            self.dma("sp", xt[:], xsrc, [], XT)
            self.dma("sp", posi[:], psrc.partition_broadcast(64), [], ["posi"])
            norm_to_hT(0)
            if STOP == 31:
                return
            C1 = 6.28125
            C2 = 2 * math.pi - C1
            a, kf, t2, r2 = pf[:, 0, :], pf[:, 1, :], pf[:, 2, :], pf[:, 3, :]
            self.cp("dve", kf, posi[:], ["posi"], ["pf"])
            self.ts("dve", a, kf, invf[:, 0:1], None, ALU.mult, None, ["pf", "invf"], ["pf"])
            self.ts("dve", kf, a, 1.0 / (2 * math.pi), None, ALU.mult, None, ["pf"], ["pf"])
            self.cp("dve", posi[:], kf, ["pf"], ["posi"])
            self.cp("dve", kf, posi[:], ["posi"], ["pf"])
            self.stt("dve", a, kf, -C1, a, ALU.mult, ALU.add, ["pf"], ["pf"])
            self.stt("dve", a, kf, -C2, a, ALU.mult, ALU.add, ["pf"], ["pf"])
            self.ts("dve", t2, a, math.pi, 2 * math.pi, ALU.is_gt, ALU.mult, ["pf"], ["pf"])
            self.tt("dve", a, a, t2, ALU.subtract, ["pf"], ["pf"])
            self.ts("dve", r2, a, math.pi / 2, None, ALU.add, None, ["pf"], ["pf"])
            self.ts("dve", t2, r2, math.pi, 2 * math.pi, ALU.is_gt, ALU.mult, ["pf"], ["pf"])
            self.tt("dve", r2, r2, t2, ALU.subtract, ["pf"], ["pf"])
            self.act(cs[:, 0, :], r2, AF.Sin, ["pf"], ["cs"])
            self.act(cs[:, 1, :], a, AF.Sin, ["pf", "cs"], ["cs"])
            self.ts("dve", cs[:, 1, :], cs[:, 1, :], sgn[:, 0:1], None, ALU.mult, None, ["cs", "sgn"], ["cs"])

            if STOP == 32:
                return
            w, wk = self.wload("gk", 0, 0, 8)
            w2, wk2 = self.wload("gk", 0, 8, 8)
            ps, pk = self.bank()
            for kc in range(16):
                ww, wwk = (w, wk) if kc < 8 else (w2, wk2)
                self.mm(ps[0:16, 0:128], ww[:, kc % 8, 0:16], hT[:, kc, :], kc == 0, kc == 15, [wwk, ("hT", kc)], [pk])
            self.cp("dve", pgk[0:16, :], ps[0:16, 0:128], [pk, "pgk"], ["pgk"])
            for h in range(4):
                ps, pk = self.bank()
                self.mm(ps[:, 0:256], pgk[0:17, :], gkw[0:17, h * 256:(h + 1) * 256], True, True, ["pgk", "gkw"], [pk])
                self.act(eL[:, 0, :], ps[:, 0:256], AF.Exp, [pk], ["e"], scale=-1.0)
                self.act(eL[:, 1, :], eL[:, 0, :], AF.Ln, ["e"], ["L"], bias=1.0)
                Lh = eL[:, 1, :]
                psb_, pkb = self.bank()
                for dkc in range(2):
                    self.mm(psb_[:, dkc * 128:(dkc + 1) * 128], Lh[:, dkc * 128:(dkc + 1) * 128], tri[:], True, True, ["L", "tri"], [pkb])
                self.mm(psb_[:, 256:512], uni[:], Lh, True, True, ["L", "uni"], [pkb])
                self.act(ebuf[:, 0, :], psb_[:, 0:256], AF.Exp, [pkb], ["eb"])
                self.act(ebuf[:, 1, :], psb_[:, 0:256], AF.Exp, [pkb], ["enb"], scale=-1.0)
                self.act(ebuf[:, 2, :], psb_[:, 256:512], AF.Exp, [pkb], ["er"])
                wA, wAk = self.wload("qk", h, 0, 8)
                wB, wBk = self.wload("qk", h, 8, 8)
                for (c0, dst, dkey, eidx, ekey) in ((0, qp, "qp", 0, "eb"), (256, kp, "kp", 1, "enb")):
                    if prefix and dkey == "qp":
                        continue
                    pq, pqk = self.bank()
                    for dkc in range(2):
                        for kc in range(16):
                            ww, wwk = (wA, wAk) if kc < 8 else (wB, wBk)
                            self.mm(pq[:, dkc * 128:(dkc + 1) * 128], ww[:, kc % 8, c0 + dkc * 128:c0 + (dkc + 1) * 128],
                                    hT[:, kc, :], kc == 0, kc == 15, [wwk, ("hT", kc)], [pqk])
                    self.tt("dve", dst[:].rearrange("p a b -> p (a b)"), pq[:, 0:256], ebuf[:, eidx, :], ALU.mult, [pqk, ekey], [dkey])
                pq, pqk = self.bank()
                for kc in range(16):
                    ww, wwk = (wA, wAk) if kc < 8 else (wB, wBk)
                    self.mm(pq[:, 0:256], hT[:, kc, :], ww[:, kc % 8, 256:512], kc == 0, kc == 15, [wwk, ("hT", kc)], [pqk])
                self.tt("dve", kd[:], pq[:, 0:256], ebuf[:, 2, :], ALU.mult, [pqk, "er"], ["kd"])
                pv, pvk = self.proj_tm("v", h, hT, HT)
                self.cp("act", vtm[:], pv[:, :], [pvk], ["vtm"])
                if not prefix:
                    pg, pgk_ = self.proj_tm("g", h, hT, HT)
                    ge, gsl = tmp[0], tmp[1]
                    self.act(ge[:], pg[:, :], AF.Exp, [pgk_], [("tmp", 0)], scale=-1.0)
                    self.ts("pool", ge[:], ge[:], 1.0, None, ALU.add, None, [("tmp", 0)], [("tmp", 0)])
                    self.recip(ge[:], ge[:], [("tmp", 0)], [("tmp", 0)])
                    self.tt("dve", gsl[:], pg[:, :], ge[:], ALU.mult, [pgk_, ("tmp", 0)], [("tmp", 1)])
                    self.tt("pool", gsl[:], gsl[:], onorm[:], ALU.mult, [("tmp", 1), "onorm"], [("tmp", 1)])
                    pa, pak = self.bank()
                    for dkc in range(2):
                        self.mm(pa[:, 0:128], kp[:, dkc, :], qp[:, dkc, :], dkc == 0, dkc == 1, ["kp", "qp"], [pak])
                    self.tt("dve", AT[:], pa[:, 0:128], mask01[:], ALU.mult, [pak, "mask01"], ["AT"])
                    po, pok = self.bank()
                    self.mm(po[:, :], AT[:], vtm[:], True, False, ["AT", "vtm"], [pok])
                    for dkc in range(2):
                        self.mm(po[:, :], qp[:, dkc, :], Sb[:, h, dkc, :], False, dkc == 1, ["qp", ("Sb", h, dkc)], [pok])
                    self.act(tmp[2][:], po[:, :], AF.Square, [pok], [("tmp", 2), "ss2"], accum=st1[:, 2:3])
                    rstd_from_ss(st1[:, 2:3], 1.0 / (512.0 * 256.0), st1[:, 3:4], ["ss2"], ["rstd2"], extra_bias=cst[:, 1:2])
                    self.stt("dve", on[:], po[:, :], st1[:, 3:4], gsl[:], ALU.mult, ALU.mult, [pok, "rstd2", ("tmp", 1)], ["on"])
                    for vc in range(4):
                        self.tr(psb[:, vc * 128:(vc + 1) * 128], on[:, vc * 128:(vc + 1) * 128], identb[:], ["on", "identb"], ["psb"])
                    self.cp("act", big[:, h * 4:(h + 1) * 4, :].rearrange("p a b -> p (a b)"), psb[:, 0:512], ["psb"],
                            [("big", h * 4 + j) for j in range(4)])
                for dkc in range(2):
                    pd, pdk = self.bank()
                    self.mm(pd[:, :], kd[:, dkc * 128:(dkc + 1) * 128], vtm[:], True, True, ["kd", "vtm"], [pdk])
                    self.stt("dve", S32[:, h, dkc, :], S32[:, h, dkc, :], ebuf[:, 0, dkc * 128 + 127:dkc * 128 + 128], pd[:, :],
                             ALU.mult, ALU.add, [("S", h, dkc), "eb", pdk], [("S", h, dkc)])
                    if not (prefix and ti < NTP - 1):
                        if prefix:
                            self.ts("dve", S32[:, h, dkc, :], S32[:, h, dkc, :], flag[:, 0:1], None, ALU.mult, None,
                                    [("S", h, dkc), "flag"], [("S", h, dkc)])
                        self.cp("pool", Sb[:, h, dkc, :], S32[:, h, dkc, :], [("S", h, dkc)], [("Sb", h, dkc)])

            if STOP == 34:
                return
            def latent(blk, gfm, gkey, dst, dkey):
                pc, pck = self.bank()
                wA, wAk = self.wload(blk, 0, 0, 8)
                wB, wBk = self.wload(blk, 0, 8, 8)
                for c in range(4):
                    for kc in range(16):
                        ww, wwk = (wA, wAk) if kc < 8 else (wB, wBk)
                        self.mm(pc[:, c * 128:(c + 1) * 128], ww[:, kc % 8, c * 128:(c + 1) * 128], hT[:, kc, :],
                                kc == 0, kc == 15, [wwk, ("hT", kc)], [pck])
                self.cp("dve", cT[:], pc[:, :], [pck], ["cT"])
                if KSUB == 1:
                    return
                self.act(sq[:], pc[:, :], AF.Square, [pck], ["sq"])
                if KSUB == 2:
                    return
                p2, p2k = self.bank()
                for c in range(4):
                    self.mm(p2[:, 0:128], identb_ones[:], sq[:, c * 128:(c + 1) * 128], c == 0, c == 3, ["sq", "onesb"], [p2k])
                self.cp("dve", rq[:], p2[:, 0:128], [p2k], ["rq"])
                if KSUB == 3:
                    return
                self.act(rq[:], rq[:], AF.Ln, ["rq", "cst"], ["rq"], scale=1.0 / 512.0, bias=EPS)
                self.act(rq[:], rq[:], AF.Exp, ["rq"], ["rq"], scale=-0.5)
                for c in range(4):
                    self.tt("dve", cT[:, c * 128:(c + 1) * 128], cT[:, c * 128:(c + 1) * 128], rq[:], ALU.mult, ["cT", "rq"], ["cT"])
                    self.ts("dve", dst[:, c, :], cT[:, c * 128:(c + 1) * 128], gfm[:, c:c + 1], None, ALU.mult, None,
                            ["cT", gkey], [(dkey, c)])

            identb_ones = self.onesb
            latent("ckv", kvng, "kvng_fm", ckvn, "ckvn")
            CKV = [("ckvn", c) for c in range(4)]
            if STOP == 35:
                return
            wA, wAk = self.wload("kr", 0, 0, 8)
            wB, wBk = self.wload("kr", 0, 8, 8)
            pr, prk = self.bank()
            for half in range(2):
                for kc in range(16):
                    ww, wwk = (wA, wAk) if kc < 8 else (wB, wBk)
                    self.mm(pr[0:64, half * 128:(half + 1) * 128], ww[:, kc % 8, half * 64:(half + 1) * 64], hT[:, kc, :],
                            kc == 0, kc == 15, [wwk, ("hT", kc)], [prk])
            t1, t2_ = tmp[2][0:64, 0:128], tmp[2][0:64, 128:256]
            self.tt("dve", t1, pr[0:64, 0:128], cs[:, 0, :], ALU.mult, [prk, "cs"], [("tmp", 2)])
            self.tt("dve", t2_, pr[0:64, 128:256], cs[:, 1, :], ALU.mult, [prk, "cs", ("tmp", 2)], [("tmp", 2)])
            self.tt("dve", krT[:], t1, t2_, ALU.add, [("tmp", 2)], ["krT"])
            self.dma("sp", KRc[:, key0:key0 + 128], krT[:], ["krT"], [("KRc", key0 // 128)])
            if STOP == 36:
                return
            wk_, wkk = self.wload_multi("uk", 0, 2)
            wk2_, wkk2 = self.wload_multi("uk", 2, 2)
            for g in range(4):
                ww, wwk = (wk_, wkk) if g < 2 else (wk2_, wkk2)
                pk_, pkk = self.bank()
                for j in range(4):
                    for kc in range(4):
                        self.mm(pk_[:, j * 128:(j + 1) * 128], ww[:, (g % 2) * 4 + kc, j * 128:(j + 1) * 128], ckvn[:, kc, :],
                                kc == 0, kc == 3, [wwk] + CKV, [pkk])
                self.cp(self.rot(), KTt[:, g * 4:(g + 1) * 4, :].rearrange("p a b -> p (a b)"), pk_[:, :], [pkk], [("KTt", g)])
            for hh_ in range(16):
                self.dma("sp", KTc[hh_, :, key0:key0 + 128], KTt[:, hh_, :], [("KTt", hh_ // 4)], [("KTc", key0 // 128, hh_)])
            wv_, wvk = self.wload_multi("uv", 0, 2)
            wv2_, wvk2 = self.wload_multi("uv", 2, 2)
            for g in range(4):
                ww, wwk = (wv_, wvk) if g < 2 else (wv2_, wvk2)
                pv_, pvk_ = self.bank()
                for kc in range(4):
                    self.mm(pv_[:, :], ckvn[:, kc, :], ww[:, (g % 2) * 4 + kc, :], kc == 0, kc == 3, [wwk] + CKV, [pvk_])
                self.cp(self.rot(), Vt[:, g * 512:(g + 1) * 512], pv_[:, :], [pvk_], [("Vt", g)])
            self.dma("sp", Vc[key0:key0 + 128, :], Vt[:], [("Vt", g) for g in range(4)], [("Vc", key0 // 128)])
            if prefix:
                return

            latent("cq", qng, "qng_fm", cqn, "cqn")
            CQ = [("cqn", c) for c in range(4)]
            for s2 in range(4):
                wq_, wqk = self.wload_multi("uq", s2 * 2, 2)
                pn, pnk = self.bank()
                pra, prak = self.bank()
                prb, prbk = self.bank()
                for j in range(4):
                    g_, hh = j // 2, j % 2
                    for kc in range(4):
                        self.mm(pn[:, j * 128:(j + 1) * 128], wq_[:, g_ * 4 + kc, hh * 256:hh * 256 + 128], cqn[:, kc, :],
                                kc == 0, kc == 3, [wqk] + CQ, [pnk])
                    for kc in range(4):
                        self.mm(pra[0:64, j * 128:(j + 1) * 128], wq_[:, g_ * 4 + kc, hh * 256 + 128:hh * 256 + 192], cqn[:, kc, :],
                                kc == 0, kc == 3, [wqk] + CQ, [prak])
                    for kc in range(4):
                        self.mm(prb[0:64, j * 128:(j + 1) * 128], wq_[:, g_ * 4 + kc, hh * 256 + 192:hh * 256 + 256], cqn[:, kc, :],
                                kc == 0, kc == 3, [wqk] + CQ, [prbk])
                self.cp("act", QN[:, s2 * 4:(s2 + 1) * 4, :].rearrange("p a b -> p (a b)"), pn[:, :], [pnk], [("QN", s2 * 4 + j) for j in range(4)])
                for j in range(4):
                    h = s2 * 4 + j
                    ta, tb = tmp[3][0:64, 0:128], tmp[3][0:64, 128:256]
                    self.tt("dve", ta, pra[0:64, j * 128:(j + 1) * 128], cs[:, 0, :], ALU.mult, [prak, "cs", ("tmp", 3)], [("tmp", 3)])
                    self.tt("dve", tb, prb[0:64, j * 128:(j + 1) * 128], cs[:, 1, :], ALU.mult, [prbk, "cs", ("tmp", 3)], [("tmp", 3)])
                    self.tt("pool", QR[:, h, :], ta, tb, ALU.add, [("tmp", 3)], [("QR", h)])

            nkb = (key0 + 128) // 128
            npre = PRE // 128
            SC = 192.0 ** -0.5
            KTb = [Fbf[:, par * 2:(par + 1) * 2, :].rearrange("p a k -> p (a k)") for par in range(2)]
            KRb = [Fbf[0:64, 4 + par * 2:4 + (par + 1) * 2, :].rearrange("p a k -> p (a k)") for par in range(2)]
            for h in range(16):
                po, pok = self.ps[3 + h % 2], ("ps", 3 + h % 2)
                for sb0 in range(0, nkb, 16):
                    nb = min(16, nkb - sb0)
                    par = self.kvpar
                    self.kvpar ^= 1
                    k0 = sb0 * 128
                    ckeys = [(nm, kb) for nm in ("KRc", "Vc") for kb in range(sb0, sb0 + nb)] + [("KTc", kb, h) for kb in range(sb0, sb0 + nb)]
                    self.dma("sp", KTb[par][:, 0:nb * 128], KTc[h, :, k0:k0 + nb * 128], ckeys, [("KTb", par)])
                    self.dma("sp", KRb[par][:, 0:nb * 128], KRc[:, k0:k0 + nb * 128], ckeys, [("KRb", par)])
                    self.dma("sp", Vb[:, par, 0:nb, 0:128],
                             Vc[k0:k0 + nb * 128, h * 128:(h + 1) * 128].rearrange("(b p) d -> p b d", p=128), ckeys, [("Vb", par)])
                    for j in range(nb):
                        kb = sb0 + j
                        si = self.st_i
                        self.st_i = (si + 1) % 3
                        pst, pstk = self.ps[si], ("ps", si)
                        diag = kb == nkb - 1
                        self.mm(pst[:, 0:128], KTb[par][:, j * 128:(j + 1) * 128], QN[:, h, :], True, False, [("KTb", par), ("QN", h)], [pstk])
                        self.mm(pst[:, 0:128], KRb[par][:, j * 128:(j + 1) * 128], QR[:, h, :], False, not diag, [("KRb", par), ("QR", h)], [pstk])
                        if diag:
                            self.mm(pst[:, 0:128], identb[:], maskneg[:], False, True, ["identb", "maskneg"], [pstk])
                        bias = flag[:, 1:2] if kb < npre else cst[:, 2:3]
                        self.act(pb[:, si, :], pst[:, 0:128], AF.Exp, [pstk, "flag", "cst"], [("pb", si)], scale=SC, bias=bias)
                        self.mm(po[:, 0:129], pb[:, si, :], Vb[:, par, j, 0:129], kb == 0, kb == nkb - 1, [("pb", si), ("Vb", par)], [pok])
                self.recip(rec[:, 0:1], po[:, 128:129], [pok], ["rec"])
                self.ts("dve", otm[:, h * 128:(h + 1) * 128], po[:, 0:128], rec[:, 0:1], None, ALU.mult, None, [pok, "rec"], [("otm", h)])
            for q4 in range(4):
                for j in range(4):
                    c = q4 * 4 + j
                    self.tr(psb[:, j * 128:(j + 1) * 128], otm[:, c * 128:(c + 1) * 128], identb[:], [("otm", c), "identb"], ["psb"])
                self.cp(self.rot(), big[:, 16 + q4 * 4:16 + (q4 + 1) * 4, :].rearrange("p a b -> p (a b)"), psb[:, 0:512], ["psb"],
                        [("big", 16 + q4 * 4 + j) for j in range(4)])

            ON = [("big", c) for c in range(16)]
            OM = [("big", 16 + c) for c in range(16)]
            onT = big[:, 0:16, :]
            omT = big[:, 16:32, :]
            for g in range(4):
                pa_, pak_ = self.proj_tm("ga", g, hT, HT)
                self.act(tmp[0][:], pa_[:, :], AF.Exp, [pak_], [("tmp", 0)], scale=-1.0)
                pb_, pbk_ = self.proj_tm("gb", g, hT, HT)
                self.act(tmp[1][:], pb_[:, :], AF.Exp, [pbk_], [("tmp", 1)], scale=-1.0)
                for i in range(2):
                    self.ts("pool", tmp[i][:], tmp[i][:], 1.0, None, ALU.add, None, [("tmp", i)], [("tmp", i)])
                    self.recip(tmp[i][:], tmp[i][:], [("tmp", i)], [("tmp", i)])
                py, pyk = self.proj_tm("go", g, onT, ON)
                self.tt("dve", tmp[0][:], py[:, :], tmp[0][:], ALU.mult, [pyk, ("tmp", 0)], [("tmp", 0)])
                pz, pzk = self.proj_tm("mo", g, omT, OM)
                self.tt("dve", tmp[1][:], pz[:, :], tmp[1][:], ALU.mult, [pzk, ("tmp", 1)], [("tmp", 1)])
                self.tt("pool", mtm[:, g * 512:(g + 1) * 512], tmp[0][:], tmp[1][:], ALU.add, [("tmp", 0), ("tmp", 1)], [("mtm", g)])
            mT = QN
            for q4 in range(4):
                for j in range(4):
                    c = q4 * 4 + j
                    self.tr(psb[:, j * 128:(j + 1) * 128], mtm[:, c * 128:(c + 1) * 128], identb[:], [("mtm", q4), "identb"], ["psb"])
                self.cp(self.rot(), mT[:, q4 * 4:(q4 + 1) * 4, :].rearrange("p a b -> p (a b)"), psb[:, 0:512], ["psb"],
                        [("QN", q4 * 4 + j) for j in range(4)])
            MT = [("QN", c) for c in range(16)]
            for g in range(4):
                px, pxk = self.proj_tm("out", g, mT, MT)
                self.tt("dve", tmp[2][:], px[:, :], gate_m[:, g * 512:(g + 1) * 512], ALU.mult, [pxk, ("gate", 2, g)], [("tmp", 2)])
                self.tt("pool", xt[:, g * 512:(g + 1) * 512], xt[:, g * 512:(g + 1) * 512], tmp[2][:], ALU.add, XT + [("tmp", 2)], XT)

            norm_to_hT(1)
            for g in range(11):
                pg_, pgk2 = self.proj_tm("fin", g, hT, HT)
                self.act(tmp[0][:], pg_[:, :], AF.Exp, [pgk2], [("tmp", 0)], scale=-1.0)
                pu_, puk = self.proj_tm("fin", 11 + g, hT, HT)
                self.ts("pool", tmp[0][:], tmp[0][:], 1.0, None, ALU.add, None, [("tmp", 0)], [("tmp", 0)])
                self.recip(tmp[0][:], tmp[0][:], [("tmp", 0)], [("tmp", 0)])
                self.tt("dve", tmp[1][:], pu_[:, :], tmp[0][:], ALU.mult, [puk, ("tmp", 0)], [("tmp", 1)])
                self.tt("dve", on[:], pg_[:, :], tmp[1][:], ALU.mult, [pgk2, ("tmp", 1)], ["on"])
                for j in range(4):
                    self.tr(psb[:, j * 128:(j + 1) * 128], on[:, j * 128:(j + 1) * 128], identb[:], ["on", "identb"], ["psb"])
                self.cp(self.rot(), big[:, g * 4:(g + 1) * 4, :].rearrange("p a b -> p (a b)"), psb[:, 0:512], ["psb"],
                        [("big", g * 4 + j) for j in range(4)])
            AK = [("big", c) for c in range(44)]
            for g in range(4):
                pd_, pdk_ = self.proj_tm("dn", g, big, AK, kcw=44)
                self.tt("dve", tmp[2][:], pd_[:, :], gate_f[:, g * 512:(g + 1) * 512], ALU.mult, [pdk_, ("gate", 5, g)], [("tmp", 2)])
                self.tt("pool", xt[:, g * 512:(g + 1) * 512], xt[:, g * 512:(g + 1) * 512], tmp[2][:], ALU.add, XT + [("tmp", 2)], XT)
            self.act(xstg[:], xt[:], AF.Square, XT, ["xstg", "ss"], accum=st1[:, 0:1])
            rstd_from_ss(st1[:, 0:1], 1.0 / D, st1[:, 1:2], ["ss"], ["rstd"])
            self.stt("dve", xstg[:], xt[:], st1[:, 1:2], fing[:], ALU.mult, ALU.mult, XT + ["rstd", "fing", "xstg"], ["xstg"])
            self.dma("sp", out[ti * 128:(ti + 1) * 128, :], xstg[:], ["xstg"], [("out", ti)])

        self.kvpar = 0
        self.st_i = 0
        self.onesb = sb("onesb", [128, 128], BF16)
        self.memset("pool", self.onesb[:], 1.0, ["onesb"])
        for ti in range(NTP):
            tile(ti, True)
        if STOP == 3 or STOP > 30:
            S.barrier(); S.emit(nc); self.st.close(); return
        for ti in range(NTO):
            tile(ti, False)
        S.add("sp", None, reads=[("out", ti) for ti in range(NTO)])
        S.emit(nc)
        self.st.close()

    def _cast_partial(self, cast_unit, src2d, c0, dst4, g, perm, hf):
        srcv = src2d.rearrange("(kc p) n -> p kc n", p=128)
        p2 = [(d0 - hf * 256, s0, n) for (d0, s0, n) in perm]
        cast_unit(srcv[:, 0:4, c0:c0 + 512], dst4[g, :, 0:4, hf * 256:hf * 256 + 256], 4, 512, p2)


def I_adab_row(b, gi, q4):
    return b._adab_row[:, gi * 2048 + q4 * 512: gi * 2048 + (q4 + 1) * 512].partition_broadcast(128)


_CACHE = {}


def _build(PRE, OWN):
    key = (PRE, OWN)
    if key in _CACHE:
        return _CACHE[key]
    nc = bass.Bass("TRN2", target_bir_lowering=False)
    b = Builder(nc, PRE, OWN)
    b._adab_row = b.din("ada_b_row", [1, 6 * D])
    b.build()
    _CACHE[key] = nc
    return nc


def kernel(x, c, positions, ada_w, ada_b, norm_mix_g, w_in, gla_gk_w, gla_gk_b, gla_onorm_g, gla_wo, mla_q_norm_g,
           mla_wuq, mla_kv_norm_g, mla_wukv, mla_wo, w_out, norm_ffn_g, ffn_w_in, ffn_w_down, final_norm_g):
    f32 = np.float32
    x = np.asarray(x, f32)
    B, SEQ, _ = x.shape
    HALF = SEQ // 2
    nc = _build(HALF, HALF)

    def fm(v, n):
        return np.ascontiguousarray(np.asarray(v, f32).reshape(n, 128).T)

    jj = np.arange(128)
    tri = np.where(jj[:, None] <= jj[None, :], -1.0 / 16.0, 0.0).astype(f32)
    uni = np.where(jj[:, None] > jj[None, :], -1.0 / 16.0, 0.0).astype(f32)
    mask01 = (jj[:, None] <= jj[None, :]).astype(f32)
    maskneg = np.where(jj[:, None] > jj[None, :], NEG, 0.0).astype(ml_dtypes.bfloat16)
    inv_freq = (10000.0 ** (-np.arange(0, 64, 2, dtype=f32) / 64.0)).astype(f32)
    invf = np.concatenate([inv_freq, inv_freq]).reshape(64, 1).astype(f32)
    sgn = np.concatenate([-np.ones(32, f32), np.ones(32, f32)]).reshape(64, 1)
    shared = dict(
        ada_w=np.ascontiguousarray(np.asarray(ada_w, f32)[0]),
        ada_b_fm=fm(np.asarray(ada_b)[0], 96), ada_b_row=np.ascontiguousarray(np.asarray(ada_b, f32)[0].reshape(1, -1)),
        nmg_fm=fm(np.asarray(norm_mix_g)[0], 16), nfg_fm=fm(np.asarray(norm_ffn_g)[0], 16),
        w_in=np.ascontiguousarray(np.asarray(w_in, f32)[0]),
        gkw=np.ascontiguousarray(np.concatenate([np.asarray(gla_gk_w, f32)[0], np.asarray(gla_gk_b, f32)[0][None, :]], 0)),
        onorm=np.ascontiguousarray(np.asarray(gla_onorm_g, f32)[0].reshape(1, 512)),
        gla_wo=np.ascontiguousarray(np.asarray(gla_wo, f32)[0]),
        qng_fm=fm(np.asarray(mla_q_norm_g)[0], 4), kvng_fm=fm(np.asarray(mla_kv_norm_g)[0], 4),
        wuq=np.ascontiguousarray(np.asarray(mla_wuq, f32)[0]), wukv=np.ascontiguousarray(np.asarray(mla_wukv, f32)[0]),
        mla_wo=np.ascontiguousarray(np.asarray(mla_wo, f32)[0]), w_out=np.ascontiguousarray(np.asarray(w_out, f32)[0]),
        ffn_in=np.ascontiguousarray(np.asarray(ffn_w_in, f32)[0]), ffn_dn=np.ascontiguousarray(np.asarray(ffn_w_down, f32)[0]),
        fing=np.ascontiguousarray(np.asarray(final_norm_g, f32).reshape(1, D)),
        ident=np.eye(128, dtype=f32), identb=np.eye(128).astype(ml_dtypes.bfloat16), tri=tri, uni=uni, mask01=mask01,
        maskneg=maskneg, invf=invf, sgn=sgn,
    )
    positions = np.asarray(positions).astype(np.int32)
    c = np.asarray(c, f32)
    in_maps = []
    for core in range(2 * B):
        b, half = core // 2, core % 2
        m = dict(shared)
        m["xp"] = np.ascontiguousarray(x[b, 0:HALF])
        m["xo"] = np.ascontiguousarray(x[b, half * HALF:(half + 1) * HALF])
        m["posp"] = np.ascontiguousarray(positions[b, 0:HALF].reshape(1, HALF))
        m["poso"] = np.ascontiguousarray(positions[b, half * HALF:(half + 1) * HALF].reshape(1, HALF))
        m["c_fm"] = fm(c[b], 16)
        fl = np.zeros((128, 2), f32)
        fl[:, 0] = float(half)
        fl[:, 1] = 0.0 if half == 1 else NEG
        m["flag"] = fl
        in_maps.append(m)
    NCR = int(os.environ.get('KCORES', str(2 * B)))
    res = run_bass_kernel_spmd(nc, in_maps[:NCR], core_ids=list(range(NCR)))
    out = np.zeros((B, SEQ, D), f32)
    for core in range(NCR):
        b, half = core // 2, core % 2
        out[b, half * HALF:(half + 1) * HALF] = res.results[core]["out"]
    return out
```

```python
import math
import os
STOP = int(os.environ.get('KSTOP', '0'))
KSUB = int(os.environ.get('KSUB', '0'))
SQ = os.environ.get('KQ', 'pool')
NSLOT = int(os.environ.get('KNSLOT', '4'))
import contextlib
import numpy as np
import ml_dtypes
import concourse.bass as bass
import concourse.mybir as mybir
from concourse.bass_utils import run_bass_kernel_spmd

F32 = mybir.dt.float32
BF16 = mybir.dt.bfloat16
I32 = mybir.dt.int32
AF = mybir.ActivationFunctionType
ALU = mybir.AluOpType

D = 2048
NKC = 16
FFH = 5632
NEG = -30000.0


class _Op:
    __slots__ = ("eng", "fn", "reads", "writes", "dma", "idx", "waits", "signaled", "dsem", "dval", "rank", "barrier")


class Sched:
    ENGS = ("pe", "act", "dve", "pool", "sp")

    def __init__(self, n_dma_sems=14):
        self.ops = []
        self.n_dma_sems = n_dma_sems

    def add(self, eng, fn, reads=(), writes=(), dma=False, barrier=False):
        op = _Op()
        op.eng = eng
        op.fn = fn
        writes = list(writes) + [r for r in reads if r == "psb" or (isinstance(r, tuple) and r[0] == "ps")]
        op.reads = tuple(reads)
        op.writes = tuple(writes)
        op.dma = dma
        op.signaled = False
        op.barrier = barrier
        self.ops.append(op)
        return op

    def barrier(self):
        for e in self.ENGS:
            self.add(e, None, barrier=True)

    def analyze(self):
        last_w = {}
        readers = {}
        cnt = {e: 0 for e in self.ENGS}
        clock = {e: {} for e in self.ENGS}
        opclock = {}
        dma_rr = {e: 0 for e in self.ENGS}
        dma_val = {}
        dma_last = {}
        last_on = {}
        ops = self.ops
        for i, op in enumerate(ops):
            e = op.eng
            cnt[e] += 1
            op.idx = cnt[e]
            deps = set()
            if op.barrier:
                deps.update(last_on.values())
                deps.update(dma_last.values())
            for r in op.reads:
                w = last_w.get(r)
                if w is not None:
                    deps.add(w)
            for w_ in op.writes:
                w = last_w.get(w_)
                if w is not None:
                    deps.add(w)
                for rd in readers.get(w_, ()):
                    deps.add(rd)
            if op.dma:
                k = dma_rr[e]
                dma_rr[e] = (k + 1) % self.n_dma_sems
                skey = ("dma", e, k)
                op.dsem = skey
                op.dval = dma_val.get(skey, 0) + 16
                dma_val[skey] = op.dval
                prev = dma_last.get(skey)
                if prev is not None:
                    deps.add(prev)
                dma_last[skey] = i
            ck = clock[e]
            waits = []
            for d in sorted(deps):
                dop = ops[d]
                if dop.fn is None:
                    continue
                if dop.dma:
                    key, val = dop.dsem, dop.dval
                else:
                    key, val = dop.eng, dop.idx
                    if key == e and e in ("pe", "sp"):
                        continue
                if ck.get(key, 0) >= val:
                    continue
                waits.append((key, val, d))
                dop.signaled = True
                ck[key] = val
                oc = opclock.get(d)
                if oc is not None:
                    for k2, v2 in oc.items():
                        if ck.get(k2, 0) < v2:
                            ck[k2] = v2
            op.waits = waits
            oc = dict(ck)
            if not op.dma:
                oc[e] = op.idx
                if op.fn is not None:
                    last_on[e] = i
            opclock[i] = oc
            for r in op.reads:
                readers.setdefault(r, []).append(i)
            for w_ in op.writes:
                last_w[w_] = i
                readers[w_] = []
            if len(opclock) > 60000:
                for j in list(opclock.keys())[:30000]:
                    del opclock[j]
        rank = {e: 0 for e in self.ENGS}
        for op in ops:
            if not op.dma and op.signaled:
                rank[op.eng] += 1
                op.rank = rank[op.eng]
        self.stats = dict(cnt=cnt, rank=rank, dma=dict(dma_val))

    def emit(self, nc):
        self.analyze()
        ops = self.ops
        with contextlib.ExitStack() as st:
            sems = {}
            for e in self.ENGS:
                sems[e] = st.enter_context(nc.semaphore("s_" + e))
            for key in self.stats["dma"]:
                sems[key] = st.enter_context(nc.semaphore("d_%s_%d" % (key[1], key[2])))
            block = st.enter_context(nc.Block())
            per = {e: [op for op in ops if op.eng == e] for e in self.ENGS}

            def make(ename):
                def body(eng):
                    for op in per[ename]:
                        for (key, val, d) in op.waits:
                            dop = ops[d]
                            eng.wait_ge(sems[key], val if dop.dma else dop.rank)
                        if op.fn is None:
                            continue
                        ins = op.fn(eng)
                        if op.dma:
                            ins.then_inc(sems[op.dsem], 16)
                        elif op.signaled:
                            ins.then_inc(sems[ename], 1)
                return body

            block.tensor(make("pe"))
            block.scalar(make("act"))
            block.vector(make("dve"))
            block.gpsimd(make("pool"))
            block.sync(make("sp"))


WB = {
    "qk": (4, 16, 512), "v": (4, 16, 512), "g": (4, 16, 512), "gk": (1, 16, 16), "cq": (1, 16, 512),
    "ckv": (1, 16, 512), "kr": (1, 16, 128), "ga": (4, 16, 512), "gb": (4, 16, 512),
    "uq": (8, 4, 512), "uk": (4, 4, 512), "uv": (4, 4, 512),
    "mo": (4, 16, 512), "go": (4, 16, 512), "out": (4, 16, 512), "fin": (22, 16, 512), "dn": (4, 44, 512),
}


class Builder:
    def __init__(self, nc, PRE, OWN):
        self.nc = nc
        self.PRE = PRE
        self.OWN = OWN
        self.NK = PRE + OWN
        self.S = Sched()
        self.st = contextlib.ExitStack()
        self.bank_i = 0
        self.eng_i = 0
        self.wslot_i = 0

    def din(self, name, shape, dt=F32):
        return self.nc.dram_tensor(name, list(shape), dt, kind="ExternalInput").ap()

    def sb(self, name, shape, dt):
        return self.st.enter_context(self.nc.sbuf_tensor("sb_" + name, list(shape), dt))

    def mm(self, out, lhsT, rhs, start, stop, R, W):
        self.S.add("pe", lambda e: e.matmul(out, lhsT=lhsT, rhs=rhs, start=start, stop=stop), R, W)

    def tr(self, out, in_, ident, R, W):
        self.S.add("pe", lambda e: e.transpose(out=out, in_=in_, identity=ident), R, W)

    def act(self, out, in_, func, R, W, scale=1.0, bias=None, accum=None):
        kw = {}
        if bias is not None:
            kw["bias"] = bias
        if accum is not None:
            kw["accum_out"] = accum
        self.S.add("act", lambda e: e.activation(out=out, in_=in_, func=func, scale=scale, **kw), R, W)

    def tt(self, eng, out, a, b, op, R, W):
        self.S.add(eng, lambda e: e.tensor_tensor(out=out, in0=a, in1=b, op=op), R, W)

    def ts(self, eng, out, a, s1, s2, op0, op1, R, W):
        if s2 is None:
            self.S.add(eng, lambda e: e.tensor_scalar(out=out, in0=a, scalar1=s1, scalar2=None, op0=op0), R, W)
        else:
            self.S.add(eng, lambda e: e.tensor_scalar(out=out, in0=a, scalar1=s1, scalar2=s2, op0=op0, op1=op1), R, W)

    def stt(self, eng, out, in0, scalar, in1, op0, op1, R, W):
        self.S.add(eng, lambda e: e.scalar_tensor_tensor(out=out, in0=in0, scalar=scalar, in1=in1, op0=op0, op1=op1), R, W)

    def cp(self, eng, out, in_, R, W):
        if eng == "act":
            self.S.add("act", lambda e: e.activation(out=out, in_=in_, func=AF.Identity), R, W)
        else:
            self.S.add(eng, lambda e: e.tensor_copy(out=out, in_=in_), R, W)

    def recip(self, out, in_, R, W):
        self.S.add("dve", lambda e: e.reciprocal(out=out, in_=in_), R, W)

    def memset(self, eng, ap, val, W):
        self.S.add(eng, lambda e: e.memset(ap, val), (), W)

    def dma(self, q, out, in_, R, W):
        self.S.add(q, lambda e: e.dma_start(out=out, in_=in_), R, W, dma=True)

    def bank(self):
        i = self.bank_i
        self.bank_i = (i + 1) % 7
        return self.ps[i], ("ps", i)

    def rot(self, engs=("act", "dve")):
        self.eng_i += 1
        return engs[self.eng_i % len(engs)]

    def wload(self, blk, g, kc0, nkc):
        i = self.wslot_i
        self.wslot_i = (i + 1) % self.NSLOT
        ng, kcw, ncols = WB[blk]
        dst = self.wring[:, i, 0:nkc, 0:ncols]
        src = self.wscr[blk][g, :, kc0:kc0 + nkc, :]
        self.dma("sp", dst, src, [], [("w", i)])
        return self.wring[:, i], ("w", i)

    def wload_multi(self, blk, g0, ng):
        i = self.wslot_i
        self.wslot_i = (i + 1) % self.NSLOT
        for j in range(ng):
            self.dma("sp", self.wring[:, i, j * 4:(j + 1) * 4, :], self.wscr[blk][g0 + j], [], [("w", i)])
        return self.wring[:, i], ("w", i)

    def proj_tm(self, blk, g, actT, actkeys, ncols=512, col0=0, kcw=16):
        ps, pk = self.bank()
        for half in range(0, kcw, 8):
            n = min(8, kcw - half)
            w, wk = self.wload(blk, g, half, n)
            for j in range(n):
                kc = half + j
                self.mm(ps[:, 0:ncols], actT[:, kc, :], w[:, j, col0:col0 + ncols], kc == 0, kc == kcw - 1,
                        [wk] + actkeys, [pk])
        return ps, pk

    def build(self):
        nc = self.nc
        PRE, OWN, NK = self.PRE, self.OWN, self.NK
        NTP, NTO = PRE // 128, OWN // 128
        self.NSLOT = NSLOT
        I = {}
        I["xp"] = self.din("xp", [PRE, D])
        I["xo"] = self.din("xo", [OWN, D])
        I["posp"] = self.din("posp", [1, PRE], I32)
        I["poso"] = self.din("poso", [1, OWN], I32)
        I["c_fm"] = self.din("c_fm", [128, 16])
        I["flag"] = self.din("flag", [128, 2])
        I["ada_w"] = self.din("ada_w", [D, 6 * D])
        I["ada_b_fm"] = self.din("ada_b_fm", [128, 96])
        I["nmg_fm"] = self.din("nmg_fm", [128, 16])
        I["nfg_fm"] = self.din("nfg_fm", [128, 16])
        I["w_in"] = self.din("w_in", [D, 11344])
        I["gkw"] = self.din("gkw", [17, 1024])
        I["onorm"] = self.din("onorm", [1, 512])
        I["gla_wo"] = self.din("gla_wo", [D, D])
        I["qng_fm"] = self.din("qng_fm", [128, 4])
        I["kvng_fm"] = self.din("kvng_fm", [128, 4])
        I["wuq"] = self.din("wuq", [512, 3072])
        I["wukv"] = self.din("wukv", [512, 4096])
        I["mla_wo"] = self.din("mla_wo", [D, D])
        I["w_out"] = self.din("w_out", [D, D])
        I["ffn_in"] = self.din("ffn_in", [D, 2 * FFH])
        I["ffn_dn"] = self.din("ffn_dn", [FFH, D])
        I["fing"] = self.din("fing", [1, D])
        I["ident"] = self.din("ident", [128, 128])
        I["identb"] = self.din("identb", [128, 128], BF16)
        I["tri"] = self.din("tri", [128, 128])
        I["uni"] = self.din("uni", [128, 128])
        I["mask01"] = self.din("mask01", [128, 128])
        I["maskneg"] = self.din("maskneg", [128, 128], BF16)
        I["invf"] = self.din("invf", [64, 1])
        I["sgn"] = self.din("sgn", [64, 1])
        out = nc.dram_tensor("out", [OWN, D], F32, kind="ExternalOutput").ap()
        self.wscr = {}
        for name, (ng, kcw, ncols) in WB.items():
            self.wscr[name] = nc.dram_tensor("ws_" + name, [ng, 128, kcw, ncols], BF16).ap()
        KTc = nc.dram_tensor("KTc", [16, 128, NK], BF16).ap()
        KRc = nc.dram_tensor("KRc", [64, NK], BF16).ap()
        Vc = nc.dram_tensor("Vc", [NK, 2048], BF16).ap()

        sb = self.sb
        self.ps = [self.st.enter_context(nc.psum_tensor("ps%d" % i, [128, 512], F32)) for i in range(7)]
        psb = self.st.enter_context(nc.psum_tensor("psb", [128, 1024], BF16))
        xstg = sb("xstg", [128, D], F32)
        xt = sb("xt", [128, D], F32)
        hT = sb("hT", [128, 16, 128], BF16)
        self.wring = sb("wring", [128, self.NSLOT, 8, 512], BF16)
        S32 = sb("S32", [128, 4, 2, 512], F32)
        Sb = sb("Sb", [128, 4, 2, 512], BF16)
        Fb = sb("Fb", [128, 8, 512], F32)
        Fbf = Fb[:].bitcast(BF16)
        Vb = sb("Vb", [128, 2, 16, 132], BF16)
        pb = sb("pb", [128, 3, 512], BF16)
        big = sb("big", [128, 48, 128], BF16)
        otm = sb("otm", [128, D], BF16)
        mtm = sb("mtm", [128, D], BF16)
        tmp = [sb("tmp%d" % i, [128, 512], F32) for i in range(4)]
        gate_m = sb("gate_m", [128, D], F32)
        gate_f = sb("gate_f", [128, D], F32)
        fing = sb("fing", [128, D], F32)
        onorm = sb("onormb", [128, 512], F32)
        ident = sb("ident", [128, 128], F32)
        identb = sb("identb", [128, 128], BF16)
        tri = sb("tri", [128, 128], F32)
        uni = sb("uni", [128, 128], F32)
        mask01 = sb("mask01", [128, 128], F32)
        maskneg = sb("maskneg", [128, 128], BF16)
        cst = sb("cst", [128, 8], F32)
        flag = sb("flag", [128, 2], F32)
        mods = sb("mods", [128, 96], F32)
        g1 = sb("g1", [128, 2, 16], F32)
        cfm = sb("cfm", [128, 16], F32)
        cact = sb("cact", [128, 16], F32)
        nmg = sb("nmg", [128, 2, 16], F32)
        adab = sb("adab", [128, 96], F32)
        qng = sb("qng", [128, 4], F32)
        kvng = sb("kvng", [128, 4], F32)
        invf = sb("invf", [64, 1], F32)
        sgn = sb("sgn", [64, 1], F32)
        st1 = sb("st1", [128, 8], F32)
        pgk = sb("pgk", [32, 128], F32)
        gkw = sb("gkw", [17, 1024], F32)
        eL = sb("eL", [128, 2, 256], F32)
        ebuf = sb("ebuf", [128, 3, 256], F32)
        qp = sb("qp", [128, 2, 128], BF16)
        kp = sb("kp", [128, 2, 128], BF16)
        kd = sb("kd", [128, 256], BF16)
        AT = sb("AT", [128, 128], BF16)
        vtm = sb("vtm", [128, 512], BF16)
        on = sb("on", [128, 512], BF16)
        cT = sb("cT", [128, 512], F32)
        sq = sb("sq", [128, 512], BF16)
        rq = sb("rq", [128, 128], F32)
        cqn = sb("cqn", [128, 4, 128], BF16)
        ckvn = sb("ckvn", [128, 4, 128], BF16)
        posi = sb("posi", [64, 128], I32)
        pf = sb("pf", [64, 4, 128], F32)
        cs = sb("cs", [64, 2, 128], F32)
        QN = sb("QN", [128, 16, 128], BF16)
        QR = sb("QR", [64, 16, 128], BF16)
        krT = sb("krT", [64, 128], BF16)
        KTt = sb("KTt", [128, 16, 128], BF16)
        Vt = sb("Vt", [128, D], BF16)
        rec = sb("rec", [128, 2], F32)

        S = self.S
        EPS = cst[:, 0:1]
        if os.environ.get('KDEBUG'):
            print('SBUF bytes remaining', nc.sbuf_bytes_remaining)
        for nm, t in (("ident", ident), ("identb", identb), ("tri", tri), ("uni", uni), ("mask01", mask01),
                      ("maskneg", maskneg), ("flag", flag), ("qng_fm", qng), ("kvng_fm", kvng), ("invf", invf),
                      ("sgn", sgn), ("c_fm", cfm), ("ada_b_fm", adab), ("gkw", gkw)):
            self.dma("sp", t[:], I[nm], [], [nm])
        self.dma("sp", nmg[:, 0, :], I["nmg_fm"], [], ["nmg0"])
        self.dma("sp", nmg[:, 1, :], I["nfg_fm"], [], ["nmg1"])
        self.dma("sp", onorm[:], I["onorm"].partition_broadcast(128), [], ["onorm"])
        self.dma("sp", fing[:], I["fing"].partition_broadcast(128), [], ["fing"])
        self.memset("pool", cst[:, 0:1], 1e-6, ["cst"])
        self.memset("pool", cst[:, 1:2], math.log(1.0 / 16.0), ["cst"])
        self.memset("pool", cst[:, 2:3], 0.0, ["cst"])
        self.memset("pool", pgk[:], 1.0, ["pgk"])
        self.memset("pool", Vb[:], 1.0, [("Vb", 0), ("Vb", 1)])
        self.act(cact[:], cfm[:], AF.Exp, ["c_fm"], ["cact"], scale=-1.0)
        self.ts("dve", cact[:], cact[:], 1.0, None, ALU.add, None, ["cact"], ["cact"])
        self.recip(cact[:], cact[:], ["cact"], ["cact"])
        self.tt("dve", cact[:], cact[:], cfm[:], ALU.mult, ["cact", "c_fm"], ["cact"])
        stgA = S32[:].rearrange("p a b c -> p (a b) c")
        stgB = Fb[:]
        SKEYS = [("S", h, k) for h in range(4) for k in range(2)]
        pm, pmk = self.ps[6], ("ps", 6)
        adaw = I["ada_w"].rearrange("(kc p) n -> p kc n", p=128)
        def crep_k(kc):
            return tmp[kc // 4][:, (kc % 4) * 128:(kc % 4 + 1) * 128], ("tmp", kc // 4)
        for q4 in range(4):
            self.memset("pool", tmp[q4][:], 1.0, [("tmp", q4)])
        for kc in range(16):
            ck_, ckk_ = crep_k(kc)
            self.ts("dve", ck_, ck_, cact[:, kc:kc + 1], None, ALU.mult, None, ["cact", ckk_], [ckk_])
        for g in range(24):
            self.dma("sp", stgA, adaw[:, 0:8, g * 512:(g + 1) * 512], [], SKEYS)
            self.dma("sp", stgB, adaw[:, 8:16, g * 512:(g + 1) * 512], [], ["Fb"])
            for cc in range(4):
                j = g * 4 + cc
                for kc in range(16):
                    stg, skeys = (stgA, SKEYS) if kc < 8 else (stgB, ["Fb"])
                    self.mm(pm[:, j:j + 1], stg[:, kc % 8, cc * 128:(cc + 1) * 128], cact[:, kc:kc + 1],
                            kc == 0, kc == 15, skeys + ["cact"], [pmk])
            if g // 4 in (2, 5):
                gbank, gbk = self.ps[g % 2], ("ps", g % 2)
                for kc in range(16):
                    stg, skeys = (stgA, SKEYS) if kc < 8 else (stgB, ["Fb"])
                    ck_, ckk_ = crep_k(kc)
                    self.mm(gbank[:, :], ck_, stg[:, kc % 8, :], kc == 0, kc == 15, skeys + [ckk_], [gbk])
                dst = gate_m if g // 4 == 2 else gate_f
                c0 = (g % 4) * 512
                self.cp("act", dst[:, c0:c0 + 512], gbank[:, :], [gbk], [("gate", g // 4, g % 4)])
        self.tt("dve", mods[:], pm[:, 0:96], adab[:], ALU.add, [pmk, "ada_b_fm"], ["mods"])
        for gi, dst in ((2, gate_m), (5, gate_f)):
            for q4 in range(4):
                self.dma("sp", tmp[q4][:], I_adab_row(self, gi, q4), [], [("tmp", q4)])
                self.tt("dve", dst[:, q4 * 512:(q4 + 1) * 512], dst[:, q4 * 512:(q4 + 1) * 512], tmp[q4][:], ALU.add,
                        [("gate", gi, q4), ("tmp", q4)], [("gate", gi, q4)])
        for i, off in ((0, 16), (1, 64)):
            self.ts("dve", g1[:, i, :], mods[:, off:off + 16], 1.0, None, ALU.add, None, ["mods"], [("g1", i)])
            self.tt("dve", g1[:, i, :], g1[:, i, :], nmg[:, i, :], ALU.mult, [("g1", i), "nmg%d" % i], [("g1", i)])

        if STOP == 1:
            S.barrier(); S.emit(nc); self.st.close(); return
        self.cast_i = 0

        def cast_unit(src, dst, nkc, ncols, perm=None):
            i = self.cast_i
            self.cast_i += 1
            stg, skeys = (stgA, SKEYS) if i % 2 == 0 else (stgB, ["Fb"])
            slot = i % self.NSLOT
            cb = self.wring[:, slot]
            self.dma("sp", stg[:, 0:nkc, 0:ncols], src, [], skeys)
            eng = ("dve", "act")[i % 2]
            if perm is None:
                self.cp(eng, cb[:, 0:nkc, 0:ncols], stg[:, 0:nkc, 0:ncols], skeys, [("w", slot)])
                self.dma("sp", dst, cb[:, 0:nkc, 0:ncols], [("w", slot)], ["wscr"])
            else:
                dcols = 0
                for (d0, s0, n) in perm:
                    self.cp(eng, cb[:, 0:nkc, d0:d0 + n], stg[:, 0:nkc, s0:s0 + n], skeys, [("w", slot)])
                    dcols = max(dcols, d0 + n)
                self.dma("sp", dst, cb[:, 0:nkc, 0:dcols], [("w", slot)], ["wscr"])

        def cast_block(src2d, K, c0, ncols, dst4, g, dcol0=0, perm=None, dcols=None):
            srcv = src2d.rearrange("(kc p) n -> p kc n", p=128)
            kcw = K // 128
            for k0 in range(0, kcw, 8):
                n = min(8, kcw - k0)
                dc = ncols if dcols is None else dcols
                cast_unit(srcv[:, k0:k0 + n, c0:c0 + ncols], dst4[g, :, k0:k0 + n, dcol0:dcol0 + dc], n, ncols, perm)

        W = self.wscr
        win = I["w_in"]
        for h in range(4):
            cast_block(win, D, h * 256, 256, W["qk"], h, 0)
            cast_block(win, D, 1024 + h * 256, 256, W["qk"], h, 256)
            cast_block(win, D, 2048 + h * 512, 512, W["v"], h)
            cast_block(win, D, 4096 + h * 512, 512, W["g"], h)
            cast_block(win, D, 7248 + h * 512, 512, W["ga"], h)
            cast_block(win, D, 9296 + h * 512, 512, W["gb"], h)
            cast_block(I["mla_wo"], D, h * 512, 512, W["mo"], h)
            cast_block(I["gla_wo"], D, h * 512, 512, W["go"], h)
            cast_block(I["w_out"], D, h * 512, 512, W["out"], h)
            cast_block(I["ffn_dn"], FFH, h * 512, 512, W["dn"], h)
        cast_block(win, D, 6144, 16, W["gk"], 0)
        cast_block(win, D, 6160, 512, W["cq"], 0)
        cast_block(win, D, 6672, 512, W["ckv"], 0)
        cast_block(win, D, 7184, 64, W["kr"], 0, perm=[(0, 0, 64), (64, 32, 32), (96, 0, 32)], dcols=128)
        for g in range(22):
            cast_block(I["ffn_in"], D, g * 512, 512, W["fin"], g)
        for g in range(8):
            perm = []
            for hh in range(2):
                perm += [(hh * 256, hh * 192, 192), (hh * 256 + 192, hh * 192 + 160, 32), (hh * 256 + 224, hh * 192 + 128, 32)]
            cast_block(I["wuq"], 512, g * 384, 384, W["uq"], g, perm=perm, dcols=512)
        for g in range(4):
            permk = [(j * 128, j * 256, 128) for j in range(4)]
            permv = [(j * 128, j * 256 + 128, 128) for j in range(4)]
            for hf in range(2):
                pk_ = [(hf * 256 + j * 128, j * 256, 128) for j in range(2)]
                pv_ = [(hf * 256 + j * 128, j * 256 + 128, 128) for j in range(2)]
                self._cast_partial(cast_unit, I["wukv"], g * 1024 + hf * 512, W["uk"], g, pk_, hf)
                self._cast_partial(cast_unit, I["wukv"], g * 1024 + hf * 512, W["uv"], g, pv_, hf)
        S.barrier()
        if STOP == 2:
            S.emit(nc); self.st.close(); return
        self.memset("dve", S32[:].rearrange("p a b c -> p (a b c)"), 0.0, [("S", h, k) for h in range(4) for k in range(2)])
        self.memset("dve", Sb[:].rearrange("p a b c -> p (a b c)"), 0.0, [("Sb", h, k) for h in range(4) for k in range(2)])

        XT = ["xt"]
        HT = [("hT", kc) for kc in range(16)]

        def rstd_from_ss(ss_ap, inv_n, out_ap, R, Wk, extra_bias=None):
            self.act(out_ap, ss_ap, AF.Ln, R + ["cst"], Wk, scale=inv_n, bias=EPS)
            self.act(out_ap, out_ap, AF.Exp, Wk + ["cst"], Wk, scale=-0.5, bias=(extra_bias if extra_bias is not None else cst[:, 2:3]))

        def norm_to_hT(which):
            self.act(xstg[:], xt[:], AF.Square, XT, ["xstg", "ss"], accum=st1[:, 0:1])
            rstd_from_ss(st1[:, 0:1], 1.0 / D, st1[:, 1:2], ["ss"], ["rstd"])
            self.ts("dve", xstg[:], xt[:], st1[:, 1:2], None, ALU.mult, None, XT + ["rstd", "xstg"], ["xstg"])
            sh0 = 0 if which == 0 else 48
            for q4 in range(4):
                ps, pk = self.bank()
                for j in range(4):
                    kc = q4 * 4 + j
                    self.tr(ps[:, j * 128:(j + 1) * 128], xstg[:, kc * 128:(kc + 1) * 128], ident[:], ["xstg", "ident"], [pk])
                for j in range(4):
                    kc = q4 * 4 + j
                    if True:
                        self.act(hT[:, kc, :], ps[:, j * 128:(j + 1) * 128], AF.Identity, [pk, ("g1", which), "mods"], [("hT", kc)],
                                 scale=g1[:, which, kc:kc + 1], bias=mods[:, sh0 + kc:sh0 + kc + 1])
                    else:
                        self.ts("dve", hT[:, kc, :], ps[:, j * 128:(j + 1) * 128], g1[:, which, kc:kc + 1],
                                mods[:, sh0 + kc:sh0 + kc + 1], ALU.mult, ALU.add, [pk, ("g1", which), "mods"], [("hT", kc)])

        def fm_proj(ps_out, pk, w, wk, kcs, col0, M, rhsT, rkeys, start_kc=0, total=None):
            n = len(kcs)
            for j, kc in enumerate(kcs):
                self.mm(ps_out, w[:, j, col0:col0 + M], rhsT[:, kc, :], kc == 0, kc == (total or 16) - 1, [wk] + rkeys, [pk])

        def tile(ti, prefix):
            key0 = ti * 128 if prefix else PRE + ti * 128
            xsrc = (I["xp"] if prefix else I["xo"])[ti * 128:(ti + 1) * 128, :]
            psrc = (I["posp"] if prefix else I["poso"])[:, ti * 128:(ti + 1) * 128]
            self.dma(SQ, xt[:], xsrc, [], XT)
            self.dma(SQ, posi[:], psrc.partition_broadcast(64), [], ["posi"])
            norm_to_hT(0)
            if STOP == 31:
                return
            C1 = 6.28125
            C2 = 2 * math.pi - C1
            a, kf, t2, r2 = pf[:, 0, :], pf[:, 1, :], pf[:, 2, :], pf[:, 3, :]
            self.cp("dve", kf, posi[:], ["posi"], ["pf"])
            self.ts("dve", a, kf, invf[:, 0:1], None, ALU.mult, None, ["pf", "invf"], ["pf"])
            self.ts("dve", kf, a, 1.0 / (2 * math.pi), None, ALU.mult, None, ["pf"], ["pf"])
            self.cp("dve", posi[:], kf, ["pf"], ["posi"])
            self.cp("dve", kf, posi[:], ["posi"], ["pf"])
            self.stt("dve", a, kf, -C1, a, ALU.mult, ALU.add, ["pf"], ["pf"])
            self.stt("dve", a, kf, -C2, a, ALU.mult, ALU.add, ["pf"], ["pf"])
            self.ts("dve", t2, a, math.pi, 2 * math.pi, ALU.is_gt, ALU.mult, ["pf"], ["pf"])
            self.tt("dve", a, a, t2, ALU.subtract, ["pf"], ["pf"])
            self.ts("dve", r2, a, math.pi / 2, None, ALU.add, None, ["pf"], ["pf"])
            self.ts("dve", t2, r2, math.pi, 2 * math.pi, ALU.is_gt, ALU.mult, ["pf"], ["pf"])
            self.tt("dve", r2, r2, t2, ALU.subtract, ["pf"], ["pf"])
            self.act(cs[:, 0, :], r2, AF.Sin, ["pf"], ["cs"])
            self.act(cs[:, 1, :], a, AF.Sin, ["pf", "cs"], ["cs"])
            self.ts("dve", cs[:, 1, :], cs[:, 1, :], sgn[:, 0:1], None, ALU.mult, None, ["cs", "sgn"], ["cs"])

            if STOP == 32:
                return
            w, wk = self.wload("gk", 0, 0, 8)
            w2, wk2 = self.wload("gk", 0, 8, 8)
            ps, pk = self.bank()
            for kc in range(16):
                ww, wwk = (w, wk) if kc < 8 else (w2, wk2)
                self.mm(ps[0:16, 0:128], ww[:, kc % 8, 0:16], hT[:, kc, :], kc == 0, kc == 15, [wwk, ("hT", kc)], [pk])
            self.cp("dve", pgk[0:16, :], ps[0:16, 0:128], [pk, "pgk"], ["pgk"])
            for h in range(4):
                ps, pk = self.bank()
                self.mm(ps[:, 0:256], pgk[0:17, :], gkw[0:17, h * 256:(h + 1) * 256], True, True, ["pgk", "gkw"], [pk])
                self.act(eL[:, 0, :], ps[:, 0:256], AF.Exp, [pk], ["e"], scale=-1.0)
                self.act(eL[:, 1, :], eL[:, 0, :], AF.Ln, ["e"], ["L"], bias=1.0)
                Lh = eL[:, 1, :]
                psb_, pkb = self.bank()
                for dkc in range(2):
                    self.mm(psb_[:, dkc * 128:(dkc + 1) * 128], Lh[:, dkc * 128:(dkc + 1) * 128], tri[:], True, True, ["L", "tri"], [pkb])
                self.mm(psb_[:, 256:512], uni[:], Lh, True, True, ["L", "uni"], [pkb])
                self.act(ebuf[:, 0, :], psb_[:, 0:256], AF.Exp, [pkb], ["eb"])
                if not prefix:
                    self.act(ebuf[:, 1, :], psb_[:, 0:256], AF.Exp, [pkb], ["enb"], scale=-1.0)
                self.act(ebuf[:, 2, :], psb_[:, 256:512], AF.Exp, [pkb], ["er"])
                wA, wAk = self.wload("qk", h, 0, 8)
                wB, wBk = self.wload("qk", h, 8, 8)
                for (c0, dst, dkey, eidx, ekey) in ((0, qp, "qp", 0, "eb"), (256, kp, "kp", 1, "enb")):
                    if prefix:
                        continue
                    pq, pqk = self.bank()
                    for dkc in range(2):
                        for kc in range(16):
                            ww, wwk = (wA, wAk) if kc < 8 else (wB, wBk)
                            self.mm(pq[:, dkc * 128:(dkc + 1) * 128], ww[:, kc % 8, c0 + dkc * 128:c0 + (dkc + 1) * 128],
                                    hT[:, kc, :], kc == 0, kc == 15, [wwk, ("hT", kc)], [pqk])
                    self.tt("dve", dst[:].rearrange("p a b -> p (a b)"), pq[:, 0:256], ebuf[:, eidx, :], ALU.mult, [pqk, ekey], [dkey])
                pq, pqk = self.bank()
                for kc in range(16):
                    ww, wwk = (wA, wAk) if kc < 8 else (wB, wBk)
                    self.mm(pq[:, 0:256], hT[:, kc, :], ww[:, kc % 8, 256:512], kc == 0, kc == 15, [wwk, ("hT", kc)], [pqk])
                self.tt("dve", kd[:], pq[:, 0:256], ebuf[:, 2, :], ALU.mult, [pqk, "er"], ["kd"])
                pv, pvk = self.proj_tm("v", h, hT, HT)
                self.cp("act", vtm[:], pv[:, :], [pvk], ["vtm"])
                if not prefix:
                    pg, pgk_ = self.proj_tm("g", h, hT, HT)
                    ge, gsl = tmp[0], tmp[1]
                    self.act(ge[:], pg[:, :], AF.Exp, [pgk_], [("tmp", 0)], scale=-1.0)
                    self.ts("dve", ge[:], ge[:], 1.0, None, ALU.add, None, [("tmp", 0)], [("tmp", 0)])
                    self.recip(ge[:], ge[:], [("tmp", 0)], [("tmp", 0)])
                    self.tt("dve", gsl[:], pg[:, :], ge[:], ALU.mult, [pgk_, ("tmp", 0)], [("tmp", 1)])
                    self.tt("dve", gsl[:], gsl[:], onorm[:], ALU.mult, [("tmp", 1), "onorm"], [("tmp", 1)])
                    pa, pak = self.bank()
                    for dkc in range(2):
                        self.mm(pa[:, 0:128], kp[:, dkc, :], qp[:, dkc, :], dkc == 0, dkc == 1, ["kp", "qp"], [pak])
                    self.tt("dve", AT[:], pa[:, 0:128], mask01[:], ALU.mult, [pak, "mask01"], ["AT"])
                    po, pok = self.bank()
                    self.mm(po[:, :], AT[:], vtm[:], True, False, ["AT", "vtm"], [pok])
                    for dkc in range(2):
                        self.mm(po[:, :], qp[:, dkc, :], Sb[:, h, dkc, :], False, dkc == 1, ["qp", ("Sb", h, dkc)], [pok])
                    self.act(tmp[2][:], po[:, :], AF.Square, [pok], [("tmp", 2), "ss2"], accum=st1[:, 2:3])
                    rstd_from_ss(st1[:, 2:3], 1.0 / (512.0 * 256.0), st1[:, 3:4], ["ss2"], ["rstd2"], extra_bias=cst[:, 1:2])
                    self.stt("dve", on[:], po[:, :], st1[:, 3:4], gsl[:], ALU.mult, ALU.mult, [pok, "rstd2", ("tmp", 1)], ["on"])
                    for vc in range(4):
                        self.tr(psb[:, vc * 128:(vc + 1) * 128], on[:, vc * 128:(vc + 1) * 128], identb[:], ["on", "identb"], ["psb"])
                    self.cp("act", big[:, h * 4:(h + 1) * 4, :].rearrange("p a b -> p (a b)"), psb[:, 0:512], ["psb"],
                            [("big", h * 4 + j) for j in range(4)])
                for dkc in range(2):
                    pd, pdk = self.bank()
                    self.mm(pd[:, :], kd[:, dkc * 128:(dkc + 1) * 128], vtm[:], True, True, ["kd", "vtm"], [pdk])
                    self.stt("dve", S32[:, h, dkc, :], S32[:, h, dkc, :], ebuf[:, 0, dkc * 128 + 127:dkc * 128 + 128], pd[:, :],
                             ALU.mult, ALU.add, [("S", h, dkc), "eb", pdk], [("S", h, dkc)])
                    if not (prefix and ti < NTP - 1):
                        if prefix:
                            self.ts("dve", S32[:, h, dkc, :], S32[:, h, dkc, :], flag[:, 0:1], None, ALU.mult, None,
                                    [("S", h, dkc), "flag"], [("S", h, dkc)])
                        self.cp("act", Sb[:, h, dkc, :], S32[:, h, dkc, :], [("S", h, dkc)], [("Sb", h, dkc)])

            if STOP == 34:
                return
            def latent(blk, gfm, gkey, dst, dkey):
                pc, pck = self.bank()
                wA, wAk = self.wload(blk, 0, 0, 8)
                wB, wBk = self.wload(blk, 0, 8, 8)
                for c in range(4):
                    for kc in range(16):
                        ww, wwk = (wA, wAk) if kc < 8 else (wB, wBk)
                        self.mm(pc[:, c * 128:(c + 1) * 128], ww[:, kc % 8, c * 128:(c + 1) * 128], hT[:, kc, :],
                                kc == 0, kc == 15, [wwk, ("hT", kc)], [pck])
                self.cp("dve", cT[:], pc[:, :], [pck], ["cT"])
                if KSUB == 1:
                    return
                self.act(sq[:], pc[:, :], AF.Square, [pck], ["sq"])
                if KSUB == 2:
                    return
                p2, p2k = self.bank()
                for c in range(4):
                    self.mm(p2[:, 0:128], identb_ones[:], sq[:, c * 128:(c + 1) * 128], c == 0, c == 3, ["sq", "onesb"], [p2k])
                self.cp("dve", rq[:], p2[:, 0:128], [p2k], ["rq"])
                if KSUB == 3:
                    return
                self.act(rq[:], rq[:], AF.Ln, ["rq", "cst"], ["rq"], scale=1.0 / 512.0, bias=EPS)
                self.act(rq[:], rq[:], AF.Exp, ["rq"], ["rq"], scale=-0.5)
                for c in range(4):
                    self.tt("dve", cT[:, c * 128:(c + 1) * 128], cT[:, c * 128:(c + 1) * 128], rq[:], ALU.mult, ["cT", "rq"], ["cT"])
                    self.ts("dve", dst[:, c, :], cT[:, c * 128:(c + 1) * 128], gfm[:, c:c + 1], None, ALU.mult, None,
                            ["cT", gkey], [(dkey, c)])

            identb_ones = self.onesb
            latent("ckv", kvng, "kvng_fm", ckvn, "ckvn")
            CKV = [("ckvn", c) for c in range(4)]
            if STOP == 35:
                return
            wA, wAk = self.wload("kr", 0, 0, 8)
            wB, wBk = self.wload("kr", 0, 8, 8)
            pr, prk = self.bank()
            for half in range(2):
                for kc in range(16):
                    ww, wwk = (wA, wAk) if kc < 8 else (wB, wBk)
                    self.mm(pr[0:64, half * 128:(half + 1) * 128], ww[:, kc % 8, half * 64:(half + 1) * 64], hT[:, kc, :],
                            kc == 0, kc == 15, [wwk, ("hT", kc)], [prk])
            t1, t2_ = tmp[2][0:64, 0:128], tmp[2][0:64, 128:256]
            self.tt("dve", t1, pr[0:64, 0:128], cs[:, 0, :], ALU.mult, [prk, "cs"], [("tmp", 2)])
            self.tt("dve", t2_, pr[0:64, 128:256], cs[:, 1, :], ALU.mult, [prk, "cs", ("tmp", 2)], [("tmp", 2)])
            self.tt("dve", krT[:], t1, t2_, ALU.add, [("tmp", 2)], ["krT"])
            self.dma(SQ, KRc[:, key0:key0 + 128], krT[:], ["krT"], [("KRc", key0 // 128)])
            if STOP == 36:
                return
            wk_, wkk = self.wload_multi("uk", 0, 2)
            wk2_, wkk2 = self.wload_multi("uk", 2, 2)
            for g in range(4):
                ww, wwk = (wk_, wkk) if g < 2 else (wk2_, wkk2)
                pk_, pkk = self.bank()
                for j in range(4):
                    for kc in range(4):
                        self.mm(pk_[:, j * 128:(j + 1) * 128], ww[:, (g % 2) * 4 + kc, j * 128:(j + 1) * 128], ckvn[:, kc, :],
                                kc == 0, kc == 3, [wwk] + CKV, [pkk])
                self.cp(self.rot(), KTt[:, g * 4:(g + 1) * 4, :].rearrange("p a b -> p (a b)"), pk_[:, :], [pkk], [("KTt", g)])
            for hh_ in range(16):
                self.dma(SQ, KTc[hh_, :, key0:key0 + 128], KTt[:, hh_, :], [("KTt", hh_ // 4)], [("KTc", key0 // 128, hh_)])
            wv_, wvk = self.wload_multi("uv", 0, 2)
            wv2_, wvk2 = self.wload_multi("uv", 2, 2)
            for g in range(4):
                ww, wwk = (wv_, wvk) if g < 2 else (wv2_, wvk2)
                pv_, pvk_ = self.bank()
                for kc in range(4):
                    self.mm(pv_[:, :], ckvn[:, kc, :], ww[:, (g % 2) * 4 + kc, :], kc == 0, kc == 3, [wwk] + CKV, [pvk_])
                self.cp(self.rot(), Vt[:, g * 512:(g + 1) * 512], pv_[:, :], [pvk_], [("Vt", g)])
            self.dma(SQ, Vc[key0:key0 + 128, :], Vt[:], [("Vt", g) for g in range(4)], [("Vc", key0 // 128)])
            if prefix:
                return

            latent("cq", qng, "qng_fm", cqn, "cqn")
            CQ = [("cqn", c) for c in range(4)]
            for s2 in range(4):
                wq_, wqk = self.wload_multi("uq", s2 * 2, 2)
                pn, pnk = self.bank()
                pra, prak = self.bank()
                prb, prbk = self.bank()
                for j in range(4):
                    g_, hh = j // 2, j % 2
                    for kc in range(4):
                        self.mm(pn[:, j * 128:(j + 1) * 128], wq_[:, g_ * 4 + kc, hh * 256:hh * 256 + 128], cqn[:, kc, :],
                                kc == 0, kc == 3, [wqk] + CQ, [pnk])
                    for kc in range(4):
                        self.mm(pra[0:64, j * 128:(j + 1) * 128], wq_[:, g_ * 4 + kc, hh * 256 + 128:hh * 256 + 192], cqn[:, kc, :],
                                kc == 0, kc == 3, [wqk] + CQ, [prak])
                    for kc in range(4):
                        self.mm(prb[0:64, j * 128:(j + 1) * 128], wq_[:, g_ * 4 + kc, hh * 256 + 192:hh * 256 + 256], cqn[:, kc, :],
                                kc == 0, kc == 3, [wqk] + CQ, [prbk])
                self.cp("act", QN[:, s2 * 4:(s2 + 1) * 4, :].rearrange("p a b -> p (a b)"), pn[:, :], [pnk], [("QN", s2 * 4 + j) for j in range(4)])
                for j in range(4):
                    h = s2 * 4 + j
                    ta, tb = tmp[3][0:64, 0:128], tmp[3][0:64, 128:256]
                    self.tt("dve", ta, pra[0:64, j * 128:(j + 1) * 128], cs[:, 0, :], ALU.mult, [prak, "cs", ("tmp", 3)], [("tmp", 3)])
                    self.tt("dve", tb, prb[0:64, j * 128:(j + 1) * 128], cs[:, 1, :], ALU.mult, [prbk, "cs", ("tmp", 3)], [("tmp", 3)])
                    self.tt("dve", QR[:, h, :], ta, tb, ALU.add, [("tmp", 3)], [("QR", h)])

            nkb = (key0 + 128) // 128
            npre = PRE // 128
            SC = 192.0 ** -0.5
            KTb = [Fbf[:, par * 2:(par + 1) * 2, :].rearrange("p a k -> p (a k)") for par in range(2)]
            KRb = [Fbf[0:64, 4 + par * 2:4 + (par + 1) * 2, :].rearrange("p a k -> p (a k)") for par in range(2)]
            items = []
            for h in range(16):
                for sb0 in range(0, nkb, 16):
                    nb = min(16, nkb - sb0)
                    par = self.kvpar
                    self.kvpar ^= 1
                    j = 0
                    while j < nb:
                        kb = sb0 + j
                        lim = min(4, nb - j)
                        if kb < npre:
                            lim = min(lim, npre - kb)
                        items.append(dict(h=h, par=par, j=j, kb=kb, n=lim, load=(sb0, nb) if j == 0 else None))
                        j += lim

            def emit_S(it):
                h, par = it["h"], it["par"]
                if it["load"] is not None:
                    sb0, nb = it["load"]
                    k0 = sb0 * 128
                    ckeys = [(nm, kb) for nm in ("KRc", "Vc") for kb in range(sb0, sb0 + nb)] + [("KTc", kb, h) for kb in range(sb0, sb0 + nb)]
                    self.dma("sp", KTb[par][:, 0:nb * 128], KTc[h, :, k0:k0 + nb * 128], ckeys, [("KTb", par)])
                    self.dma("sp", KRb[par][:, 0:nb * 128], KRc[:, k0:k0 + nb * 128], ckeys, [("KRb", par)])
                    self.dma("sp", Vb[:, par, 0:nb, 0:128],
                             Vc[k0:k0 + nb * 128, h * 128:(h + 1) * 128].rearrange("(b p) d -> p b d", p=128), ckeys, [("Vb", par)])
                si = self.st_i
                self.st_i = (si + 1) % 3
                it["si"] = si
                pst, pstk = self.ps[si], ("ps", si)
                n = it["n"]
                for jj in range(n):
                    kb = it["kb"] + jj
                    j = it["j"] + jj
                    diag = kb == nkb - 1
                    c0 = jj * 128
                    self.mm(pst[:, c0:c0 + 128], KTb[par][:, j * 128:(j + 1) * 128], QN[:, h, :], True, False, [("KTb", par), ("QN", h)], [pstk])
                    self.mm(pst[:, c0:c0 + 128], KRb[par][:, j * 128:(j + 1) * 128], QR[:, h, :], False, not diag, [("KRb", par), ("QR", h)], [pstk])
                    if diag:
                        self.mm(pst[:, c0:c0 + 128], identb[:], maskneg[:], False, True, ["identb", "maskneg"], [pstk])
                bias = flag[:, 1:2] if it["kb"] < npre else cst[:, 2:3]
                self.act(pb[:, si, 0:n * 128], pst[:, 0:n * 128], AF.Exp, [pstk, "flag", "cst"], [("pb", si)], scale=SC, bias=bias)

            def emit_PV(it):
                h, par, si = it["h"], it["par"], it["si"]
                po, pok = self.ps[3 + h % 2], ("ps", 3 + h % 2)
                for jj in range(it["n"]):
                    kb = it["kb"] + jj
                    j = it["j"] + jj
                    self.mm(po[:, 0:129], pb[:, si, jj * 128:(jj + 1) * 128], Vb[:, par, j, 0:129], kb == 0, kb == nkb - 1,
                            [("pb", si), ("Vb", par)], [pok])
                if it["kb"] + it["n"] == nkb:
                    self.recip(rec[:, 0:1], po[:, 128:129], [pok], ["rec"])
                    self.ts("dve", otm[:, h * 128:(h + 1) * 128], po[:, 0:128], rec[:, 0:1], None, ALU.mult, None, [pok, "rec"], [("otm", h)])

            emit_S(items[0])
            for i in range(len(items)):
                if i + 1 < len(items):
                    emit_S(items[i + 1])
                emit_PV(items[i])
            for q4 in range(4):
                for j in range(4):
                    c = q4 * 4 + j
                    self.tr(psb[:, j * 128:(j + 1) * 128], otm[:, c * 128:(c + 1) * 128], identb[:], [("otm", c), "identb"], ["psb"])
                self.cp(self.rot(), big[:, 16 + q4 * 4:16 + (q4 + 1) * 4, :].rearrange("p a b -> p (a b)"), psb[:, 0:512], ["psb"],
                        [("big", 16 + q4 * 4 + j) for j in range(4)])

            ON = [("big", c) for c in range(16)]
            OM = [("big", 16 + c) for c in range(16)]
            onT = big[:, 0:16, :]
            omT = big[:, 16:32, :]
            for g in range(4):
                pa_, pak_ = self.proj_tm("ga", g, hT, HT)
                self.act(tmp[0][:], pa_[:, :], AF.Exp, [pak_], [("tmp", 0)], scale=-1.0)
                pb_, pbk_ = self.proj_tm("gb", g, hT, HT)
                self.act(tmp[1][:], pb_[:, :], AF.Exp, [pbk_], [("tmp", 1)], scale=-1.0)
                for i in range(2):
                    self.ts("dve", tmp[i][:], tmp[i][:], 1.0, None, ALU.add, None, [("tmp", i)], [("tmp", i)])
                    self.recip(tmp[i][:], tmp[i][:], [("tmp", i)], [("tmp", i)])
                py, pyk = self.proj_tm("go", g, onT, ON)
                self.tt("dve", tmp[0][:], py[:, :], tmp[0][:], ALU.mult, [pyk, ("tmp", 0)], [("tmp", 0)])
                pz, pzk = self.proj_tm("mo", g, omT, OM)
                self.tt("dve", tmp[1][:], pz[:, :], tmp[1][:], ALU.mult, [pzk, ("tmp", 1)], [("tmp", 1)])
                self.tt("dve", mtm[:, g * 512:(g + 1) * 512], tmp[0][:], tmp[1][:], ALU.add, [("tmp", 0), ("tmp", 1)], [("mtm", g)])
            mT = QN
            for q4 in range(4):
                for j in range(4):
                    c = q4 * 4 + j
                    self.tr(psb[:, j * 128:(j + 1) * 128], mtm[:, c * 128:(c + 1) * 128], identb[:], [("mtm", q4), "identb"], ["psb"])
                self.cp(self.rot(), mT[:, q4 * 4:(q4 + 1) * 4, :].rearrange("p a b -> p (a b)"), psb[:, 0:512], ["psb"],
                        [("QN", q4 * 4 + j) for j in range(4)])
            MT = [("QN", c) for c in range(16)]
            for g in range(4):
                px, pxk = self.proj_tm("out", g, mT, MT)
                self.tt("dve", tmp[2][:], px[:, :], gate_m[:, g * 512:(g + 1) * 512], ALU.mult, [pxk, ("gate", 2, g)], [("tmp", 2)])
                self.tt("dve", xt[:, g * 512:(g + 1) * 512], xt[:, g * 512:(g + 1) * 512], tmp[2][:], ALU.add, XT + [("tmp", 2)], XT)

            norm_to_hT(1)
            for g in range(11):
                pg_, pgk2 = self.proj_tm("fin", g, hT, HT)
                self.act(tmp[0][:], pg_[:, :], AF.Exp, [pgk2], [("tmp", 0)], scale=-1.0)
                pu_, puk = self.proj_tm("fin", 11 + g, hT, HT)
                self.ts("dve", tmp[0][:], tmp[0][:], 1.0, None, ALU.add, None, [("tmp", 0)], [("tmp", 0)])
                self.recip(tmp[0][:], tmp[0][:], [("tmp", 0)], [("tmp", 0)])
                self.tt("dve", tmp[1][:], pu_[:, :], tmp[0][:], ALU.mult, [puk, ("tmp", 0)], [("tmp", 1)])
                self.tt("dve", on[:], pg_[:, :], tmp[1][:], ALU.mult, [pgk2, ("tmp", 1)], ["on"])
                for j in range(4):
                    self.tr(psb[:, j * 128:(j + 1) * 128], on[:, j * 128:(j + 1) * 128], identb[:], ["on", "identb"], ["psb"])
                self.cp(self.rot(), big[:, g * 4:(g + 1) * 4, :].rearrange("p a b -> p (a b)"), psb[:, 0:512], ["psb"],
                        [("big", g * 4 + j) for j in range(4)])
            AK = [("big", c) for c in range(44)]
            for g in range(4):
                pd_, pdk_ = self.proj_tm("dn", g, big, AK, kcw=44)
                self.tt("dve", tmp[2][:], pd_[:, :], gate_f[:, g * 512:(g + 1) * 512], ALU.mult, [pdk_, ("gate", 5, g)], [("tmp", 2)])
                self.tt("dve", xt[:, g * 512:(g + 1) * 512], xt[:, g * 512:(g + 1) * 512], tmp[2][:], ALU.add, XT + [("tmp", 2)], XT)
            self.act(xstg[:], xt[:], AF.Square, XT, ["xstg", "ss"], accum=st1[:, 0:1])
            rstd_from_ss(st1[:, 0:1], 1.0 / D, st1[:, 1:2], ["ss"], ["rstd"])
            self.stt("dve", xstg[:], xt[:], st1[:, 1:2], fing[:], ALU.mult, ALU.mult, XT + ["rstd", "fing", "xstg"], ["xstg"])
            self.dma(SQ, out[ti * 128:(ti + 1) * 128, :], xstg[:], ["xstg"], [("out", ti)])

        self.kvpar = 0
        self.st_i = 0
        self.onesb = sb("onesb", [128, 128], BF16)
        self.memset("pool", self.onesb[:], 1.0, ["onesb"])
        for ti in range(NTP):
            tile(ti, True)
        if STOP == 3 or STOP > 30:
            S.barrier(); S.emit(nc); self.st.close(); return
        for ti in range(NTO):
            tile(ti, False)
        S.add("sp", None, reads=[("out", ti) for ti in range(NTO)])
        S.emit(nc)
        self.st.close()

    def _cast_partial(self, cast_unit, src2d, c0, dst4, g, perm, hf):
        srcv = src2d.rearrange("(kc p) n -> p kc n", p=128)
        p2 = [(d0 - hf * 256, s0, n) for (d0, s0, n) in perm]
        cast_unit(srcv[:, 0:4, c0:c0 + 512], dst4[g, :, 0:4, hf * 256:hf * 256 + 256], 4, 512, p2)


def I_adab_row(b, gi, q4):
    return b._adab_row[:, gi * 2048 + q4 * 512: gi * 2048 + (q4 + 1) * 512].partition_broadcast(128)


_CACHE = {}


def _build(PRE, OWN):
    key = (PRE, OWN)
    if key in _CACHE:
        return _CACHE[key]
    nc = bass.Bass("TRN2", target_bir_lowering=False)
    b = Builder(nc, PRE, OWN)
    b._adab_row = b.din("ada_b_row", [1, 6 * D])
    b.build()
    _CACHE[key] = nc
    return nc


def kernel(x, c, positions, ada_w, ada_b, norm_mix_g, w_in, gla_gk_w, gla_gk_b, gla_onorm_g, gla_wo, mla_q_norm_g,
           mla_wuq, mla_kv_norm_g, mla_wukv, mla_wo, w_out, norm_ffn_g, ffn_w_in, ffn_w_down, final_norm_g):
    f32 = np.float32
    x = np.asarray(x, f32)
    B, SEQ, _ = x.shape
    HALF = SEQ // 2
    nc = _build(HALF, HALF)

    def fm(v, n):
        return np.ascontiguousarray(np.asarray(v, f32).reshape(n, 128).T)

    jj = np.arange(128)
    tri = np.where(jj[:, None] <= jj[None, :], -1.0 / 16.0, 0.0).astype(f32)
    uni = np.where(jj[:, None] > jj[None, :], -1.0 / 16.0, 0.0).astype(f32)
    mask01 = (jj[:, None] <= jj[None, :]).astype(f32)
    maskneg = np.where(jj[:, None] > jj[None, :], NEG, 0.0).astype(ml_dtypes.bfloat16)
    inv_freq = (10000.0 ** (-np.arange(0, 64, 2, dtype=f32) / 64.0)).astype(f32)
    invf = np.concatenate([inv_freq, inv_freq]).reshape(64, 1).astype(f32)
    sgn = np.concatenate([-np.ones(32, f32), np.ones(32, f32)]).reshape(64, 1)
    shared = dict(
        ada_w=np.ascontiguousarray(np.asarray(ada_w, f32)[0]),
        ada_b_fm=fm(np.asarray(ada_b)[0], 96), ada_b_row=np.ascontiguousarray(np.asarray(ada_b, f32)[0].reshape(1, -1)),
        nmg_fm=fm(np.asarray(norm_mix_g)[0], 16), nfg_fm=fm(np.asarray(norm_ffn_g)[0], 16),
        w_in=np.ascontiguousarray(np.asarray(w_in, f32)[0]),
        gkw=np.ascontiguousarray(np.concatenate([np.asarray(gla_gk_w, f32)[0], np.asarray(gla_gk_b, f32)[0][None, :]], 0)),
        onorm=np.ascontiguousarray(np.asarray(gla_onorm_g, f32)[0].reshape(1, 512)),
        gla_wo=np.ascontiguousarray(np.asarray(gla_wo, f32)[0]),
        qng_fm=fm(np.asarray(mla_q_norm_g)[0], 4), kvng_fm=fm(np.asarray(mla_kv_norm_g)[0], 4),
        wuq=np.ascontiguousarray(np.asarray(mla_wuq, f32)[0]), wukv=np.ascontiguousarray(np.asarray(mla_wukv, f32)[0]),
        mla_wo=np.ascontiguousarray(np.asarray(mla_wo, f32)[0]), w_out=np.ascontiguousarray(np.asarray(w_out, f32)[0]),
        ffn_in=np.ascontiguousarray(np.asarray(ffn_w_in, f32)[0]), ffn_dn=np.ascontiguousarray(np.asarray(ffn_w_down, f32)[0]),
        fing=np.ascontiguousarray(np.asarray(final_norm_g, f32).reshape(1, D)),
        ident=np.eye(128, dtype=f32), identb=np.eye(128).astype(ml_dtypes.bfloat16), tri=tri, uni=uni, mask01=mask01,
        maskneg=maskneg, invf=invf, sgn=sgn,
    )
    positions = np.asarray(positions).astype(np.int32)
    c = np.asarray(c, f32)
    in_maps = []
    for core in range(2 * B):
        b, half = core // 2, core % 2
        m = dict(shared)
        m["xp"] = np.ascontiguousarray(x[b, 0:HALF])
        m["xo"] = np.ascontiguousarray(x[b, half * HALF:(half + 1) * HALF])
        m["posp"] = np.ascontiguousarray(positions[b, 0:HALF].reshape(1, HALF))
        m["poso"] = np.ascontiguousarray(positions[b, half * HALF:(half + 1) * HALF].reshape(1, HALF))
        m["c_fm"] = fm(c[b], 16)
        fl = np.zeros((128, 2), f32)
        fl[:, 0] = float(half)
        fl[:, 1] = 0.0 if half == 1 else NEG
        m["flag"] = fl
        in_maps.append(m)
    NCR = int(os.environ.get('KCORES', str(2 * B)))
    res = run_bass_kernel_spmd(nc, in_maps[:NCR], core_ids=list(range(NCR)))
    out = np.zeros((B, SEQ, D), f32)
    for core in range(NCR):
        b, half = core // 2, core % 2
        out[b, half * HALF:(half + 1) * HALF] = res.results[core]["out"]
    return out
```

```python
import math
import os
STOP = int(os.environ.get('KSTOP', '0'))
KSUB = int(os.environ.get('KSUB', '0'))
SQ = os.environ.get('KQ', 'pool')
NSLOT = int(os.environ.get('KNSLOT', '4'))
import contextlib
import numpy as np
import ml_dtypes
import concourse.bass as bass
import concourse.mybir as mybir
from concourse.bass_utils import run_bass_kernel_spmd

F32 = mybir.dt.float32
BF16 = mybir.dt.bfloat16
I32 = mybir.dt.int32
AF = mybir.ActivationFunctionType
ALU = mybir.AluOpType

D = 2048
NKC = 16
FFH = 5632
NEG = -30000.0


class _Op:
    __slots__ = ("eng", "fn", "reads", "writes", "dma", "idx", "waits", "signaled", "dsem", "dval", "rank", "barrier")


class Sched:
    ENGS = ("pe", "act", "dve", "pool", "sp")

    def __init__(self, n_dma_sems=14):
        self.ops = []
        self.n_dma_sems = n_dma_sems

    def add(self, eng, fn, reads=(), writes=(), dma=False, barrier=False):
        op = _Op()
        op.eng = eng
        op.fn = fn
        writes = list(writes) + [r for r in reads if r == "psb" or (isinstance(r, tuple) and r[0] == "ps")]
        op.reads = tuple(reads)
        op.writes = tuple(writes)
        op.dma = dma
        op.signaled = False
        op.barrier = barrier
        self.ops.append(op)
        return op

    def barrier(self):
        for e in self.ENGS:
            self.add(e, None, barrier=True)

    def analyze(self):
        last_w = {}
        readers = {}
        cnt = {e: 0 for e in self.ENGS}
        clock = {e: {} for e in self.ENGS}
        opclock = {}
        dma_rr = {e: 0 for e in self.ENGS}
        dma_val = {}
        dma_last = {}
        last_on = {}
        ops = self.ops
        for i, op in enumerate(ops):
            e = op.eng
            cnt[e] += 1
            op.idx = cnt[e]
            deps = set()
            if op.barrier:
                deps.update(last_on.values())
                deps.update(dma_last.values())
            for r in op.reads:
                w = last_w.get(r)
                if w is not None:
                    deps.add(w)
            for w_ in op.writes:
                w = last_w.get(w_)
                if w is not None:
                    deps.add(w)
                for rd in readers.get(w_, ()):
                    deps.add(rd)
            if op.dma:
                k = dma_rr[e]
                dma_rr[e] = (k + 1) % self.n_dma_sems
                skey = ("dma", e, k)
                op.dsem = skey
                op.dval = dma_val.get(skey, 0) + 16
                dma_val[skey] = op.dval
                prev = dma_last.get(skey)
                if prev is not None:
                    deps.add(prev)
                dma_last[skey] = i
            ck = clock[e]
            waits = []
            for d in sorted(deps):
                dop = ops[d]
                if dop.fn is None:
                    continue
                if dop.dma:
                    key, val = dop.dsem, dop.dval
                else:
                    key, val = dop.eng, dop.idx
                    if key == e and e in ("pe", "sp"):
                        continue
                if ck.get(key, 0) >= val:
                    continue
                waits.append((key, val, d))
                dop.signaled = True
                ck[key] = val
                oc = opclock.get(d)
                if oc is not None:
                    for k2, v2 in oc.items():
                        if ck.get(k2, 0) < v2:
                            ck[k2] = v2
            op.waits = waits
            oc = dict(ck)
            if not op.dma:
                oc[e] = op.idx
                if op.fn is not None:
                    last_on[e] = i
            opclock[i] = oc
            for r in op.reads:
                readers.setdefault(r, []).append(i)
            for w_ in op.writes:
                last_w[w_] = i
                readers[w_] = []
            if len(opclock) > 60000:
                for j in list(opclock.keys())[:30000]:
                    del opclock[j]
        rank = {e: 0 for e in self.ENGS}
        for op in ops:
            if not op.dma and op.signaled:
                rank[op.eng] += 1
                op.rank = rank[op.eng]
        self.stats = dict(cnt=cnt, rank=rank, dma=dict(dma_val))

    def emit(self, nc):
        self.analyze()
        ops = self.ops
        with contextlib.ExitStack() as st:
            sems = {}
            for e in self.ENGS:
                sems[e] = st.enter_context(nc.semaphore("s_" + e))
            for key in self.stats["dma"]:
                sems[key] = st.enter_context(nc.semaphore("d_%s_%d" % (key[1], key[2])))
            block = st.enter_context(nc.Block())
            per = {e: [op for op in ops if op.eng == e] for e in self.ENGS}

            def make(ename):
                def body(eng):
                    for op in per[ename]:
                        for (key, val, d) in op.waits:
                            dop = ops[d]
                            eng.wait_ge(sems[key], val if dop.dma else dop.rank)
                        if op.fn is None:
                            continue
                        ins = op.fn(eng)
                        if op.dma:
                            ins.then_inc(sems[op.dsem], 16)
                        elif op.signaled:
                            ins.then_inc(sems[ename], 1)
                return body

            block.tensor(make("pe"))
            block.scalar(make("act"))
            block.vector(make("dve"))
            block.gpsimd(make("pool"))
            block.sync(make("sp"))


WB = {
    "qk": (4, 16, 512), "v": (4, 16, 512), "g": (4, 16, 512), "gk": (1, 16, 16), "cq": (1, 16, 512),
    "ckv": (1, 16, 512), "kr": (1, 16, 128), "ga": (4, 16, 512), "gb": (4, 16, 512),
    "uq": (8, 4, 512), "uk": (4, 4, 512), "uv": (4, 4, 512),
    "mo": (4, 16, 512), "go": (4, 16, 512), "out": (4, 16, 512), "fin": (22, 16, 512), "dn": (4, 44, 512),
}


class Builder:
    def __init__(self, nc, PRE, OWN):
        self.nc = nc
        self.PRE = PRE
        self.OWN = OWN
        self.NK = PRE + OWN
        self.S = Sched()
        self.st = contextlib.ExitStack()
        self.bank_i = 0
        self.eng_i = 0
        self.wslot_i = 0

    def din(self, name, shape, dt=F32):
        return self.nc.dram_tensor(name, list(shape), dt, kind="ExternalInput").ap()

    def sb(self, name, shape, dt):
        return self.st.enter_context(self.nc.sbuf_tensor("sb_" + name, list(shape), dt))

    def mm(self, out, lhsT, rhs, start, stop, R, W):
        self.S.add("pe", lambda e: e.matmul(out, lhsT=lhsT, rhs=rhs, start=start, stop=stop), R, W)

    def tr(self, out, in_, ident, R, W):
        self.S.add("pe", lambda e: e.transpose(out=out, in_=in_, identity=ident), R, W)

    def act(self, out, in_, func, R, W, scale=1.0, bias=None, accum=None):
        kw = {}
        if bias is not None:
            kw["bias"] = bias
        if accum is not None:
            kw["accum_out"] = accum
        self.S.add("act", lambda e: e.activation(out=out, in_=in_, func=func, scale=scale, **kw), R, W)

    def tt(self, eng, out, a, b, op, R, W):
        self.S.add(eng, lambda e: e.tensor_tensor(out=out, in0=a, in1=b, op=op), R, W)

    def ts(self, eng, out, a, s1, s2, op0, op1, R, W):
        if s2 is None:
            self.S.add(eng, lambda e: e.tensor_scalar(out=out, in0=a, scalar1=s1, scalar2=None, op0=op0), R, W)
        else:
            self.S.add(eng, lambda e: e.tensor_scalar(out=out, in0=a, scalar1=s1, scalar2=s2, op0=op0, op1=op1), R, W)

    def stt(self, eng, out, in0, scalar, in1, op0, op1, R, W):
        self.S.add(eng, lambda e: e.scalar_tensor_tensor(out=out, in0=in0, scalar=scalar, in1=in1, op0=op0, op1=op1), R, W)

    def cp(self, eng, out, in_, R, W):
        if eng == "act":
            self.S.add("act", lambda e: e.activation(out=out, in_=in_, func=AF.Identity), R, W)
        else:
            self.S.add(eng, lambda e: e.tensor_copy(out=out, in_=in_), R, W)

    def recip(self, out, in_, R, W):
        self.S.add("dve", lambda e: e.reciprocal(out=out, in_=in_), R, W)

    def memset(self, eng, ap, val, W):
        self.S.add(eng, lambda e: e.memset(ap, val), (), W)

    def dma(self, q, out, in_, R, W):
        self.S.add(q, lambda e: e.dma_start(out=out, in_=in_), R, W, dma=True)

    def bank(self):
        i = self.bank_i
        self.bank_i = (i + 1) % 7
        return self.ps[i], ("ps", i)

    def rot(self, engs=("act", "dve")):
        self.eng_i += 1
        return engs[self.eng_i % len(engs)]

    def wload(self, blk, g, kc0, nkc):
        i = self.wslot_i
        self.wslot_i = (i + 1) % self.NSLOT
        ng, kcw, ncols = WB[blk]
        dst = self.wring[:, i, 0:nkc, 0:ncols]
        src = self.wscr[blk][g, :, kc0:kc0 + nkc, :]
        self.dma("sp", dst, src, [], [("w", i)])
        return self.wring[:, i], ("w", i)

    def wload_multi(self, blk, g0, ng):
        i = self.wslot_i
        self.wslot_i = (i + 1) % self.NSLOT
        for j in range(ng):
            self.dma("sp", self.wring[:, i, j * 4:(j + 1) * 4, :], self.wscr[blk][g0 + j], [], [("w", i)])
        return self.wring[:, i], ("w", i)

    def proj_tm(self, blk, g, actT, actkeys, ncols=512, col0=0, kcw=16):
        ps, pk = self.bank()
        for half in range(0, kcw, 8):
            n = min(8, kcw - half)
            w, wk = self.wload(blk, g, half, n)
            for j in range(n):
                kc = half + j
                self.mm(ps[:, 0:ncols], actT[:, kc, :], w[:, j, col0:col0 + ncols], kc == 0, kc == kcw - 1,
                        [wk] + actkeys, [pk])
        return ps, pk

    def build(self):
        nc = self.nc
        PRE, OWN, NK = self.PRE, self.OWN, self.NK
        NTP, NTO = PRE // 128, OWN // 128
        self.NSLOT = NSLOT
        I = {}
        I["xp"] = self.din("xp", [PRE, D])
        I["xo"] = self.din("xo", [OWN, D])
        I["posp"] = self.din("posp", [1, PRE], I32)
        I["poso"] = self.din("poso", [1, OWN], I32)
        I["c_fm"] = self.din("c_fm", [128, 16])
        I["flag"] = self.din("flag", [128, 2])
        I["ada_w"] = self.din("ada_w", [D, 6 * D])
        I["ada_b_fm"] = self.din("ada_b_fm", [128, 96])
        I["nmg_fm"] = self.din("nmg_fm", [128, 16])
        I["nfg_fm"] = self.din("nfg_fm", [128, 16])
        I["w_in"] = self.din("w_in", [D, 11344])
        I["gkw"] = self.din("gkw", [17, 1024])
        I["onorm"] = self.din("onorm", [1, 512])
        I["gla_wo"] = self.din("gla_wo", [D, D])
        I["qng_fm"] = self.din("qng_fm", [128, 4])
        I["kvng_fm"] = self.din("kvng_fm", [128, 4])
        I["wuq"] = self.din("wuq", [512, 3072])
        I["wukv"] = self.din("wukv", [512, 4096])
        I["mla_wo"] = self.din("mla_wo", [D, D])
        I["w_out"] = self.din("w_out", [D, D])
        I["ffn_in"] = self.din("ffn_in", [D, 2 * FFH])
        I["ffn_dn"] = self.din("ffn_dn", [FFH, D])
        I["fing"] = self.din("fing", [1, D])
        I["ident"] = self.din("ident", [128, 128])
        I["identb"] = self.din("identb", [128, 128], BF16)
        I["tri"] = self.din("tri", [128, 128])
        I["uni"] = self.din("uni", [128, 128])
        I["mask01"] = self.din("mask01", [128, 128])
        I["maskneg"] = self.din("maskneg", [128, 128], BF16)
        I["invf"] = self.din("invf", [64, 1])
        I["sgn"] = self.din("sgn", [64, 1])
        out = nc.dram_tensor("out", [OWN, D], F32, kind="ExternalOutput").ap()
        self.wscr = {}
        for name, (ng, kcw, ncols) in WB.items():
            self.wscr[name] = nc.dram_tensor("ws_" + name, [ng, 128, kcw, ncols], BF16).ap()
        KTc = nc.dram_tensor("KTc", [16, 128, NK], BF16).ap()
        KRc = nc.dram_tensor("KRc", [64, NK], BF16).ap()
        Vc = nc.dram_tensor("Vc", [NK, 2048], BF16).ap()

        sb = self.sb
        self.ps = [self.st.enter_context(nc.psum_tensor("ps%d" % i, [128, 512], F32)) for i in range(7)]
        psb = self.st.enter_context(nc.psum_tensor("psb", [128, 1024], BF16))
        xstg = sb("xstg", [128, D], F32)
        xt = sb("xt", [128, D], F32)
        hT = sb("hT", [128, 16, 128], BF16)
        self.wring = sb("wring", [128, self.NSLOT, 8, 512], BF16)
        S32 = sb("S32", [128, 4, 2, 512], F32)
        Sb = sb("Sb", [128, 4, 2, 512], BF16)
        Fb = sb("Fb", [128, 8, 512], F32)
        Fbf = Fb[:].bitcast(BF16)
        Vb = sb("Vb", [128, 2, 16, 132], BF16)
        pb = sb("pb", [128, 3, 512], BF16)
        big = sb("big", [128, 48, 128], BF16)
        otm = sb("otm", [128, D], BF16)
        mtm = sb("mtm", [128, D], BF16)
        tmp = [sb("tmp%d" % i, [128, 512], F32) for i in range(4)]
        gate_m = sb("gate_m", [128, D], F32)
        gate_f = sb("gate_f", [128, D], F32)
        fing = sb("fing", [128, D], F32)
        onorm = sb("onormb", [128, 512], F32)
        ident = sb("ident", [128, 128], F32)
        identb = sb("identb", [128, 128], BF16)
        tri = sb("tri", [128, 128], F32)
        uni = sb("uni", [128, 128], F32)
        mask01 = sb("mask01", [128, 128], F32)
        maskneg = sb("maskneg", [128, 128], BF16)
        cst = sb("cst", [128, 8], F32)
        flag = sb("flag", [128, 2], F32)
        mods = sb("mods", [128, 96], F32)
        g1 = sb("g1", [128, 2, 16], F32)
        cfm = sb("cfm", [128, 16], F32)
        cact = sb("cact", [128, 16], F32)
        nmg = sb("nmg", [128, 2, 16], F32)
        adab = sb("adab", [128, 96], F32)
        qng = sb("qng", [128, 4], F32)
        kvng = sb("kvng", [128, 4], F32)
        invf = sb("invf", [64, 1], F32)
        sgn = sb("sgn", [64, 1], F32)
        st1 = sb("st1", [128, 8], F32)
        pgk = sb("pgk", [32, 128], F32)
        gkw = sb("gkw", [17, 1024], F32)
        eL = sb("eL", [128, 2, 256], F32)
        ebuf = sb("ebuf", [128, 3, 256], F32)
        qp = sb("qp", [128, 2, 128], BF16)
        kp = sb("kp", [128, 2, 128], BF16)
        kd = sb("kd", [128, 256], BF16)
        AT = sb("AT", [128, 128], BF16)
        vtm = sb("vtm", [128, 512], BF16)
        on = sb("on", [128, 512], BF16)
        cT = sb("cT", [128, 512], F32)
        sq = sb("sq", [128, 512], BF16)
        rq = sb("rq", [128, 128], F32)
        cqn = sb("cqn", [128, 4, 128], BF16)
        ckvn = sb("ckvn", [128, 4, 128], BF16)
        posi = sb("posi", [64, 128], I32)
        pf = sb("pf", [64, 4, 128], F32)
        cs = sb("cs", [64, 2, 128], F32)
        QN = sb("QN", [128, 16, 128], BF16)
        QR = sb("QR", [128, 16, 128], BF16)
        krT = sb("krT", [64, 128], BF16)
        KTt = sb("KTt", [128, 16, 128], BF16)
        Vt = sb("Vt", [128, D], BF16)
        rec = sb("rec", [128, 2], F32)

        S = self.S
        EPS = cst[:, 0:1]
        if os.environ.get('KDEBUG'):
            print('SBUF bytes remaining', nc.sbuf_bytes_remaining)
        for nm, t in (("ident", ident), ("identb", identb), ("tri", tri), ("uni", uni), ("mask01", mask01),
                      ("maskneg", maskneg), ("flag", flag), ("qng_fm", qng), ("kvng_fm", kvng), ("invf", invf),
                      ("sgn", sgn), ("c_fm", cfm), ("ada_b_fm", adab), ("gkw", gkw)):
            self.dma("sp", t[:], I[nm], [], [nm])
        self.dma("sp", nmg[:, 0, :], I["nmg_fm"], [], ["nmg0"])
        self.dma("sp", nmg[:, 1, :], I["nfg_fm"], [], ["nmg1"])
        self.dma("sp", onorm[:], I["onorm"].partition_broadcast(128), [], ["onorm"])
        self.dma("sp", fing[:], I["fing"].partition_broadcast(128), [], ["fing"])
        self.memset("pool", cst[:, 0:1], 1e-6, ["cst"])
        self.memset("pool", cst[:, 1:2], math.log(1.0 / 16.0), ["cst"])
        self.memset("pool", cst[:, 2:3], 0.0, ["cst"])
        self.memset("pool", pgk[:], 1.0, ["pgk"])
        self.memset("pool", Vb[:], 1.0, [("Vb", 0), ("Vb", 1)])
        self.act(cact[:], cfm[:], AF.Exp, ["c_fm"], ["cact"], scale=-1.0)
        self.ts("dve", cact[:], cact[:], 1.0, None, ALU.add, None, ["cact"], ["cact"])
        self.recip(cact[:], cact[:], ["cact"], ["cact"])
        self.tt("dve", cact[:], cact[:], cfm[:], ALU.mult, ["cact", "c_fm"], ["cact"])
        stgA = S32[:].rearrange("p a b c -> p (a b) c")
        stgB = Fb[:]
        SKEYS = [("S", h, k) for h in range(4) for k in range(2)]
        pm, pmk = self.ps[6], ("ps", 6)
        adaw = I["ada_w"].rearrange("(kc p) n -> p kc n", p=128)
        def crep_k(kc):
            return tmp[kc // 4][:, (kc % 4) * 128:(kc % 4 + 1) * 128], ("tmp", kc // 4)
        for q4 in range(4):
            self.memset("pool", tmp[q4][:], 1.0, [("tmp", q4)])
        for kc in range(16):
            ck_, ckk_ = crep_k(kc)
            self.ts("dve", ck_, ck_, cact[:, kc:kc + 1], None, ALU.mult, None, ["cact", ckk_], [ckk_])
        for g in range(24):
            self.dma("sp", stgA, adaw[:, 0:8, g * 512:(g + 1) * 512], [], SKEYS)
            self.dma("sp", stgB, adaw[:, 8:16, g * 512:(g + 1) * 512], [], ["Fb"])
            for cc in range(4):
                j = g * 4 + cc
                for kc in range(16):
                    stg, skeys = (stgA, SKEYS) if kc < 8 else (stgB, ["Fb"])
                    self.mm(pm[:, j:j + 1], stg[:, kc % 8, cc * 128:(cc + 1) * 128], cact[:, kc:kc + 1],
                            kc == 0, kc == 15, skeys + ["cact"], [pmk])
            if g // 4 in (2, 5):
                gbank, gbk = self.ps[g % 2], ("ps", g % 2)
                for kc in range(16):
                    stg, skeys = (stgA, SKEYS) if kc < 8 else (stgB, ["Fb"])
                    ck_, ckk_ = crep_k(kc)
                    self.mm(gbank[:, :], ck_, stg[:, kc % 8, :], kc == 0, kc == 15, skeys + [ckk_], [gbk])
                dst = gate_m if g // 4 == 2 else gate_f
                c0 = (g % 4) * 512
                self.cp("act", dst[:, c0:c0 + 512], gbank[:, :], [gbk], [("gate", g // 4, g % 4)])
        self.tt("dve", mods[:], pm[:, 0:96], adab[:], ALU.add, [pmk, "ada_b_fm"], ["mods"])
        for gi, dst in ((2, gate_m), (5, gate_f)):
            for q4 in range(4):
                self.dma("sp", tmp[q4][:], I_adab_row(self, gi, q4), [], [("tmp", q4)])
                self.tt("dve", dst[:, q4 * 512:(q4 + 1) * 512], dst[:, q4 * 512:(q4 + 1) * 512], tmp[q4][:], ALU.add,
                        [("gate", gi, q4), ("tmp", q4)], [("gate", gi, q4)])
        for i, off in ((0, 16), (1, 64)):
            self.ts("dve", g1[:, i, :], mods[:, off:off + 16], 1.0, None, ALU.add, None, ["mods"], [("g1", i)])
            self.tt("dve", g1[:, i, :], g1[:, i, :], nmg[:, i, :], ALU.mult, [("g1", i), "nmg%d" % i], [("g1", i)])

        if STOP == 1:
            S.barrier(); S.emit(nc); self.st.close(); return
        self.cast_i = 0

        def cast_unit(src, dst, nkc, ncols, perm=None):
            i = self.cast_i
            self.cast_i += 1
            stg, skeys = (stgA, SKEYS) if i % 2 == 0 else (stgB, ["Fb"])
            slot = i % self.NSLOT
            cb = self.wring[:, slot]
            self.dma("sp", stg[:, 0:nkc, 0:ncols], src, [], skeys)
            eng = ("dve", "act")[i % 2]
            if perm is None:
                self.cp(eng, cb[:, 0:nkc, 0:ncols], stg[:, 0:nkc, 0:ncols], skeys, [("w", slot)])
                self.dma("sp", dst, cb[:, 0:nkc, 0:ncols], [("w", slot)], ["wscr"])
            else:
                dcols = 0
                for (d0, s0, n) in perm:
                    self.cp(eng, cb[:, 0:nkc, d0:d0 + n], stg[:, 0:nkc, s0:s0 + n], skeys, [("w", slot)])
                    dcols = max(dcols, d0 + n)
                self.dma("sp", dst, cb[:, 0:nkc, 0:dcols], [("w", slot)], ["wscr"])

        def cast_block(src2d, K, c0, ncols, dst4, g, dcol0=0, perm=None, dcols=None):
            srcv = src2d.rearrange("(kc p) n -> p kc n", p=128)
            kcw = K // 128
            for k0 in range(0, kcw, 8):
                n = min(8, kcw - k0)
                dc = ncols if dcols is None else dcols
                cast_unit(srcv[:, k0:k0 + n, c0:c0 + ncols], dst4[g, :, k0:k0 + n, dcol0:dcol0 + dc], n, ncols, perm)

        W = self.wscr
        win = I["w_in"]
        for h in range(4):
            cast_block(win, D, h * 256, 256, W["qk"], h, 0)
            cast_block(win, D, 1024 + h * 256, 256, W["qk"], h, 256)
            cast_block(win, D, 2048 + h * 512, 512, W["v"], h)
            cast_block(win, D, 4096 + h * 512, 512, W["g"], h)
            cast_block(win, D, 7248 + h * 512, 512, W["ga"], h)
            cast_block(win, D, 9296 + h * 512, 512, W["gb"], h)
            cast_block(I["mla_wo"], D, h * 512, 512, W["mo"], h)
            cast_block(I["gla_wo"], D, h * 512, 512, W["go"], h)
            cast_block(I["w_out"], D, h * 512, 512, W["out"], h)
            cast_block(I["ffn_dn"], FFH, h * 512, 512, W["dn"], h)
        cast_block(win, D, 6144, 16, W["gk"], 0)
        cast_block(win, D, 6160, 512, W["cq"], 0)
        cast_block(win, D, 6672, 512, W["ckv"], 0)
        cast_block(win, D, 7184, 64, W["kr"], 0, perm=[(0, 0, 64), (64, 32, 32), (96, 0, 32)], dcols=128)
        for g in range(22):
            cast_block(I["ffn_in"], D, g * 512, 512, W["fin"], g)
        for g in range(8):
            perm = []
            for hh in range(2):
                perm += [(hh * 256, hh * 192, 192), (hh * 256 + 192, hh * 192 + 160, 32), (hh * 256 + 224, hh * 192 + 128, 32)]
            cast_block(I["wuq"], 512, g * 384, 384, W["uq"], g, perm=perm, dcols=512)
        for g in range(4):
            permk = [(j * 128, j * 256, 128) for j in range(4)]
            permv = [(j * 128, j * 256 + 128, 128) for j in range(4)]
            for hf in range(2):
                pk_ = [(hf * 256 + j * 128, j * 256, 128) for j in range(2)]
                pv_ = [(hf * 256 + j * 128, j * 256 + 128, 128) for j in range(2)]
                self._cast_partial(cast_unit, I["wukv"], g * 1024 + hf * 512, W["uk"], g, pk_, hf)
                self._cast_partial(cast_unit, I["wukv"], g * 1024 + hf * 512, W["uv"], g, pv_, hf)
        S.barrier()
        if STOP == 2:
            S.emit(nc); self.st.close(); return
        self.memset("dve", S32[:].rearrange("p a b c -> p (a b c)"), 0.0, [("S", h, k) for h in range(4) for k in range(2)])
        self.memset("dve", Sb[:].rearrange("p a b c -> p (a b c)"), 0.0, [("Sb", h, k) for h in range(4) for k in range(2)])

        self.memset("dve", QR[:].rearrange("p a b -> p (a b)"), 0.0, [("QR", h) for h in range(16)])
        self.memset("dve", Fb[:, 4:8, :].rearrange("p a b -> p (a b)"), 0.0, [("KRb", 0), ("KRb", 1)])
        XT = ["xt"]
        HT = [("hT", kc) for kc in range(16)]

        def rstd_from_ss(ss_ap, inv_n, out_ap, R, Wk, extra_bias=None):
            self.act(out_ap, ss_ap, AF.Ln, R + ["cst"], Wk, scale=inv_n, bias=EPS)
            self.act(out_ap, out_ap, AF.Exp, Wk + ["cst"], Wk, scale=-0.5, bias=(extra_bias if extra_bias is not None else cst[:, 2:3]))

        def norm_to_hT(which):
            self.act(xstg[:], xt[:], AF.Square, XT, ["xstg", "ss"], accum=st1[:, 0:1])
            rstd_from_ss(st1[:, 0:1], 1.0 / D, st1[:, 1:2], ["ss"], ["rstd"])
            self.ts("dve", xstg[:], xt[:], st1[:, 1:2], None, ALU.mult, None, XT + ["rstd", "xstg"], ["xstg"])
            sh0 = 0 if which == 0 else 48
            for q4 in range(4):
                ps, pk = self.bank()
                for j in range(4):
                    kc = q4 * 4 + j
                    self.tr(ps[:, j * 128:(j + 1) * 128], xstg[:, kc * 128:(kc + 1) * 128], ident[:], ["xstg", "ident"], [pk])
                for j in range(4):
                    kc = q4 * 4 + j
                    if True:
                        self.act(hT[:, kc, :], ps[:, j * 128:(j + 1) * 128], AF.Identity, [pk, ("g1", which), "mods"], [("hT", kc)],
                                 scale=g1[:, which, kc:kc + 1], bias=mods[:, sh0 + kc:sh0 + kc + 1])
                    else:
                        self.ts("dve", hT[:, kc, :], ps[:, j * 128:(j + 1) * 128], g1[:, which, kc:kc + 1],
                                mods[:, sh0 + kc:sh0 + kc + 1], ALU.mult, ALU.add, [pk, ("g1", which), "mods"], [("hT", kc)])

        def fm_proj(ps_out, pk, w, wk, kcs, col0, M, rhsT, rkeys, start_kc=0, total=None):
            n = len(kcs)
            for j, kc in enumerate(kcs):
                self.mm(ps_out, w[:, j, col0:col0 + M], rhsT[:, kc, :], kc == 0, kc == (total or 16) - 1, [wk] + rkeys, [pk])

        def tile(ti, prefix):
            key0 = ti * 128 if prefix else PRE + ti * 128
            xsrc = (I["xp"] if prefix else I["xo"])[ti * 128:(ti + 1) * 128, :]
            psrc = (I["posp"] if prefix else I["poso"])[:, ti * 128:(ti + 1) * 128]
            self.dma(SQ, xt[:], xsrc, [], XT)
            self.dma(SQ, posi[:], psrc.partition_broadcast(64), [], ["posi"])
            norm_to_hT(0)
            if STOP == 31:
                return
            C1 = 6.28125
            C2 = 2 * math.pi - C1
            a, kf, t2, r2 = pf[:, 0, :], pf[:, 1, :], pf[:, 2, :], pf[:, 3, :]
            self.cp("dve", kf, posi[:], ["posi"], ["pf"])
            self.ts("dve", a, kf, invf[:, 0:1], None, ALU.mult, None, ["pf", "invf"], ["pf"])
            self.ts("dve", kf, a, 1.0 / (2 * math.pi), None, ALU.mult, None, ["pf"], ["pf"])
            self.cp("dve", posi[:], kf, ["pf"], ["posi"])
            self.cp("dve", kf, posi[:], ["posi"], ["pf"])
            self.stt("dve", a, kf, -C1, a, ALU.mult, ALU.add, ["pf"], ["pf"])
            self.stt("dve", a, kf, -C2, a, ALU.mult, ALU.add, ["pf"], ["pf"])
            self.ts("dve", t2, a, math.pi, 2 * math.pi, ALU.is_gt, ALU.mult, ["pf"], ["pf"])
            self.tt("dve", a, a, t2, ALU.subtract, ["pf"], ["pf"])
            self.ts("dve", r2, a, math.pi / 2, None, ALU.add, None, ["pf"], ["pf"])
            self.ts("dve", t2, r2, math.pi, 2 * math.pi, ALU.is_gt, ALU.mult, ["pf"], ["pf"])
            self.tt("dve", r2, r2, t2, ALU.subtract, ["pf"], ["pf"])
            self.act(cs[:, 0, :], r2, AF.Sin, ["pf"], ["cs"])
            self.act(cs[:, 1, :], a, AF.Sin, ["pf", "cs"], ["cs"])
            self.ts("dve", cs[:, 1, :], cs[:, 1, :], sgn[:, 0:1], None, ALU.mult, None, ["cs", "sgn"], ["cs"])

            if STOP == 32:
                return
            w, wk = self.wload("gk", 0, 0, 8)
            w2, wk2 = self.wload("gk", 0, 8, 8)
            ps, pk = self.bank()
            for kc in range(16):
                ww, wwk = (w, wk) if kc < 8 else (w2, wk2)
                self.mm(ps[0:16, 0:128], ww[:, kc % 8, 0:16], hT[:, kc, :], kc == 0, kc == 15, [wwk, ("hT", kc)], [pk])
            self.cp("dve", pgk[0:16, :], ps[0:16, 0:128], [pk, "pgk"], ["pgk"])
            for h in range(4):
                ps, pk = self.bank()
                self.mm(ps[:, 0:256], pgk[0:17, :], gkw[0:17, h * 256:(h + 1) * 256], True, True, ["pgk", "gkw"], [pk])
                self.act(eL[:, 0, :], ps[:, 0:256], AF.Exp, [pk], ["e"], scale=-1.0)
                self.act(eL[:, 1, :], eL[:, 0, :], AF.Ln, ["e"], ["L"], bias=1.0)
                Lh = eL[:, 1, :]
                psb_, pkb = self.bank()
                for dkc in range(2):
                    self.mm(psb_[:, dkc * 128:(dkc + 1) * 128], Lh[:, dkc * 128:(dkc + 1) * 128], tri[:], True, True, ["L", "tri"], [pkb])
                self.mm(psb_[:, 256:512], uni[:], Lh, True, True, ["L", "uni"], [pkb])
                self.act(ebuf[:, 0, :], psb_[:, 0:256], AF.Exp, [pkb], ["eb"])
                if not prefix:
                    self.act(ebuf[:, 1, :], psb_[:, 0:256], AF.Exp, [pkb], ["enb"], scale=-1.0)
                self.act(ebuf[:, 2, :], psb_[:, 256:512], AF.Exp, [pkb], ["er"])
                wA, wAk = self.wload("qk", h, 0, 8)
                wB, wBk = self.wload("qk", h, 8, 8)
                for (c0, dst, dkey, eidx, ekey) in ((0, qp, "qp", 0, "eb"), (256, kp, "kp", 1, "enb")):
                    if prefix:
                        continue
                    pq, pqk = self.bank()
                    for dkc in range(2):
                        for kc in range(16):
                            ww, wwk = (wA, wAk) if kc < 8 else (wB, wBk)
                            self.mm(pq[:, dkc * 128:(dkc + 1) * 128], ww[:, kc % 8, c0 + dkc * 128:c0 + (dkc + 1) * 128],
                                    hT[:, kc, :], kc == 0, kc == 15, [wwk, ("hT", kc)], [pqk])
                    self.tt("dve", dst[:].rearrange("p a b -> p (a b)"), pq[:, 0:256], ebuf[:, eidx, :], ALU.mult, [pqk, ekey], [dkey])
                pq, pqk = self.bank()
                for kc in range(16):
                    ww, wwk = (wA, wAk) if kc < 8 else (wB, wBk)
                    self.mm(pq[:, 0:256], hT[:, kc, :], ww[:, kc % 8, 256:512], kc == 0, kc == 15, [wwk, ("hT", kc)], [pqk])
                self.tt("dve", kd[:], pq[:, 0:256], ebuf[:, 2, :], ALU.mult, [pqk, "er"], ["kd"])
                pv, pvk = self.proj_tm("v", h, hT, HT)
                self.cp("act", vtm[:], pv[:, :], [pvk], ["vtm"])
                if not prefix:
                    pg, pgk_ = self.proj_tm("g", h, hT, HT)
                    ge, gsl = tmp[0], tmp[1]
                    self.act(ge[:], pg[:, :], AF.Exp, [pgk_], [("tmp", 0)], scale=-1.0)
                    self.ts("dve", ge[:], ge[:], 1.0, None, ALU.add, None, [("tmp", 0)], [("tmp", 0)])
                    self.recip(ge[:], ge[:], [("tmp", 0)], [("tmp", 0)])
                    self.tt("dve", gsl[:], pg[:, :], ge[:], ALU.mult, [pgk_, ("tmp", 0)], [("tmp", 1)])
                    self.tt("dve", gsl[:], gsl[:], onorm[:], ALU.mult, [("tmp", 1), "onorm"], [("tmp", 1)])
                    pa, pak = self.bank()
                    for dkc in range(2):
                        self.mm(pa[:, 0:128], kp[:, dkc, :], qp[:, dkc, :], dkc == 0, dkc == 1, ["kp", "qp"], [pak])
                    self.tt("dve", AT[:], pa[:, 0:128], mask01[:], ALU.mult, [pak, "mask01"], ["AT"])
                    po, pok = self.bank()
                    self.mm(po[:, :], AT[:], vtm[:], True, False, ["AT", "vtm"], [pok])
                    for dkc in range(2):
                        self.mm(po[:, :], qp[:, dkc, :], Sb[:, h, dkc, :], False, dkc == 1, ["qp", ("Sb", h, dkc)], [pok])
                    self.act(tmp[2][:], po[:, :], AF.Square, [pok], [("tmp", 2), "ss2"], accum=st1[:, 2:3])
                    rstd_from_ss(st1[:, 2:3], 1.0 / (512.0 * 256.0), st1[:, 3:4], ["ss2"], ["rstd2"], extra_bias=cst[:, 1:2])
                    self.stt("dve", on[:], po[:, :], st1[:, 3:4], gsl[:], ALU.mult, ALU.mult, [pok, "rstd2", ("tmp", 1)], ["on"])
                    for vc in range(4):
                        self.tr(psb[:, vc * 128:(vc + 1) * 128], on[:, vc * 128:(vc + 1) * 128], identb[:], ["on", "identb"], ["psb"])
                    self.cp("act", big[:, h * 4:(h + 1) * 4, :].rearrange("p a b -> p (a b)"), psb[:, 0:512], ["psb"],
                            [("big", h * 4 + j) for j in range(4)])
                for dkc in range(2):
                    pd, pdk = self.bank()
                    self.mm(pd[:, :], kd[:, dkc * 128:(dkc + 1) * 128], vtm[:], True, True, ["kd", "vtm"], [pdk])
                    self.stt("dve", S32[:, h, dkc, :], S32[:, h, dkc, :], ebuf[:, 0, dkc * 128 + 127:dkc * 128 + 128], pd[:, :],
                             ALU.mult, ALU.add, [("S", h, dkc), "eb", pdk], [("S", h, dkc)])
                    if not (prefix and ti < NTP - 1):
                        if prefix:
                            self.ts("dve", S32[:, h, dkc, :], S32[:, h, dkc, :], flag[:, 0:1], None, ALU.mult, None,
                                    [("S", h, dkc), "flag"], [("S", h, dkc)])
                        self.cp("act", Sb[:, h, dkc, :], S32[:, h, dkc, :], [("S", h, dkc)], [("Sb", h, dkc)])

            if STOP == 34:
                return
            def latent(blk, gfm, gkey, dst, dkey):
                pc, pck = self.bank()
                wA, wAk = self.wload(blk, 0, 0, 8)
                wB, wBk = self.wload(blk, 0, 8, 8)
                for c in range(4):
                    for kc in range(16):
                        ww, wwk = (wA, wAk) if kc < 8 else (wB, wBk)
                        self.mm(pc[:, c * 128:(c + 1) * 128], ww[:, kc % 8, c * 128:(c + 1) * 128], hT[:, kc, :],
                                kc == 0, kc == 15, [wwk, ("hT", kc)], [pck])
                self.cp("dve", cT[:], pc[:, :], [pck], ["cT"])
                if KSUB == 1:
                    return
                self.act(sq[:], pc[:, :], AF.Square, [pck], ["sq"])
                if KSUB == 2:
                    return
                p2, p2k = self.bank()
                for c in range(4):
                    self.mm(p2[:, 0:128], identb_ones[:], sq[:, c * 128:(c + 1) * 128], c == 0, c == 3, ["sq", "onesb"], [p2k])
                self.cp("dve", rq[:], p2[:, 0:128], [p2k], ["rq"])
                if KSUB == 3:
                    return
                self.act(rq[:], rq[:], AF.Ln, ["rq", "cst"], ["rq"], scale=1.0 / 512.0, bias=EPS)
                self.act(rq[:], rq[:], AF.Exp, ["rq"], ["rq"], scale=-0.5)
                for c in range(4):
                    self.tt("dve", cT[:, c * 128:(c + 1) * 128], cT[:, c * 128:(c + 1) * 128], rq[:], ALU.mult, ["cT", "rq"], ["cT"])
                    self.ts("dve", dst[:, c, :], cT[:, c * 128:(c + 1) * 128], gfm[:, c:c + 1], None, ALU.mult, None,
                            ["cT", gkey], [(dkey, c)])

            identb_ones = self.onesb
            latent("ckv", kvng, "kvng_fm", ckvn, "ckvn")
            CKV = [("ckvn", c) for c in range(4)]
            if STOP == 35:
                return
            wA, wAk = self.wload("kr", 0, 0, 8)
            wB, wBk = self.wload("kr", 0, 8, 8)
            pr, prk = self.bank()
            for half in range(2):
                for kc in range(16):
                    ww, wwk = (wA, wAk) if kc < 8 else (wB, wBk)
                    self.mm(pr[0:64, half * 128:(half + 1) * 128], ww[:, kc % 8, half * 64:(half + 1) * 64], hT[:, kc, :],
                            kc == 0, kc == 15, [wwk, ("hT", kc)], [prk])
            t1, t2_ = tmp[2][0:64, 0:128], tmp[2][0:64, 128:256]
            self.tt("dve", t1, pr[0:64, 0:128], cs[:, 0, :], ALU.mult, [prk, "cs"], [("tmp", 2)])
            self.tt("dve", t2_, pr[0:64, 128:256], cs[:, 1, :], ALU.mult, [prk, "cs", ("tmp", 2)], [("tmp", 2)])
            self.tt("dve", krT[:], t1, t2_, ALU.add, [("tmp", 2)], ["krT"])
            self.dma(SQ, KRc[:, key0:key0 + 128], krT[:], ["krT"], [("KRc", key0 // 128)])
            if STOP == 36:
                return
            wk_, wkk = self.wload_multi("uk", 0, 2)
            wk2_, wkk2 = self.wload_multi("uk", 2, 2)
            for g in range(4):
                ww, wwk = (wk_, wkk) if g < 2 else (wk2_, wkk2)
                pk_, pkk = self.bank()
                for j in range(4):
                    for kc in range(4):
                        self.mm(pk_[:, j * 128:(j + 1) * 128], ww[:, (g % 2) * 4 + kc, j * 128:(j + 1) * 128], ckvn[:, kc, :],
                                kc == 0, kc == 3, [wwk] + CKV, [pkk])
                self.cp(self.rot(), KTt[:, g * 4:(g + 1) * 4, :].rearrange("p a b -> p (a b)"), pk_[:, :], [pkk], [("KTt", g)])
            for hh_ in range(16):
                self.dma(SQ, KTc[hh_, :, key0:key0 + 128], KTt[:, hh_, :], [("KTt", hh_ // 4)], [("KTc", key0 // 128, hh_)])
            wv_, wvk = self.wload_multi("uv", 0, 2)
            wv2_, wvk2 = self.wload_multi("uv", 2, 2)
            for g in range(4):
                ww, wwk = (wv_, wvk) if g < 2 else (wv2_, wvk2)
                pv_, pvk_ = self.bank()
                for kc in range(4):
                    self.mm(pv_[:, :], ckvn[:, kc, :], ww[:, (g % 2) * 4 + kc, :], kc == 0, kc == 3, [wwk] + CKV, [pvk_])
                self.cp(self.rot(), Vt[:, g * 512:(g + 1) * 512], pv_[:, :], [pvk_], [("Vt", g)])
            self.dma(SQ, Vc[key0:key0 + 128, :], Vt[:], [("Vt", g) for g in range(4)], [("Vc", key0 // 128)])
            if prefix:
                return

            latent("cq", qng, "qng_fm", cqn, "cqn")
            CQ = [("cqn", c) for c in range(4)]
            for s2 in range(4):
                wq_, wqk = self.wload_multi("uq", s2 * 2, 2)
                pn, pnk = self.bank()
                pra, prak = self.bank()
                prb, prbk = self.bank()
                for j in range(4):
                    g_, hh = j // 2, j % 2
                    for kc in range(4):
                        self.mm(pn[:, j * 128:(j + 1) * 128], wq_[:, g_ * 4 + kc, hh * 256:hh * 256 + 128], cqn[:, kc, :],
                                kc == 0, kc == 3, [wqk] + CQ, [pnk])
                    for kc in range(4):
                        self.mm(pra[0:64, j * 128:(j + 1) * 128], wq_[:, g_ * 4 + kc, hh * 256 + 128:hh * 256 + 192], cqn[:, kc, :],
                                kc == 0, kc == 3, [wqk] + CQ, [prak])
                    for kc in range(4):
                        self.mm(prb[0:64, j * 128:(j + 1) * 128], wq_[:, g_ * 4 + kc, hh * 256 + 192:hh * 256 + 256], cqn[:, kc, :],
                                kc == 0, kc == 3, [wqk] + CQ, [prbk])
                self.cp("act", QN[:, s2 * 4:(s2 + 1) * 4, :].rearrange("p a b -> p (a b)"), pn[:, :], [pnk], [("QN", s2 * 4 + j) for j in range(4)])
                for j in range(4):
                    h = s2 * 4 + j
                    ta, tb = tmp[3][0:64, 0:128], tmp[3][0:64, 128:256]
                    self.tt("dve", ta, pra[0:64, j * 128:(j + 1) * 128], cs[:, 0, :], ALU.mult, [prak, "cs", ("tmp", 3)], [("tmp", 3)])
                    self.tt("dve", tb, prb[0:64, j * 128:(j + 1) * 128], cs[:, 1, :], ALU.mult, [prbk, "cs", ("tmp", 3)], [("tmp", 3)])
                    self.tt("dve", QR[0:64, h, :], ta, tb, ALU.add, [("tmp", 3)], [("QR", h)])

            nkb = (key0 + 128) // 128
            npre = PRE // 128
            SC = 192.0 ** -0.5
            KTb = [Fbf[:, par * 2:(par + 1) * 2, :].rearrange("p a k -> p (a k)") for par in range(2)]
            KRb = [Fbf[:, 4 + par * 2:4 + (par + 1) * 2, :].rearrange("p a k -> p (a k)") for par in range(2)]
            items = []
            for h in range(16):
                for sb0 in range(0, nkb, 16):
                    nb = min(16, nkb - sb0)
                    par = self.kvpar
                    self.kvpar ^= 1
                    j = 0
                    while j < nb:
                        kb = sb0 + j
                        lim = min(4, nb - j)
                        if kb < npre:
                            lim = min(lim, npre - kb)
                        items.append(dict(h=h, par=par, j=j, kb=kb, n=lim, load=(sb0, nb) if j == 0 else None))
                        j += lim

            def emit_S(it):
                h, par = it["h"], it["par"]
                if it["load"] is not None:
                    sb0, nb = it["load"]
                    k0 = sb0 * 128
                    ckeys = [(nm, kb) for nm in ("KRc", "Vc") for kb in range(sb0, sb0 + nb)] + [("KTc", kb, h) for kb in range(sb0, sb0 + nb)]
                    self.dma("sp", KTb[par][:, 0:nb * 128], KTc[h, :, k0:k0 + nb * 128], ckeys, [("KTb", par)])
                    self.dma("sp", KRb[par][0:64, 0:nb * 128], KRc[:, k0:k0 + nb * 128], ckeys, [("KRb", par)])
                    self.dma("sp", Vb[:, par, 0:nb, 0:128],
                             Vc[k0:k0 + nb * 128, h * 128:(h + 1) * 128].rearrange("(b p) d -> p b d", p=128), ckeys, [("Vb", par)])
                si = self.st_i
                self.st_i = (si + 1) % 3
                it["si"] = si
                pst, pstk = self.ps[si], ("ps", si)
                n = it["n"]
                for jj in range(n):
                    kb = it["kb"] + jj
                    j = it["j"] + jj
                    diag = kb == nkb - 1
                    c0 = jj * 128
                    self.mm(pst[:, c0:c0 + 128], KTb[par][:, j * 128:(j + 1) * 128], QN[:, h, :], True, False, [("KTb", par), ("QN", h)], [pstk])
                    self.mm(pst[:, c0:c0 + 128], KRb[par][:, j * 128:(j + 1) * 128], QR[:, h, :], False, not diag, [("KRb", par), ("QR", h)], [pstk])
                    if diag:
                        self.mm(pst[:, c0:c0 + 128], identb[:], maskneg[:], False, True, ["identb", "maskneg"], [pstk])
                bias = flag[:, 1:2] if it["kb"] < npre else cst[:, 2:3]
                self.act(pb[:, si, 0:n * 128], pst[:, 0:n * 128], AF.Exp, [pstk, "flag", "cst"], [("pb", si)], scale=SC, bias=bias)

            def emit_PV(it):
                h, par, si = it["h"], it["par"], it["si"]
                po, pok = self.ps[3 + h % 2], ("ps", 3 + h % 2)
                for jj in range(it["n"]):
                    kb = it["kb"] + jj
                    j = it["j"] + jj
                    self.mm(po[:, 0:129], pb[:, si, jj * 128:(jj + 1) * 128], Vb[:, par, j, 0:129], kb == 0, kb == nkb - 1,
                            [("pb", si), ("Vb", par)], [pok])
                if it["kb"] + it["n"] == nkb:
                    self.recip(rec[:, 0:1], po[:, 128:129], [pok], ["rec"])
                    self.ts("dve", otm[:, h * 128:(h + 1) * 128], po[:, 0:128], rec[:, 0:1], None, ALU.mult, None, [pok, "rec"], [("otm", h)])

            emit_S(items[0])
            for i in range(len(items)):
                if i + 1 < len(items):
                    emit_S(items[i + 1])
                emit_PV(items[i])
            for q4 in range(4):
                for j in range(4):
                    c = q4 * 4 + j
                    self.tr(psb[:, j * 128:(j + 1) * 128], otm[:, c * 128:(c + 1) * 128], identb[:], [("otm", c), "identb"], ["psb"])
                self.cp(self.rot(), big[:, 16 + q4 * 4:16 + (q4 + 1) * 4, :].rearrange("p a b -> p (a b)"), psb[:, 0:512], ["psb"],
                        [("big", 16 + q4 * 4 + j) for j in range(4)])

            ON = [("big", c) for c in range(16)]
            OM = [("big", 16 + c) for c in range(16)]
            onT = big[:, 0:16, :]
            omT = big[:, 16:32, :]
            for g in range(4):
                pa_, pak_ = self.proj_tm("ga", g, hT, HT)
                self.act(tmp[0][:], pa_[:, :], AF.Exp, [pak_], [("tmp", 0)], scale=-1.0)
                pb_, pbk_ = self.proj_tm("gb", g, hT, HT)
                self.act(tmp[1][:], pb_[:, :], AF.Exp, [pbk_], [("tmp", 1)], scale=-1.0)
                for i in range(2):
                    self.ts("dve", tmp[i][:], tmp[i][:], 1.0, None, ALU.add, None, [("tmp", i)], [("tmp", i)])
                    self.recip(tmp[i][:], tmp[i][:], [("tmp", i)], [("tmp", i)])
                py, pyk = self.proj_tm("go", g, onT, ON)
                self.tt("dve", tmp[0][:], py[:, :], tmp[0][:], ALU.mult, [pyk, ("tmp", 0)], [("tmp", 0)])
                pz, pzk = self.proj_tm("mo", g, omT, OM)
                self.tt("dve", tmp[1][:], pz[:, :], tmp[1][:], ALU.mult, [pzk, ("tmp", 1)], [("tmp", 1)])
                self.tt("dve", mtm[:, g * 512:(g + 1) * 512], tmp[0][:], tmp[1][:], ALU.add, [("tmp", 0), ("tmp", 1)], [("mtm", g)])
            mT = QN
            for q4 in range(4):
                for j in range(4):
                    c = q4 * 4 + j
                    self.tr(psb[:, j * 128:(j + 1) * 128], mtm[:, c * 128:(c + 1) * 128], identb[:], [("mtm", q4), "identb"], ["psb"])
                self.cp(self.rot(), mT[:, q4 * 4:(q4 + 1) * 4, :].rearrange("p a b -> p (a b)"), psb[:, 0:512], ["psb"],
                        [("QN", q4 * 4 + j) for j in range(4)])
            MT = [("QN", c) for c in range(16)]
            for g in range(4):
                px, pxk = self.proj_tm("out", g, mT, MT)
                self.tt("dve", tmp[2][:], px[:, :], gate_m[:, g * 512:(g + 1) * 512], ALU.mult, [pxk, ("gate", 2, g)], [("tmp", 2)])
                self.tt("dve", xt[:, g * 512:(g + 1) * 512], xt[:, g * 512:(g + 1) * 512], tmp[2][:], ALU.add, XT + [("tmp", 2)], XT)

            norm_to_hT(1)
            for g in range(11):
                pg_, pgk2 = self.proj_tm("fin", g, hT, HT)
                self.act(tmp[0][:], pg_[:, :], AF.Exp, [pgk2], [("tmp", 0)], scale=-1.0)
                pu_, puk = self.proj_tm("fin", 11 + g, hT, HT)
                self.ts("dve", tmp[0][:], tmp[0][:], 1.0, None, ALU.add, None, [("tmp", 0)], [("tmp", 0)])
                self.recip(tmp[0][:], tmp[0][:], [("tmp", 0)], [("tmp", 0)])
                self.tt("dve", tmp[1][:], pu_[:, :], tmp[0][:], ALU.mult, [puk, ("tmp", 0)], [("tmp", 1)])
                self.tt("dve", on[:], pg_[:, :], tmp[1][:], ALU.mult, [pgk2, ("tmp", 1)], ["on"])
                for j in range(4):
                    self.tr(psb[:, j * 128:(j + 1) * 128], on[:, j * 128:(j + 1) * 128], identb[:], ["on", "identb"], ["psb"])
                self.cp(self.rot(), big[:, g * 4:(g + 1) * 4, :].rearrange("p a b -> p (a b)"), psb[:, 0:512], ["psb"],
                        [("big", g * 4 + j) for j in range(4)])
            AK = [("big", c) for c in range(44)]
            for g in range(4):
                pd_, pdk_ = self.proj_tm("dn", g, big, AK, kcw=44)
                self.tt("dve", tmp[2][:], pd_[:, :], gate_f[:, g * 512:(g + 1) * 512], ALU.mult, [pdk_, ("gate", 5, g)], [("tmp", 2)])
                self.tt("dve", xt[:, g * 512:(g + 1) * 512], xt[:, g * 512:(g + 1) * 512], tmp[2][:], ALU.add, XT + [("tmp", 2)], XT)
            self.act(xstg[:], xt[:], AF.Square, XT, ["xstg", "ss"], accum=st1[:, 0:1])
            rstd_from_ss(st1[:, 0:1], 1.0 / D, st1[:, 1:2], ["ss"], ["rstd"])
            self.stt("dve", xstg[:], xt[:], st1[:, 1:2], fing[:], ALU.mult, ALU.mult, XT + ["rstd", "fing", "xstg"], ["xstg"])
            self.dma(SQ, out[ti * 128:(ti + 1) * 128, :], xstg[:], ["xstg"], [("out", ti)])

        self.kvpar = 0
        self.st_i = 0
        self.onesb = sb("onesb", [128, 128], BF16)
        self.memset("pool", self.onesb[:], 1.0, ["onesb"])
        for ti in range(NTP):
            tile(ti, True)
        if STOP == 3 or STOP > 30:
            S.barrier(); S.emit(nc); self.st.close(); return
        for ti in range(NTO):
            tile(ti, False)
        S.add("sp", None, reads=[("out", ti) for ti in range(NTO)])
        S.emit(nc)
        self.st.close()

    def _cast_partial(self, cast_unit, src2d, c0, dst4, g, perm, hf):
        srcv = src2d.rearrange("(kc p) n -> p kc n", p=128)
        p2 = [(d0 - hf * 256, s0, n) for (d0, s0, n) in perm]
        cast_unit(srcv[:, 0:4, c0:c0 + 512], dst4[g, :, 0:4, hf * 256:hf * 256 + 256], 4, 512, p2)


def I_adab_row(b, gi, q4):
    return b._adab_row[:, gi * 2048 + q4 * 512: gi * 2048 + (q4 + 1) * 512].partition_broadcast(128)


_CACHE = {}


def _build(PRE, OWN):
    key = (PRE, OWN)
    if key in _CACHE:
        return _CACHE[key]
    nc = bass.Bass("TRN2", target_bir_lowering=False)
    b = Builder(nc, PRE, OWN)
    b._adab_row = b.din("ada_b_row", [1, 6 * D])
    b.build()
    _CACHE[key] = nc
    return nc


def kernel(x, c, positions, ada_w, ada_b, norm_mix_g, w_in, gla_gk_w, gla_gk_b, gla_onorm_g, gla_wo, mla_q_norm_g,
           mla_wuq, mla_kv_norm_g, mla_wukv, mla_wo, w_out, norm_ffn_g, ffn_w_in, ffn_w_down, final_norm_g):
    f32 = np.float32
    x = np.asarray(x, f32)
    B, SEQ, _ = x.shape
    HALF = SEQ // 2
    nc = _build(HALF, HALF)

    def fm(v, n):
        return np.ascontiguousarray(np.asarray(v, f32).reshape(n, 128).T)

    jj = np.arange(128)
    tri = np.where(jj[:, None] <= jj[None, :], -1.0 / 16.0, 0.0).astype(f32)
    uni = np.where(jj[:, None] > jj[None, :], -1.0 / 16.0, 0.0).astype(f32)
    mask01 = (jj[:, None] <= jj[None, :]).astype(f32)
    maskneg = np.where(jj[:, None] > jj[None, :], NEG, 0.0).astype(ml_dtypes.bfloat16)
    inv_freq = (10000.0 ** (-np.arange(0, 64, 2, dtype=f32) / 64.0)).astype(f32)
    invf = np.concatenate([inv_freq, inv_freq]).reshape(64, 1).astype(f32)
    sgn = np.concatenate([-np.ones(32, f32), np.ones(32, f32)]).reshape(64, 1)
    shared = dict(
        ada_w=np.ascontiguousarray(np.asarray(ada_w, f32)[0]),
        ada_b_fm=fm(np.asarray(ada_b)[0], 96), ada_b_row=np.ascontiguousarray(np.asarray(ada_b, f32)[0].reshape(1, -1)),
        nmg_fm=fm(np.asarray(norm_mix_g)[0], 16), nfg_fm=fm(np.asarray(norm_ffn_g)[0], 16),
        w_in=np.ascontiguousarray(np.asarray(w_in, f32)[0]),
        gkw=np.ascontiguousarray(np.concatenate([np.asarray(gla_gk_w, f32)[0], np.asarray(gla_gk_b, f32)[0][None, :]], 0)),
        onorm=np.ascontiguousarray(np.asarray(gla_onorm_g, f32)[0].reshape(1, 512)),
        gla_wo=np.ascontiguousarray(np.asarray(gla_wo, f32)[0]),
        qng_fm=fm(np.asarray(mla_q_norm_g)[0], 4), kvng_fm=fm(np.asarray(mla_kv_norm_g)[0], 4),
        wuq=np.ascontiguousarray(np.asarray(mla_wuq, f32)[0]), wukv=np.ascontiguousarray(np.asarray(mla_wukv, f32)[0]),
        mla_wo=np.ascontiguousarray(np.asarray(mla_wo, f32)[0]), w_out=np.ascontiguousarray(np.asarray(w_out, f32)[0]),
        ffn_in=np.ascontiguousarray(np.asarray(ffn_w_in, f32)[0]), ffn_dn=np.ascontiguousarray(np.asarray(ffn_w_down, f32)[0]),
        fing=np.ascontiguousarray(np.asarray(final_norm_g, f32).reshape(1, D)),
        ident=np.eye(128, dtype=f32), identb=np.eye(128).astype(ml_dtypes.bfloat16), tri=tri, uni=uni, mask01=mask01,
        maskneg=maskneg, invf=invf, sgn=sgn,
    )
    positions = np.asarray(positions).astype(np.int32)
    c = np.asarray(c, f32)
    in_maps = []
    for core in range(2 * B):
        b, half = core // 2, core % 2
        m = dict(shared)
        m["xp"] = np.ascontiguousarray(x[b, 0:HALF])
        m["xo"] = np.ascontiguousarray(x[b, half * HALF:(half + 1) * HALF])
        m["posp"] = np.ascontiguousarray(positions[b, 0:HALF].reshape(1, HALF))
        m["poso"] = np.ascontiguousarray(positions[b, half * HALF:(half + 1) * HALF].reshape(1, HALF))
        m["c_fm"] = fm(c[b], 16)
        fl = np.zeros((128, 2), f32)
        fl[:, 0] = float(half)
        fl[:, 1] = 0.0 if half == 1 else NEG
        m["flag"] = fl
        in_maps.append(m)
    NCR = int(os.environ.get('KCORES', str(2 * B)))
    res = run_bass_kernel_spmd(nc, in_maps[:NCR], core_ids=list(range(NCR)))
    out = np.zeros((B, SEQ, D), f32)
    for core in range(NCR):
        b, half = core // 2, core % 2
        out[b, half * HALF:(half + 1) * HALF] = res.results[core]["out"]
    return out
```

```python
import math
import os
STOP = int(os.environ.get('KSTOP', '0'))
KSUB = int(os.environ.get('KSUB', '0'))
SQ = os.environ.get('KQ', 'pool')
NSLOT = int(os.environ.get('KNSLOT', '4'))
import contextlib
import numpy as np
import ml_dtypes
import concourse.bass as bass
import concourse.mybir as mybir
from concourse.bass_utils import run_bass_kernel_spmd

F32 = mybir.dt.float32
BF16 = mybir.dt.bfloat16
I32 = mybir.dt.int32
AF = mybir.ActivationFunctionType
ALU = mybir.AluOpType

D = 2048
NKC = 16
FFH = 5632
NEG = -30000.0


class _Op:
    __slots__ = ("eng", "fn", "reads", "writes", "dma", "idx", "waits", "signaled", "dsem", "dval", "rank", "barrier")


class Sched:
    ENGS = ("pe", "act", "dve", "pool", "sp")

    def __init__(self, n_dma_sems=14):
        self.ops = []
        self.n_dma_sems = n_dma_sems

    def add(self, eng, fn, reads=(), writes=(), dma=False, barrier=False):
        op = _Op()
        op.eng = eng
        op.fn = fn
        writes = list(writes) + [r for r in reads if r == "psb" or (isinstance(r, tuple) and r[0] == "ps")]
        op.reads = tuple(reads)
        op.writes = tuple(writes)
        op.dma = dma
        op.signaled = False
        op.barrier = barrier
        self.ops.append(op)
        return op

    def barrier(self):
        for e in self.ENGS:
            self.add(e, None, barrier=True)

    def analyze(self):
        last_w = {}
        readers = {}
        cnt = {e: 0 for e in self.ENGS}
        clock = {e: {} for e in self.ENGS}
        opclock = {}
        dma_rr = {e: 0 for e in self.ENGS}
        dma_val = {}
        dma_last = {}
        last_on = {}
        ops = self.ops
        for i, op in enumerate(ops):
            e = op.eng
            cnt[e] += 1
            op.idx = cnt[e]
            deps = set()
            if op.barrier:
                deps.update(last_on.values())
                deps.update(dma_last.values())
            for r in op.reads:
                w = last_w.get(r)
                if w is not None:
                    deps.add(w)
            for w_ in op.writes:
                w = last_w.get(w_)
                if w is not None:
                    deps.add(w)
                for rd in readers.get(w_, ()):
                    deps.add(rd)
            if op.dma:
                k = dma_rr[e]
                dma_rr[e] = (k + 1) % self.n_dma_sems
                skey = ("dma", e, k)
                op.dsem = skey
                op.dval = dma_val.get(skey, 0) + 16
                dma_val[skey] = op.dval
                prev = dma_last.get(skey)
                if prev is not None:
                    deps.add(prev)
                dma_last[skey] = i
            ck = clock[e]
            waits = []
            for d in sorted(deps):
                dop = ops[d]
                if dop.fn is None:
                    continue
                if dop.dma:
                    key, val = dop.dsem, dop.dval
                else:
                    key, val = dop.eng, dop.idx
                    if key == e and e in ("pe", "sp"):
                        continue
                if ck.get(key, 0) >= val:
                    continue
                waits.append((key, val, d))
                dop.signaled = True
                ck[key] = val
                oc = opclock.get(d)
                if oc is not None:
                    for k2, v2 in oc.items():
                        if ck.get(k2, 0) < v2:
                            ck[k2] = v2
            op.waits = waits
            oc = dict(ck)
            if not op.dma:
                oc[e] = op.idx
                if op.fn is not None:
                    last_on[e] = i
            opclock[i] = oc
            for r in op.reads:
                readers.setdefault(r, []).append(i)
            for w_ in op.writes:
                last_w[w_] = i
                readers[w_] = []
            if len(opclock) > 60000:
                for j in list(opclock.keys())[:30000]:
                    del opclock[j]
        rank = {e: 0 for e in self.ENGS}
        for op in ops:
            if not op.dma and op.signaled:
                rank[op.eng] += 1
                op.rank = rank[op.eng]
        self.stats = dict(cnt=cnt, rank=rank, dma=dict(dma_val))

    def emit(self, nc):
        self.analyze()
        ops = self.ops
        with contextlib.ExitStack() as st:
            sems = {}
            for e in self.ENGS:
                sems[e] = st.enter_context(nc.semaphore("s_" + e))
            for key in self.stats["dma"]:
                sems[key] = st.enter_context(nc.semaphore("d_%s_%d" % (key[1], key[2])))
            block = st.enter_context(nc.Block())
            per = {e: [op for op in ops if op.eng == e] for e in self.ENGS}

            def make(ename):
                def body(eng):
                    for op in per[ename]:
                        for (key, val, d) in op.waits:
                            dop = ops[d]
                            eng.wait_ge(sems[key], val if dop.dma else dop.rank)
                        if op.fn is None:
                            continue
                        ins = op.fn(eng)
                        if op.dma:
                            ins.then_inc(sems[op.dsem], 16)
                        elif op.signaled:
                            ins.then_inc(sems[ename], 1)
                return body

            block.tensor(make("pe"))
            block.scalar(make("act"))
            block.vector(make("dve"))
            block.gpsimd(make("pool"))
            block.sync(make("sp"))


WB = {
    "qk": (4, 16, 512), "v": (4, 16, 512), "g": (4, 16, 512), "gk": (1, 16, 16), "cq": (1, 16, 512),
    "ckv": (1, 16, 512), "kr": (1, 16, 128), "ga": (4, 16, 512), "gb": (4, 16, 512),
    "uq": (8, 4, 512), "uk": (4, 4, 512), "uv": (4, 4, 512),
    "mo": (4, 16, 512), "go": (4, 16, 512), "out": (4, 16, 512), "fin": (22, 16, 512), "dn": (4, 44, 512),
}


class Builder:
    def __init__(self, nc, PRE, OWN):
        self.nc = nc
        self.PRE = PRE
        self.OWN = OWN
        self.NK = PRE + OWN
        self.S = Sched()
        self.st = contextlib.ExitStack()
        self.bank_i = 0
        self.eng_i = 0
        self.wslot_i = 0

    def din(self, name, shape, dt=F32):
        return self.nc.dram_tensor(name, list(shape), dt, kind="ExternalInput").ap()

    def sb(self, name, shape, dt):
        return self.st.enter_context(self.nc.sbuf_tensor("sb_" + name, list(shape), dt))

    def mm(self, out, lhsT, rhs, start, stop, R, W):
        self.S.add("pe", lambda e: e.matmul(out, lhsT=lhsT, rhs=rhs, start=start, stop=stop), R, W)

    def tr(self, out, in_, ident, R, W):
        self.S.add("pe", lambda e: e.transpose(out=out, in_=in_, identity=ident), R, W)

    def act(self, out, in_, func, R, W, scale=1.0, bias=None, accum=None):
        kw = {}
        if bias is not None:
            kw["bias"] = bias
        if accum is not None:
            kw["accum_out"] = accum
        self.S.add("act", lambda e: e.activation(out=out, in_=in_, func=func, scale=scale, **kw), R, W)

    def tt(self, eng, out, a, b, op, R, W):
        self.S.add(eng, lambda e: e.tensor_tensor(out=out, in0=a, in1=b, op=op), R, W)

    def ts(self, eng, out, a, s1, s2, op0, op1, R, W):
        if s2 is None:
            self.S.add(eng, lambda e: e.tensor_scalar(out=out, in0=a, scalar1=s1, scalar2=None, op0=op0), R, W)
        else:
            self.S.add(eng, lambda e: e.tensor_scalar(out=out, in0=a, scalar1=s1, scalar2=s2, op0=op0, op1=op1), R, W)

    def stt(self, eng, out, in0, scalar, in1, op0, op1, R, W):
        self.S.add(eng, lambda e: e.scalar_tensor_tensor(out=out, in0=in0, scalar=scalar, in1=in1, op0=op0, op1=op1), R, W)

    def cp(self, eng, out, in_, R, W):
        if eng == "act":
            self.S.add("act", lambda e: e.activation(out=out, in_=in_, func=AF.Identity), R, W)
        else:
            self.S.add(eng, lambda e: e.tensor_copy(out=out, in_=in_), R, W)

    def recip(self, out, in_, R, W):
        self.S.add("dve", lambda e: e.reciprocal(out=out, in_=in_), R, W)

    def memset(self, eng, ap, val, W):
        self.S.add(eng, lambda e: e.memset(ap, val), (), W)

    def dma(self, q, out, in_, R, W):
        self.S.add(q, lambda e: e.dma_start(out=out, in_=in_), R, W, dma=True)

    def bank(self):
        i = self.bank_i
        self.bank_i = (i + 1) % 7
        return self.ps[i], ("ps", i)

    def rot(self, engs=("act", "dve")):
        self.eng_i += 1
        return engs[self.eng_i % len(engs)]

    def wload(self, blk, g, kc0, nkc):
        i = self.wslot_i
        self.wslot_i = (i + 1) % self.NSLOT
        ng, kcw, ncols = WB[blk]
        dst = self.wring[:, i, 0:nkc, 0:ncols]
        src = self.wscr[blk][g, :, kc0:kc0 + nkc, :]
        self.dma("sp", dst, src, [], [("w", i)])
        return self.wring[:, i], ("w", i)

    def wload_multi(self, blk, g0, ng):
        i = self.wslot_i
        self.wslot_i = (i + 1) % self.NSLOT
        for j in range(ng):
            self.dma("sp", self.wring[:, i, j * 4:(j + 1) * 4, :], self.wscr[blk][g0 + j], [], [("w", i)])
        return self.wring[:, i], ("w", i)

    def proj_tm(self, blk, g, actT, actkeys, ncols=512, col0=0, kcw=16):
        ps, pk = self.bank()
        for half in range(0, kcw, 8):
            n = min(8, kcw - half)
            w, wk = self.wload(blk, g, half, n)
            for j in range(n):
                kc = half + j
                self.mm(ps[:, 0:ncols], actT[:, kc, :], w[:, j, col0:col0 + ncols], kc == 0, kc == kcw - 1,
                        [wk] + actkeys, [pk])
        return ps, pk

    def build(self):
        nc = self.nc
        PRE, OWN, NK = self.PRE, self.OWN, self.NK
        NTP, NTO = PRE // 128, OWN // 128
        self.NSLOT = NSLOT
        I = {}
        I["xp"] = self.din("xp", [PRE, D])
        I["xo"] = self.din("xo", [OWN, D])
        I["posp"] = self.din("posp", [1, PRE], I32)
        I["poso"] = self.din("poso", [1, OWN], I32)
        I["c_fm"] = self.din("c_fm", [128, 16])
        I["flag"] = self.din("flag", [128, 2])
        I["ada_w"] = self.din("ada_w", [D, 6 * D])
        I["ada_b_fm"] = self.din("ada_b_fm", [128, 96])
        I["nmg_fm"] = self.din("nmg_fm", [128, 16])
        I["nfg_fm"] = self.din("nfg_fm", [128, 16])
        I["w_in"] = self.din("w_in", [D, 11344])
        I["gkw"] = self.din("gkw", [17, 1024])
        I["onorm"] = self.din("onorm", [1, 512])
        I["gla_wo"] = self.din("gla_wo", [D, D])
        I["qng_fm"] = self.din("qng_fm", [128, 4])
        I["kvng_fm"] = self.din("kvng_fm", [128, 4])
        I["wuq"] = self.din("wuq", [512, 3072])
        I["wukv"] = self.din("wukv", [512, 4096])
        I["mla_wo"] = self.din("mla_wo", [D, D])
        I["w_out"] = self.din("w_out", [D, D])
        I["ffn_in"] = self.din("ffn_in", [D, 2 * FFH])
        I["ffn_dn"] = self.din("ffn_dn", [FFH, D])
        I["fing"] = self.din("fing", [1, D])
        I["ident"] = self.din("ident", [128, 128])
        I["identb"] = self.din("identb", [128, 128], BF16)
        I["tri"] = self.din("tri", [128, 128])
        I["uni"] = self.din("uni", [128, 128])
        I["mask01"] = self.din("mask01", [128, 128])
        I["maskneg"] = self.din("maskneg", [128, 128], BF16)
        I["invf"] = self.din("invf", [64, 1])
        I["sgn"] = self.din("sgn", [64, 1])
        out = nc.dram_tensor("out", [OWN, D], F32, kind="ExternalOutput").ap()
        self.wscr = {}
        for name, (ng, kcw, ncols) in WB.items():
            self.wscr[name] = nc.dram_tensor("ws_" + name, [ng, 128, kcw, ncols], BF16).ap()
        KTc = nc.dram_tensor("KTc", [16, 128, NK], BF16).ap()
        KRc = nc.dram_tensor("KRc", [64, NK], BF16).ap()
        Vc = nc.dram_tensor("Vc", [NK, 2048], BF16).ap()

        sb = self.sb
        self.ps = [self.st.enter_context(nc.psum_tensor("ps%d" % i, [128, 512], F32)) for i in range(7)]
        psb = self.st.enter_context(nc.psum_tensor("psb", [128, 1024], BF16))
        xstg = sb("xstg", [128, D], F32)
        xt = sb("xt", [128, D], F32)
        hT = sb("hT", [128, 16, 128], BF16)
        self.wring = sb("wring", [128, self.NSLOT, 8, 512], BF16)
        S32 = sb("S32", [128, 4, 2, 512], F32)
        Sb = sb("Sb", [128, 4, 2, 512], BF16)
        Fb = sb("Fb", [128, 8, 512], F32)
        Fbf = Fb[:].bitcast(BF16)
        Vb = sb("Vb", [128, 2, 16, 132], BF16)
        pb = sb("pb", [128, 3, 512], BF16)
        big = sb("big", [128, 48, 128], BF16)
        otm = sb("otm", [128, D], BF16)
        mtm = sb("mtm", [128, D], BF16)
        tmp = [sb("tmp%d" % i, [128, 512], F32) for i in range(4)]
        gate_m = sb("gate_m", [128, D], F32)
        gate_f = sb("gate_f", [128, D], F32)
        fing = sb("fing", [128, D], F32)
        onorm = sb("onormb", [128, 512], F32)
        ident = sb("ident", [128, 128], F32)
        identb = sb("identb", [128, 128], BF16)
        tri = sb("tri", [128, 128], F32)
        uni = sb("uni", [128, 128], F32)
        mask01 = sb("mask01", [128, 128], F32)
        maskneg = sb("maskneg", [128, 128], BF16)
        cst = sb("cst", [128, 8], F32)
        flag = sb("flag", [128, 2], F32)
        mods = sb("mods", [128, 96], F32)
        g1 = sb("g1", [128, 2, 16], F32)
        cfm = sb("cfm", [128, 16], F32)
        cact = sb("cact", [128, 16], F32)
        nmg = sb("nmg", [128, 2, 16], F32)
        adab = sb("adab", [128, 96], F32)
        qng = sb("qng", [128, 4], F32)
        kvng = sb("kvng", [128, 4], F32)
        invf = sb("invf", [64, 1], F32)
        sgn = sb("sgn", [64, 1], F32)
        st1 = sb("st1", [128, 8], F32)
        pgk = sb("pgk", [32, 128], F32)
        gkw = sb("gkw", [17, 1024], F32)
        eL = sb("eL", [128, 2, 256], F32)
        ebuf = sb("ebuf", [128, 3, 256], F32)
        qp = sb("qp", [128, 2, 128], BF16)
        kp = sb("kp", [128, 2, 128], BF16)
        kd = sb("kd", [128, 256], BF16)
        AT = sb("AT", [128, 128], BF16)
        vtm = sb("vtm", [128, 512], BF16)
        on = sb("on", [128, 512], BF16)
        cT = sb("cT", [128, 512], F32)
        sq = sb("sq", [128, 512], BF16)
        rq = sb("rq", [128, 128], F32)
        cqn = sb("cqn", [128, 4, 128], BF16)
        ckvn = sb("ckvn", [128, 4, 128], BF16)
        posi = sb("posi", [64, 128], I32)
        pf = sb("pf", [64, 4, 128], F32)
        cs = sb("cs", [64, 2, 128], F32)
        QN = sb("QN", [128, 16, 128], BF16)
        QR = sb("QR", [128, 16, 128], BF16)
        krT = sb("krT", [64, 128], BF16)
        KTt = sb("KTt", [128, 16, 128], BF16)
        Vt = sb("Vt", [128, D], BF16)
        rec = sb("rec", [128, 2], F32)

        S = self.S
        EPS = cst[:, 0:1]
        if os.environ.get('KDEBUG'):
            print('SBUF bytes remaining', nc.sbuf_bytes_remaining)
        for nm, t in (("ident", ident), ("identb", identb), ("tri", tri), ("uni", uni), ("mask01", mask01),
                      ("maskneg", maskneg), ("flag", flag), ("qng_fm", qng), ("kvng_fm", kvng), ("invf", invf),
                      ("sgn", sgn), ("c_fm", cfm), ("ada_b_fm", adab), ("gkw", gkw)):
            self.dma("sp", t[:], I[nm], [], [nm])
        self.dma("sp", nmg[:, 0, :], I["nmg_fm"], [], ["nmg0"])
        self.dma("sp", nmg[:, 1, :], I["nfg_fm"], [], ["nmg1"])
        self.dma("sp", onorm[:], I["onorm"].partition_broadcast(128), [], ["onorm"])
        self.dma("sp", fing[:], I["fing"].partition_broadcast(128), [], ["fing"])
        self.memset("pool", cst[:, 0:1], 1e-6, ["cst"])
        self.memset("pool", cst[:, 1:2], math.log(1.0 / 16.0), ["cst"])
        self.memset("pool", cst[:, 2:3], 0.0, ["cst"])
        self.memset("pool", pgk[:], 1.0, ["pgk"])
        self.memset("pool", Vb[:], 1.0, [("Vb", 0), ("Vb", 1)])
        self.act(cact[:], cfm[:], AF.Exp, ["c_fm"], ["cact"], scale=-1.0)
        self.ts("dve", cact[:], cact[:], 1.0, None, ALU.add, None, ["cact"], ["cact"])
        self.recip(cact[:], cact[:], ["cact"], ["cact"])
        self.tt("dve", cact[:], cact[:], cfm[:], ALU.mult, ["cact", "c_fm"], ["cact"])
        stgA = S32[:].rearrange("p a b c -> p (a b) c")
        stgB = Fb[:]
        SKEYS = [("S", h, k) for h in range(4) for k in range(2)]
        pm, pmk = self.ps[6], ("ps", 6)
        adaw = I["ada_w"].rearrange("(kc p) n -> p kc n", p=128)
        def crep_k(kc):
            return tmp[kc // 4][:, (kc % 4) * 128:(kc % 4 + 1) * 128], ("tmp", kc // 4)
        for q4 in range(4):
            self.memset("pool", tmp[q4][:], 1.0, [("tmp", q4)])
        for kc in range(16):
            ck_, ckk_ = crep_k(kc)
            self.ts("dve", ck_, ck_, cact[:, kc:kc + 1], None, ALU.mult, None, ["cact", ckk_], [ckk_])
        for g in range(24):
            self.dma("sp", stgA, adaw[:, 0:8, g * 512:(g + 1) * 512], [], SKEYS)
            self.dma("sp", stgB, adaw[:, 8:16, g * 512:(g + 1) * 512], [], ["Fb"])
            for cc in range(4):
                j = g * 4 + cc
                for kc in range(16):
                    stg, skeys = (stgA, SKEYS) if kc < 8 else (stgB, ["Fb"])
                    self.mm(pm[:, j:j + 1], stg[:, kc % 8, cc * 128:(cc + 1) * 128], cact[:, kc:kc + 1],
                            kc == 0, kc == 15, skeys + ["cact"], [pmk])
            if g // 4 in (2, 5):
                gbank, gbk = self.ps[g % 2], ("ps", g % 2)
                for kc in range(16):
                    stg, skeys = (stgA, SKEYS) if kc < 8 else (stgB, ["Fb"])
                    ck_, ckk_ = crep_k(kc)
                    self.mm(gbank[:, :], ck_, stg[:, kc % 8, :], kc == 0, kc == 15, skeys + [ckk_], [gbk])
                dst = gate_m if g // 4 == 2 else gate_f
                c0 = (g % 4) * 512
                self.cp("act", dst[:, c0:c0 + 512], gbank[:, :], [gbk], [("gate", g // 4, g % 4)])
        self.tt("dve", mods[:], pm[:, 0:96], adab[:], ALU.add, [pmk, "ada_b_fm"], ["mods"])
        for gi, dst in ((2, gate_m), (5, gate_f)):
            for q4 in range(4):
                self.dma("sp", tmp[q4][:], I_adab_row(self, gi, q4), [], [("tmp", q4)])
                self.tt("dve", dst[:, q4 * 512:(q4 + 1) * 512], dst[:, q4 * 512:(q4 + 1) * 512], tmp[q4][:], ALU.add,
                        [("gate", gi, q4), ("tmp", q4)], [("gate", gi, q4)])
        for i, off in ((0, 16), (1, 64)):
            self.ts("dve", g1[:, i, :], mods[:, off:off + 16], 1.0, None, ALU.add, None, ["mods"], [("g1", i)])
            self.tt("dve", g1[:, i, :], g1[:, i, :], nmg[:, i, :], ALU.mult, [("g1", i), "nmg%d" % i], [("g1", i)])

        if STOP == 1:
            S.barrier(); S.emit(nc); self.st.close(); return
        self.cast_i = 0

        def cast_unit(src, dst, nkc, ncols, perm=None):
            i = self.cast_i
            self.cast_i += 1
            stg, skeys = (stgA, SKEYS) if i % 2 == 0 else (stgB, ["Fb"])
            slot = i % self.NSLOT
            cb = self.wring[:, slot]
            self.dma("sp", stg[:, 0:nkc, 0:ncols], src, [], skeys)
            eng = ("dve", "act")[i % 2]
            if perm is None:
                self.cp(eng, cb[:, 0:nkc, 0:ncols], stg[:, 0:nkc, 0:ncols], skeys, [("w", slot)])
                self.dma("sp", dst, cb[:, 0:nkc, 0:ncols], [("w", slot)], ["wscr"])
            else:
                dcols = 0
                for (d0, s0, n) in perm:
                    self.cp(eng, cb[:, 0:nkc, d0:d0 + n], stg[:, 0:nkc, s0:s0 + n], skeys, [("w", slot)])
                    dcols = max(dcols, d0 + n)
                self.dma("sp", dst, cb[:, 0:nkc, 0:dcols], [("w", slot)], ["wscr"])

        def cast_block(src2d, K, c0, ncols, dst4, g, dcol0=0, perm=None, dcols=None):
            srcv = src2d.rearrange("(kc p) n -> p kc n", p=128)
            kcw = K // 128
            for k0 in range(0, kcw, 8):
                n = min(8, kcw - k0)
                dc = ncols if dcols is None else dcols
                cast_unit(srcv[:, k0:k0 + n, c0:c0 + ncols], dst4[g, :, k0:k0 + n, dcol0:dcol0 + dc], n, ncols, perm)

        W = self.wscr
        win = I["w_in"]
        for h in range(4):
            cast_block(win, D, h * 256, 256, W["qk"], h, 0)
            cast_block(win, D, 1024 + h * 256, 256, W["qk"], h, 256)
            cast_block(win, D, 2048 + h * 512, 512, W["v"], h)
            cast_block(win, D, 4096 + h * 512, 512, W["g"], h)
            cast_block(win, D, 7248 + h * 512, 512, W["ga"], h)
            cast_block(win, D, 9296 + h * 512, 512, W["gb"], h)
            cast_block(I["mla_wo"], D, h * 512, 512, W["mo"], h)
            cast_block(I["gla_wo"], D, h * 512, 512, W["go"], h)
            cast_block(I["w_out"], D, h * 512, 512, W["out"], h)
            cast_block(I["ffn_dn"], FFH, h * 512, 512, W["dn"], h)
        cast_block(win, D, 6144, 16, W["gk"], 0)
        cast_block(win, D, 6160, 512, W["cq"], 0)
        cast_block(win, D, 6672, 512, W["ckv"], 0)
        cast_block(win, D, 7184, 64, W["kr"], 0, perm=[(0, 0, 64), (64, 32, 32), (96, 0, 32)], dcols=128)
        for g in range(22):
            cast_block(I["ffn_in"], D, g * 512, 512, W["fin"], g)
        for g in range(8):
            perm = []
            for hh in range(2):
                perm += [(hh * 256, hh * 192, 192), (hh * 256 + 192, hh * 192 + 160, 32), (hh * 256 + 224, hh * 192 + 128, 32)]
            cast_block(I["wuq"], 512, g * 384, 384, W["uq"], g, perm=perm, dcols=512)
        for g in range(4):
            permk = [(j * 128, j * 256, 128) for j in range(4)]
            permv = [(j * 128, j * 256 + 128, 128) for j in range(4)]
            for hf in range(2):
                pk_ = [(hf * 256 + j * 128, j * 256, 128) for j in range(2)]
                pv_ = [(hf * 256 + j * 128, j * 256 + 128, 128) for j in range(2)]
                self._cast_partial(cast_unit, I["wukv"], g * 1024 + hf * 512, W["uk"], g, pk_, hf)
                self._cast_partial(cast_unit, I["wukv"], g * 1024 + hf * 512, W["uv"], g, pv_, hf)
        S.barrier()
        if STOP == 2:
            S.emit(nc); self.st.close(); return
        self.memset("dve", S32[:].rearrange("p a b c -> p (a b c)"), 0.0, [("S", h, k) for h in range(4) for k in range(2)])
        self.memset("dve", Sb[:].rearrange("p a b c -> p (a b c)"), 0.0, [("Sb", h, k) for h in range(4) for k in range(2)])

        self.memset("dve", QR[:].rearrange("p a b -> p (a b)"), 0.0, [("QR", h) for h in range(16)])
        self.memset("dve", Fb[:, 4:8, :].rearrange("p a b -> p (a b)"), 0.0, [("KRb", 0), ("KRb", 1)])
        XT = ["xt"]
        HT = [("hT", kc) for kc in range(16)]

        def rstd_from_ss(ss_ap, inv_n, out_ap, R, Wk, extra_bias=None):
            self.act(out_ap, ss_ap, AF.Ln, R + ["cst"], Wk, scale=inv_n, bias=EPS)
            self.act(out_ap, out_ap, AF.Exp, Wk + ["cst"], Wk, scale=-0.5, bias=(extra_bias if extra_bias is not None else cst[:, 2:3]))

        def norm_to_hT(which):
            self.act(xstg[:], xt[:], AF.Square, XT, ["xstg", "ss"], accum=st1[:, 0:1])
            rstd_from_ss(st1[:, 0:1], 1.0 / D, st1[:, 1:2], ["ss"], ["rstd"])
            self.ts("dve", xstg[:], xt[:], st1[:, 1:2], None, ALU.mult, None, XT + ["rstd", "xstg"], ["xstg"])
            sh0 = 0 if which == 0 else 48
            for q4 in range(4):
                ps, pk = self.bank()
                for j in range(4):
                    kc = q4 * 4 + j
                    self.tr(ps[:, j * 128:(j + 1) * 128], xstg[:, kc * 128:(kc + 1) * 128], ident[:], ["xstg", "ident"], [pk])
                for j in range(4):
                    kc = q4 * 4 + j
                    if True:
                        self.act(hT[:, kc, :], ps[:, j * 128:(j + 1) * 128], AF.Identity, [pk, ("g1", which), "mods"], [("hT", kc)],
                                 scale=g1[:, which, kc:kc + 1], bias=mods[:, sh0 + kc:sh0 + kc + 1])
                    else:
                        self.ts("dve", hT[:, kc, :], ps[:, j * 128:(j + 1) * 128], g1[:, which, kc:kc + 1],
                                mods[:, sh0 + kc:sh0 + kc + 1], ALU.mult, ALU.add, [pk, ("g1", which), "mods"], [("hT", kc)])

        def fm_proj(ps_out, pk, w, wk, kcs, col0, M, rhsT, rkeys, start_kc=0, total=None):
            n = len(kcs)
            for j, kc in enumerate(kcs):
                self.mm(ps_out, w[:, j, col0:col0 + M], rhsT[:, kc, :], kc == 0, kc == (total or 16) - 1, [wk] + rkeys, [pk])

        def tile(ti, prefix):
            key0 = ti * 128 if prefix else PRE + ti * 128
            xsrc = (I["xp"] if prefix else I["xo"])[ti * 128:(ti + 1) * 128, :]
            psrc = (I["posp"] if prefix else I["poso"])[:, ti * 128:(ti + 1) * 128]
            self.dma(SQ, xt[:], xsrc, [], XT)
            self.dma(SQ, posi[:], psrc.partition_broadcast(64), [], ["posi"])
            norm_to_hT(0)
            if STOP == 31:
                return
            C1 = 6.28125
            C2 = 2 * math.pi - C1
            a, kf, t2, r2 = pf[:, 0, :], pf[:, 1, :], pf[:, 2, :], pf[:, 3, :]
            self.cp("dve", kf, posi[:], ["posi"], ["pf"])
            self.ts("dve", a, kf, invf[:, 0:1], None, ALU.mult, None, ["pf", "invf"], ["pf"])
            self.ts("dve", kf, a, 1.0 / (2 * math.pi), None, ALU.mult, None, ["pf"], ["pf"])
            self.cp("dve", posi[:], kf, ["pf"], ["posi"])
            self.cp("dve", kf, posi[:], ["posi"], ["pf"])
            self.stt("dve", a, kf, -C1, a, ALU.mult, ALU.add, ["pf"], ["pf"])
            self.stt("dve", a, kf, -C2, a, ALU.mult, ALU.add, ["pf"], ["pf"])
            self.ts("dve", t2, a, math.pi, 2 * math.pi, ALU.is_gt, ALU.mult, ["pf"], ["pf"])
            self.tt("dve", a, a, t2, ALU.subtract, ["pf"], ["pf"])
            self.ts("dve", r2, a, math.pi / 2, None, ALU.add, None, ["pf"], ["pf"])
            self.ts("dve", t2, r2, math.pi, 2 * math.pi, ALU.is_gt, ALU.mult, ["pf"], ["pf"])
            self.tt("dve", r2, r2, t2, ALU.subtract, ["pf"], ["pf"])
            self.act(cs[:, 0, :], r2, AF.Sin, ["pf"], ["cs"])
            self.act(cs[:, 1, :], a, AF.Sin, ["pf", "cs"], ["cs"])
            self.ts("dve", cs[:, 1, :], cs[:, 1, :], sgn[:, 0:1], None, ALU.mult, None, ["cs", "sgn"], ["cs"])

            if STOP == 32:
                return
            w, wk = self.wload("gk", 0, 0, 8)
            w2, wk2 = self.wload("gk", 0, 8, 8)
            ps, pk = self.bank()
            for kc in range(16):
                ww, wwk = (w, wk) if kc < 8 else (w2, wk2)
                self.mm(ps[0:16, 0:128], ww[:, kc % 8, 0:16], hT[:, kc, :], kc == 0, kc == 15, [wwk, ("hT", kc)], [pk])
            self.cp("dve", pgk[0:16, :], ps[0:16, 0:128], [pk, "pgk"], ["pgk"])
            for h in range(4):
                ps, pk = self.bank()
                self.mm(ps[:, 0:256], pgk[0:17, :], gkw[0:17, h * 256:(h + 1) * 256], True, True, ["pgk", "gkw"], [pk])
                self.act(eL[:, 0, :], ps[:, 0:256], AF.Exp, [pk], ["e"], scale=-1.0)
                self.act(eL[:, 1, :], eL[:, 0, :], AF.Ln, ["e"], ["L"], bias=1.0)
                Lh = eL[:, 1, :]
                psb_, pkb = self.bank()
                for dkc in range(2):
                    self.mm(psb_[:, dkc * 128:(dkc + 1) * 128], Lh[:, dkc * 128:(dkc + 1) * 128], tri[:], True, True, ["L", "tri"], [pkb])
                self.mm(psb_[:, 256:512], uni[:], Lh, True, True, ["L", "uni"], [pkb])
                self.act(ebuf[:, 0, :], psb_[:, 0:256], AF.Exp, [pkb], ["eb"])
                if not prefix:
                    self.act(ebuf[:, 1, :], psb_[:, 0:256], AF.Exp, [pkb], ["enb"], scale=-1.0)
                self.act(ebuf[:, 2, :], psb_[:, 256:512], AF.Exp, [pkb], ["er"])
                wA, wAk = self.wload("qk", h, 0, 8)
                wB, wBk = self.wload("qk", h, 8, 8)
                for (c0, dst, dkey, eidx, ekey) in ((0, qp, "qp", 0, "eb"), (256, kp, "kp", 1, "enb")):
                    if prefix:
                        continue
                    pq, pqk = self.bank()
                    for dkc in range(2):
                        for kc in range(16):
                            ww, wwk = (wA, wAk) if kc < 8 else (wB, wBk)
                            self.mm(pq[:, dkc * 128:(dkc + 1) * 128], ww[:, kc % 8, c0 + dkc * 128:c0 + (dkc + 1) * 128],
                                    hT[:, kc, :], kc == 0, kc == 15, [wwk, ("hT", kc)], [pqk])
                    self.tt("dve", dst[:].rearrange("p a b -> p (a b)"), pq[:, 0:256], ebuf[:, eidx, :], ALU.mult, [pqk, ekey], [dkey])
                pq, pqk = self.bank()
                for kc in range(16):
                    ww, wwk = (wA, wAk) if kc < 8 else (wB, wBk)
                    self.mm(pq[:, 0:256], hT[:, kc, :], ww[:, kc % 8, 256:512], kc == 0, kc == 15, [wwk, ("hT", kc)], [pqk])
                self.tt("dve", kd[:], pq[:, 0:256], ebuf[:, 2, :], ALU.mult, [pqk, "er"], ["kd"])
                pv, pvk = self.proj_tm("v", h, hT, HT)
                self.cp("act", vtm[:], pv[:, :], [pvk], ["vtm"])
                if not prefix:
                    pg, pgk_ = self.proj_tm("g", h, hT, HT)
                    ge, gsl = tmp[0], tmp[1]
                    self.act(ge[:], pg[:, :], AF.Exp, [pgk_], [("tmp", 0)], scale=-1.0)
                    self.ts("dve", ge[:], ge[:], 1.0, None, ALU.add, None, [("tmp", 0)], [("tmp", 0)])
                    self.recip(ge[:], ge[:], [("tmp", 0)], [("tmp", 0)])
                    self.tt("dve", gsl[:], pg[:, :], ge[:], ALU.mult, [pgk_, ("tmp", 0)], [("tmp", 1)])
                    self.tt("dve", gsl[:], gsl[:], onorm[:], ALU.mult, [("tmp", 1), "onorm"], [("tmp", 1)])
                    pa, pak = self.bank()
                    for dkc in range(2):
                        self.mm(pa[:, 0:128], kp[:, dkc, :], qp[:, dkc, :], dkc == 0, dkc == 1, ["kp", "qp"], [pak])
                    self.tt("dve", AT[:], pa[:, 0:128], mask01[:], ALU.mult, [pak, "mask01"], ["AT"])
                    po, pok = self.bank()
                    self.mm(po[:, :], AT[:], vtm[:], True, False, ["AT", "vtm"], [pok])
                    for dkc in range(2):
                        self.mm(po[:, :], qp[:, dkc, :], Sb[:, h, dkc, :], False, dkc == 1, ["qp", ("Sb", h, dkc)], [pok])
                    self.act(tmp[2][:], po[:, :], AF.Square, [pok], [("tmp", 2), "ss2"], accum=st1[:, 2:3])
                    rstd_from_ss(st1[:, 2:3], 1.0 / (512.0 * 256.0), st1[:, 3:4], ["ss2"], ["rstd2"], extra_bias=cst[:, 1:2])
                    self.stt("dve", on[:], po[:, :], st1[:, 3:4], gsl[:], ALU.mult, ALU.mult, [pok, "rstd2", ("tmp", 1)], ["on"])
                    for vc in range(4):
                        self.tr(psb[:, vc * 128:(vc + 1) * 128], on[:, vc * 128:(vc + 1) * 128], identb[:], ["on", "identb"], ["psb"])
                    self.cp("act", big[:, h * 4:(h + 1) * 4, :].rearrange("p a b -> p (a b)"), psb[:, 0:512], ["psb"],
                            [("big", h * 4 + j) for j in range(4)])
                for dkc in range(2):
                    pd, pdk = self.bank()
                    self.mm(pd[:, :], kd[:, dkc * 128:(dkc + 1) * 128], vtm[:], True, True, ["kd", "vtm"], [pdk])
                    self.stt("dve", S32[:, h, dkc, :], S32[:, h, dkc, :], ebuf[:, 0, dkc * 128 + 127:dkc * 128 + 128], pd[:, :],
                             ALU.mult, ALU.add, [("S", h, dkc), "eb", pdk], [("S", h, dkc)])
                    if not (prefix and ti < NTP - 1):
                        if prefix:
                            self.ts("dve", S32[:, h, dkc, :], S32[:, h, dkc, :], flag[:, 0:1], None, ALU.mult, None,
                                    [("S", h, dkc), "flag"], [("S", h, dkc)])
                        self.cp("act", Sb[:, h, dkc, :], S32[:, h, dkc, :], [("S", h, dkc)], [("Sb", h, dkc)])

            if STOP == 34:
                return
            def latent(blk, gfm, gkey, dst, dkey):
                pc, pck = self.bank()
                wA, wAk = self.wload(blk, 0, 0, 8)
                wB, wBk = self.wload(blk, 0, 8, 8)
                for c in range(4):
                    for kc in range(16):
                        ww, wwk = (wA, wAk) if kc < 8 else (wB, wBk)
                        self.mm(pc[:, c * 128:(c + 1) * 128], ww[:, kc % 8, c * 128:(c + 1) * 128], hT[:, kc, :],
                                kc == 0, kc == 15, [wwk, ("hT", kc)], [pck])
                self.cp("dve", cT[:], pc[:, :], [pck], ["cT"])
                if KSUB == 1:
                    return
                self.act(sq[:], pc[:, :], AF.Square, [pck], ["sq"])
                if KSUB == 2:
                    return
                p2, p2k = self.bank()
                for c in range(4):
                    self.mm(p2[:, 0:128], identb_ones[:], sq[:, c * 128:(c + 1) * 128], c == 0, c == 3, ["sq", "onesb"], [p2k])
                self.cp("dve", rq[:], p2[:, 0:128], [p2k], ["rq"])
                if KSUB == 3:
                    return
                self.act(rq[:], rq[:], AF.Ln, ["rq", "cst"], ["rq"], scale=1.0 / 512.0, bias=EPS)
                self.act(rq[:], rq[:], AF.Exp, ["rq"], ["rq"], scale=-0.5)
                for c in range(4):
                    self.tt("dve", cT[:, c * 128:(c + 1) * 128], cT[:, c * 128:(c + 1) * 128], rq[:], ALU.mult, ["cT", "rq"], ["cT"])
                    self.ts("dve", dst[:, c, :], cT[:, c * 128:(c + 1) * 128], gfm[:, c:c + 1], None, ALU.mult, None,
                            ["cT", gkey], [(dkey, c)])

            identb_ones = self.onesb
            latent("ckv", kvng, "kvng_fm", ckvn, "ckvn")
            CKV = [("ckvn", c) for c in range(4)]
            if STOP == 35:
                return
            wA, wAk = self.wload("kr", 0, 0, 8)
            wB, wBk = self.wload("kr", 0, 8, 8)
            pr, prk = self.bank()
            for half in range(2):
                for kc in range(16):
                    ww, wwk = (wA, wAk) if kc < 8 else (wB, wBk)
                    self.mm(pr[0:64, half * 128:(half + 1) * 128], ww[:, kc % 8, half * 64:(half + 1) * 64], hT[:, kc, :],
                            kc == 0, kc == 15, [wwk, ("hT", kc)], [prk])
            t1, t2_ = tmp[2][0:64, 0:128], tmp[2][0:64, 128:256]
            self.tt("dve", t1, pr[0:64, 0:128], cs[:, 0, :], ALU.mult, [prk, "cs"], [("tmp", 2)])
            self.tt("dve", t2_, pr[0:64, 128:256], cs[:, 1, :], ALU.mult, [prk, "cs", ("tmp", 2)], [("tmp", 2)])
            self.tt("dve", krT[:], t1, t2_, ALU.add, [("tmp", 2)], ["krT"])
            self.dma(SQ, KRc[:, key0:key0 + 128], krT[:], ["krT"], [("KRc", key0 // 128)])
            if STOP == 36:
                return
            wk_, wkk = self.wload_multi("uk", 0, 2)
            wk2_, wkk2 = self.wload_multi("uk", 2, 2)
            for g in range(4):
                ww, wwk = (wk_, wkk) if g < 2 else (wk2_, wkk2)
                pk_, pkk = self.bank()
                for j in range(4):
                    for kc in range(4):
                        self.mm(pk_[:, j * 128:(j + 1) * 128], ww[:, (g % 2) * 4 + kc, j * 128:(j + 1) * 128], ckvn[:, kc, :],
                                kc == 0, kc == 3, [wwk] + CKV, [pkk])
                self.cp(self.rot(), KTt[:, g * 4:(g + 1) * 4, :].rearrange("p a b -> p (a b)"), pk_[:, :], [pkk], [("KTt", g)])
            for hh_ in range(16):
                self.dma(SQ, KTc[hh_, :, key0:key0 + 128], KTt[:, hh_, :], [("KTt", hh_ // 4)], [("KTc", key0 // 128, hh_)])
            wv_, wvk = self.wload_multi("uv", 0, 2)
            wv2_, wvk2 = self.wload_multi("uv", 2, 2)
            for g in range(4):
                ww, wwk = (wv_, wvk) if g < 2 else (wv2_, wvk2)
                pv_, pvk_ = self.bank()
                for kc in range(4):
                    self.mm(pv_[:, :], ckvn[:, kc, :], ww[:, (g % 2) * 4 + kc, :], kc == 0, kc == 3, [wwk] + CKV, [pvk_])
                self.cp(self.rot(), Vt[:, g * 512:(g + 1) * 512], pv_[:, :], [pvk_], [("Vt", g)])
            self.dma(SQ, Vc[key0:key0 + 128, :], Vt[:], [("Vt", g) for g in range(4)], [("Vc", key0 // 128)])
            if prefix:
                return

            latent("cq", qng, "qng_fm", cqn, "cqn")
            CQ = [("cqn", c) for c in range(4)]
            for s2 in range(4):
                wq_, wqk = self.wload_multi("uq", s2 * 2, 2)
                pn, pnk = self.bank()
                pra, prak = self.bank()
                prb, prbk = self.bank()
                for j in range(4):
                    g_, hh = j // 2, j % 2
                    for kc in range(4):
                        self.mm(pn[:, j * 128:(j + 1) * 128], wq_[:, g_ * 4 + kc, hh * 256:hh * 256 + 128], cqn[:, kc, :],
                                kc == 0, kc == 3, [wqk] + CQ, [pnk])
                    for kc in range(4):
                        self.mm(pra[0:64, j * 128:(j + 1) * 128], wq_[:, g_ * 4 + kc, hh * 256 + 128:hh * 256 + 192], cqn[:, kc, :],
                                kc == 0, kc == 3, [wqk] + CQ, [prak])
                    for kc in range(4):
                        self.mm(prb[0:64, j * 128:(j + 1) * 128], wq_[:, g_ * 4 + kc, hh * 256 + 192:hh * 256 + 256], cqn[:, kc, :],
                                kc == 0, kc == 3, [wqk] + CQ, [prbk])
                self.cp("act", QN[:, s2 * 4:(s2 + 1) * 4, :].rearrange("p a b -> p (a b)"), pn[:, :], [pnk], [("QN", s2 * 4 + j) for j in range(4)])
                for j in range(4):
                    h = s2 * 4 + j
                    ta, tb = tmp[3][0:64, 0:128], tmp[3][0:64, 128:256]
                    self.tt("dve", ta, pra[0:64, j * 128:(j + 1) * 128], cs[:, 0, :], ALU.mult, [prak, "cs", ("tmp", 3)], [("tmp", 3)])
                    self.tt("dve", tb, prb[0:64, j * 128:(j + 1) * 128], cs[:, 1, :], ALU.mult, [prbk, "cs", ("tmp", 3)], [("tmp", 3)])
                    self.tt("dve", QR[0:64, h, :], ta, tb, ALU.add, [("tmp", 3)], [("QR", h)])

            nkb = (key0 + 128) // 128
            npre = PRE // 128
            SC = 192.0 ** -0.5
            KTb = [Fbf[:, par * 2:(par + 1) * 2, :].rearrange("p a k -> p (a k)") for par in range(2)]
            KRb = [Fbf[:, 4 + par * 2:4 + (par + 1) * 2, :].rearrange("p a k -> p (a k)") for par in range(2)]
            items = []
            sbs = []
            for h in range(16):
                for sb0 in range(0, nkb, 16):
                    nb = min(16, nkb - sb0)
                    par = self.kvpar
                    self.kvpar ^= 1
                    sbs.append((h, sb0, nb, par))
                    j = 0
                    while j < nb:
                        kb = sb0 + j
                        lim = min(4, nb - j)
                        if kb < npre:
                            lim = min(lim, npre - kb)
                        items.append(dict(h=h, par=par, j=j, kb=kb, n=lim, sbi=len(sbs) - 1, last_in_sb=False))
                        j += lim
                    items[-1]["last_in_sb"] = True

            def emit_load(k):
                h, sb0, nb, par = sbs[k]
                k0 = sb0 * 128
                ckeys = [(nm, kb) for nm in ("KRc", "Vc") for kb in range(sb0, sb0 + nb)] + [("KTc", kb, h) for kb in range(sb0, sb0 + nb)]
                self.dma("sp", KTb[par][:, 0:nb * 128], KTc[h, :, k0:k0 + nb * 128], ckeys, [("KTb", par)])
                self.dma("sp", KRb[par][0:64, 0:nb * 128], KRc[:, k0:k0 + nb * 128], ckeys, [("KRb", par)])
                self.dma("sp", Vb[:, par, 0:nb, 0:128],
                         Vc[k0:k0 + nb * 128, h * 128:(h + 1) * 128].rearrange("(b p) d -> p b d", p=128), ckeys, [("Vb", par)])

            def emit_S(it):
                h, par = it["h"], it["par"]
                si = self.st_i
                self.st_i = (si + 1) % 3
                it["si"] = si
                pst, pstk = self.ps[si], ("ps", si)
                n = it["n"]
                for jj in range(n):
                    kb = it["kb"] + jj
                    j = it["j"] + jj
                    diag = kb == nkb - 1
                    c0 = jj * 128
                    self.mm(pst[:, c0:c0 + 128], KTb[par][:, j * 128:(j + 1) * 128], QN[:, h, :], True, False, [("KTb", par), ("QN", h)], [pstk])
                    self.mm(pst[:, c0:c0 + 128], KRb[par][:, j * 128:(j + 1) * 128], QR[:, h, :], False, not diag, [("KRb", par), ("QR", h)], [pstk])
                    if diag:
                        self.mm(pst[:, c0:c0 + 128], identb[:], maskneg[:], False, True, ["identb", "maskneg"], [pstk])
                bias = flag[:, 1:2] if it["kb"] < npre else cst[:, 2:3]
                self.act(pb[:, si, 0:n * 128], pst[:, 0:n * 128], AF.Exp, [pstk, "flag", "cst"], [("pb", si)], scale=SC, bias=bias)

            def emit_PV(it):
                h, par, si = it["h"], it["par"], it["si"]
                po, pok = self.ps[3 + h % 2], ("ps", 3 + h % 2)
                for jj in range(it["n"]):
                    kb = it["kb"] + jj
                    j = it["j"] + jj
                    self.mm(po[:, 0:129], pb[:, si, jj * 128:(jj + 1) * 128], Vb[:, par, j, 0:129], kb == 0, kb == nkb - 1,
                            [("pb", si), ("Vb", par)], [pok])
                if it["kb"] + it["n"] == nkb:
                    self.recip(rec[:, 0:1], po[:, 128:129], [pok], ["rec"])
                    self.ts("dve", otm[:, h * 128:(h + 1) * 128], po[:, 0:128], rec[:, 0:1], None, ALU.mult, None, [pok, "rec"], [("otm", h)])

            emit_load(0)
            if len(sbs) > 1:
                emit_load(1)
            emit_S(items[0])
            for i in range(len(items)):
                if i + 1 < len(items):
                    emit_S(items[i + 1])
                emit_PV(items[i])
                if items[i]["last_in_sb"] and items[i]["sbi"] + 2 < len(sbs):
                    emit_load(items[i]["sbi"] + 2)
            for q4 in range(4):
                for j in range(4):
                    c = q4 * 4 + j
                    self.tr(psb[:, j * 128:(j + 1) * 128], otm[:, c * 128:(c + 1) * 128], identb[:], [("otm", c), "identb"], ["psb"])
                self.cp(self.rot(), big[:, 16 + q4 * 4:16 + (q4 + 1) * 4, :].rearrange("p a b -> p (a b)"), psb[:, 0:512], ["psb"],
                        [("big", 16 + q4 * 4 + j) for j in range(4)])

            ON = [("big", c) for c in range(16)]
            OM = [("big", 16 + c) for c in range(16)]
            onT = big[:, 0:16, :]
            omT = big[:, 16:32, :]
            for g in range(4):
                pa_, pak_ = self.proj_tm("ga", g, hT, HT)
                self.act(tmp[0][:], pa_[:, :], AF.Exp, [pak_], [("tmp", 0)], scale=-1.0)
                pb_, pbk_ = self.proj_tm("gb", g, hT, HT)
                self.act(tmp[1][:], pb_[:, :], AF.Exp, [pbk_], [("tmp", 1)], scale=-1.0)
                for i in range(2):
                    self.ts("dve", tmp[i][:], tmp[i][:], 1.0, None, ALU.add, None, [("tmp", i)], [("tmp", i)])
                    self.recip(tmp[i][:], tmp[i][:], [("tmp", i)], [("tmp", i)])
                py, pyk = self.proj_tm("go", g, onT, ON)
                self.tt("dve", tmp[0][:], py[:, :], tmp[0][:], ALU.mult, [pyk, ("tmp", 0)], [("tmp", 0)])
                pz, pzk = self.proj_tm("mo", g, omT, OM)
                self.tt("dve", tmp[1][:], pz[:, :], tmp[1][:], ALU.mult, [pzk, ("tmp", 1)], [("tmp", 1)])
                self.tt("dve", mtm[:, g * 512:(g + 1) * 512], tmp[0][:], tmp[1][:], ALU.add, [("tmp", 0), ("tmp", 1)], [("mtm", g)])
            mT = QN
            for q4 in range(4):
                for j in range(4):
                    c = q4 * 4 + j
                    self.tr(psb[:, j * 128:(j + 1) * 128], mtm[:, c * 128:(c + 1) * 128], identb[:], [("mtm", q4), "identb"], ["psb"])
                self.cp(self.rot(), mT[:, q4 * 4:(q4 + 1) * 4, :].rearrange("p a b -> p (a b)"), psb[:, 0:512], ["psb"],
                        [("QN", q4 * 4 + j) for j in range(4)])
            MT = [("QN", c) for c in range(16)]
            for g in range(4):
                px, pxk = self.proj_tm("out", g, mT, MT)
                self.tt("dve", tmp[2][:], px[:, :], gate_m[:, g * 512:(g + 1) * 512], ALU.mult, [pxk, ("gate", 2, g)], [("tmp", 2)])
                self.tt("dve", xt[:, g * 512:(g + 1) * 512], xt[:, g * 512:(g + 1) * 512], tmp[2][:], ALU.add, XT + [("tmp", 2)], XT)

            norm_to_hT(1)
            for g in range(11):
                pg_, pgk2 = self.proj_tm("fin", g, hT, HT)
                self.act(tmp[0][:], pg_[:, :], AF.Exp, [pgk2], [("tmp", 0)], scale=-1.0)
                pu_, puk = self.proj_tm("fin", 11 + g, hT, HT)
                self.ts("dve", tmp[0][:], tmp[0][:], 1.0, None, ALU.add, None, [("tmp", 0)], [("tmp", 0)])
                self.recip(tmp[0][:], tmp[0][:], [("tmp", 0)], [("tmp", 0)])
                self.tt("dve", tmp[1][:], pu_[:, :], tmp[0][:], ALU.mult, [puk, ("tmp", 0)], [("tmp", 1)])
                self.tt("dve", on[:], pg_[:, :], tmp[1][:], ALU.mult, [pgk2, ("tmp", 1)], ["on"])
                for j in range(4):
                    self.tr(psb[:, j * 128:(j + 1) * 128], on[:, j * 128:(j + 1) * 128], identb[:], ["on", "identb"], ["psb"])
                self.cp(self.rot(), big[:, g * 4:(g + 1) * 4, :].rearrange("p a b -> p (a b)"), psb[:, 0:512], ["psb"],
                        [("big", g * 4 + j) for j in range(4)])
            AK = [("big", c) for c in range(44)]
            for g in range(4):
                pd_, pdk_ = self.proj_tm("dn", g, big, AK, kcw=44)
                self.tt("dve", tmp[2][:], pd_[:, :], gate_f[:, g * 512:(g + 1) * 512], ALU.mult, [pdk_, ("gate", 5, g)], [("tmp", 2)])
                self.tt("dve", xt[:, g * 512:(g + 1) * 512], xt[:, g * 512:(g + 1) * 512], tmp[2][:], ALU.add, XT + [("tmp", 2)], XT)
            self.act(xstg[:], xt[:], AF.Square, XT, ["xstg", "ss"], accum=st1[:, 0:1])
            rstd_from_ss(st1[:, 0:1], 1.0 / D, st1[:, 1:2], ["ss"], ["rstd"])
            self.stt("dve", xstg[:], xt[:], st1[:, 1:2], fing[:], ALU.mult, ALU.mult, XT + ["rstd", "fing", "xstg"], ["xstg"])
            self.dma(SQ, out[ti * 128:(ti + 1) * 128, :], xstg[:], ["xstg"], [("out", ti)])

        self.kvpar = 0
        self.st_i = 0
        self.onesb = sb("onesb", [128, 128], BF16)
        self.memset("pool", self.onesb[:], 1.0, ["onesb"])
        for ti in range(NTP):
            tile(ti, True)
        if STOP == 3 or STOP > 30:
            S.barrier(); S.emit(nc); self.st.close(); return
        for ti in range(NTO):
            tile(ti, False)
        S.add("sp", None, reads=[("out", ti) for ti in range(NTO)])
        S.emit(nc)
        self.st.close()

    def _cast_partial(self, cast_unit, src2d, c0, dst4, g, perm, hf):
        srcv = src2d.rearrange("(kc p) n -> p kc n", p=128)
        p2 = [(d0 - hf * 256, s0, n) for (d0, s0, n) in perm]
        cast_unit(srcv[:, 0:4, c0:c0 + 512], dst4[g, :, 0:4, hf * 256:hf * 256 + 256], 4, 512, p2)


def I_adab_row(b, gi, q4):
    return b._adab_row[:, gi * 2048 + q4 * 512: gi * 2048 + (q4 + 1) * 512].partition_broadcast(128)


_CACHE = {}


def _build(PRE, OWN):
    key = (PRE, OWN)
    if key in _CACHE:
        return _CACHE[key]
    nc = bass.Bass("TRN2", target_bir_lowering=False)
    b = Builder(nc, PRE, OWN)
    b._adab_row = b.din("ada_b_row", [1, 6 * D])
    b.build()
    _CACHE[key] = nc
    return nc


def kernel(x, c, positions, ada_w, ada_b, norm_mix_g, w_in, gla_gk_w, gla_gk_b, gla_onorm_g, gla_wo, mla_q_norm_g,
           mla_wuq, mla_kv_norm_g, mla_wukv, mla_wo, w_out, norm_ffn_g, ffn_w_in, ffn_w_down, final_norm_g):
    f32 = np.float32
    x = np.asarray(x, f32)
    B, SEQ, _ = x.shape
    HALF = SEQ // 2
    nc = _build(HALF, HALF)

    def fm(v, n):
        return np.ascontiguousarray(np.asarray(v, f32).reshape(n, 128).T)

    jj = np.arange(128)
    tri = np.where(jj[:, None] <= jj[None, :], -1.0 / 16.0, 0.0).astype(f32)
    uni = np.where(jj[:, None] > jj[None, :], -1.0 / 16.0, 0.0).astype(f32)
    mask01 = (jj[:, None] <= jj[None, :]).astype(f32)
    maskneg = np.where(jj[:, None] > jj[None, :], NEG, 0.0).astype(ml_dtypes.bfloat16)
    inv_freq = (10000.0 ** (-np.arange(0, 64, 2, dtype=f32) / 64.0)).astype(f32)
    invf = np.concatenate([inv_freq, inv_freq]).reshape(64, 1).astype(f32)
    sgn = np.concatenate([-np.ones(32, f32), np.ones(32, f32)]).reshape(64, 1)
    shared = dict(
        ada_w=np.ascontiguousarray(np.asarray(ada_w, f32)[0]),
        ada_b_fm=fm(np.asarray(ada_b)[0], 96), ada_b_row=np.ascontiguousarray(np.asarray(ada_b, f32)[0].reshape(1, -1)),
        nmg_fm=fm(np.asarray(norm_mix_g)[0], 16), nfg_fm=fm(np.asarray(norm_ffn_g)[0], 16),
        w_in=np.ascontiguousarray(np.asarray(w_in, f32)[0]),
        gkw=np.ascontiguousarray(np.concatenate([np.asarray(gla_gk_w, f32)[0], np.asarray(gla_gk_b, f32)[0][None, :]], 0)),
        onorm=np.ascontiguousarray(np.asarray(gla_onorm_g, f32)[0].reshape(1, 512)),
        gla_wo=np.ascontiguousarray(np.asarray(gla_wo, f32)[0]),
        qng_fm=fm(np.asarray(mla_q_norm_g)[0], 4), kvng_fm=fm(np.asarray(mla_kv_norm_g)[0], 4),
        wuq=np.ascontiguousarray(np.asarray(mla_wuq, f32)[0]), wukv=np.ascontiguousarray(np.asarray(mla_wukv, f32)[0]),
        mla_wo=np.ascontiguousarray(np.asarray(mla_wo, f32)[0]), w_out=np.ascontiguousarray(np.asarray(w_out, f32)[0]),
        ffn_in=np.ascontiguousarray(np.asarray(ffn_w_in, f32)[0]), ffn_dn=np.ascontiguousarray(np.asarray(ffn_w_down, f32)[0]),
        fing=np.ascontiguousarray(np.asarray(final_norm_g, f32).reshape(1, D)),
        ident=np.eye(128, dtype=f32), identb=np.eye(128).astype(ml_dtypes.bfloat16), tri=tri, uni=uni, mask01=mask01,
        maskneg=maskneg, invf=invf, sgn=sgn,
    )
    positions = np.asarray(positions).astype(np.int32)
    c = np.asarray(c, f32)
    in_maps = []
    for core in range(2 * B):
        b, half = core // 2, core % 2
        m = dict(shared)
        m["xp"] = np.ascontiguousarray(x[b, 0:HALF])
        m["xo"] = np.ascontiguousarray(x[b, half * HALF:(half + 1) * HALF])
        m["posp"] = np.ascontiguousarray(positions[b, 0:HALF].reshape(1, HALF))
        m["poso"] = np.ascontiguousarray(positions[b, half * HALF:(half + 1) * HALF].reshape(1, HALF))
        m["c_fm"] = fm(c[b], 16)
        fl = np.zeros((128, 2), f32)
        fl[:, 0] = float(half)
        fl[:, 1] = 0.0 if half == 1 else NEG
        m["flag"] = fl
        in_maps.append(m)
    NCR = int(os.environ.get('KCORES', str(2 * B)))
    res = run_bass_kernel_spmd(nc, in_maps[:NCR], core_ids=list(range(NCR)))
    out = np.zeros((B, SEQ, D), f32)
    for core in range(NCR):
        b, half = core // 2, core % 2
        out[b, half * HALF:(half + 1) * HALF] = res.results[core]["out"]
    return out
```
